# Optimizing a Trainium2 kernel written in Bass

```python
import math
import jax, jax.numpy as jnp
from jax import lax
import numpy as np

D_MODEL = 1024
BATCH = 4
SEQ = 4096
DEPTH = 1

N_META = 16
BLOCK_Q = 128
RMS_EPS = 1e-6

MLA_HEADS = 16
MLA_Q_RANK = 256
MLA_KV_RANK = 128
MLA_NOPE_DIM = 64
MLA_ROPE_DIM = 32
MLA_V_DIM = 64
MLA_WIDTH = MLA_HEADS * MLA_V_DIM
MLA_SCALE = 1.0 / math.sqrt(MLA_NOPE_DIM + MLA_ROPE_DIM)
ROPE_THETA = 10000.0

FOX_HEADS = 16
FOX_HEAD_DIM = 64
FOX_WIDTH = FOX_HEADS * FOX_HEAD_DIM
FOX_SCALE = 1.0 / math.sqrt(FOX_HEAD_DIM)

IN_SPLITS = (MLA_Q_RANK, MLA_KV_RANK, MLA_ROPE_DIM, MLA_WIDTH,
             FOX_WIDTH, FOX_WIDTH, FOX_WIDTH, FOX_HEADS, FOX_WIDTH,
             D_MODEL, D_MODEL)
IN_WIDTH = sum(IN_SPLITS)

kernel_name = 'hybrid_mla_fox_gated_merge'


def rmsnorm(x, g):
    xf = x.astype(jnp.float32)
    y = xf * lax.rsqrt(jnp.mean(xf * xf, axis=-1, keepdims=True) + RMS_EPS)
    return (y * g.astype(jnp.float32)).astype(x.dtype)


def rope(x, pos):
    half = x.shape[-1] // 2
    inv_freq = ROPE_THETA ** (-jnp.arange(half, dtype=jnp.float32) / half)
    ang = pos.astype(jnp.float32)[:, None] * inv_freq[None, :]
    cos, sin = jnp.cos(ang), jnp.sin(ang)
    x1 = x[..., :half].astype(jnp.float32)
    x2 = x[..., half:].astype(jnp.float32)
    return jnp.concatenate([x1 * cos - x2 * sin, x1 * sin + x2 * cos], axis=-1).astype(x.dtype)


def causal_block_attention(q, k, v, scale, cum=None):
    B, H, L, _ = q.shape
    n_real = L - N_META
    nb = n_real // BLOCK_Q
    key_pos = jnp.arange(L)

    def attend(qb, qpos, cq=None):
        s = jnp.einsum('bhqd,bhkd->bhqk', qb, k, preferred_element_type=jnp.float32) * scale
        if cq is not None:
            s = s + (cq[..., :, None] - cum[:, :, None, :])
        s = jnp.where(key_pos[None, :] <= qpos[:, None], s, -jnp.inf)
        p = jax.nn.softmax(s, axis=-1)
        return jnp.einsum('bhqk,bhkd->bhqd', p.astype(v.dtype), v)

    out_meta = attend(q[:, :, :N_META], key_pos[:N_META],
                      None if cum is None else cum[:, :, :N_META])

    def to_blocks(a):
        a = a[:, :, N_META:]
        a = a.reshape((B, H, nb, BLOCK_Q) + a.shape[3:])
        return jnp.moveaxis(a, 2, 0)

    xs = (to_blocks(q), key_pos[N_META:].reshape(nb, BLOCK_Q))
    if cum is not None:
        xs = xs + (to_blocks(cum),)
    out_real = lax.map(lambda args: attend(*args), xs)
    out_real = jnp.moveaxis(out_real, 0, 2).reshape(B, H, n_real, v.shape[-1])
    return jnp.concatenate([out_meta, out_real], axis=2)


def setup_inputs(seed: int = 0) -> dict:
    key = jax.random.key(seed)
    ks = jax.random.split(key, 13)
    f32 = jnp.float32

    def gain(k, n):
        return 1.0 + 0.1 * jax.random.normal(k, (DEPTH, n), f32)

    def dense(k, fan_in, fan_out):
        return jax.random.normal(k, (DEPTH, fan_in, fan_out), f32) * fan_in ** -0.5

    return {
        'x': jax.random.normal(ks[0], (BATCH, SEQ, D_MODEL), f32),
        'meta_tokens': jax.random.normal(ks[1], (N_META, D_MODEL), f32),
        'pre_norm_g': gain(ks[2], D_MODEL),
        'w_in': dense(ks[3], D_MODEL, IN_WIDTH),
        'fox_forget_b': jax.random.uniform(ks[4], (DEPTH, FOX_HEADS), f32, 1.0, 4.0),
        'mla_q_norm_g': gain(ks[5], MLA_Q_RANK),
        'mla_kv_norm_g': gain(ks[6], MLA_KV_RANK),
        'w_uq': dense(ks[7], MLA_Q_RANK, MLA_HEADS * (MLA_NOPE_DIM + MLA_ROPE_DIM)),
        'w_ukv': dense(ks[8], MLA_KV_RANK, MLA_HEADS * (MLA_NOPE_DIM + MLA_V_DIM)),
        'w_br_mla': dense(ks[9], MLA_WIDTH, D_MODEL),
        'w_br_fox': dense(ks[10], FOX_WIDTH, D_MODEL),
        'w_out': dense(ks[11], D_MODEL, D_MODEL),
        'post_norm_g': gain(ks[12], D_MODEL),
    }


def reference(x, meta_tokens, pre_norm_g, w_in, fox_forget_b, mla_q_norm_g, mla_kv_norm_g,
              w_uq, w_ukv, w_br_mla, w_br_fox, w_out, post_norm_g):
    B, S, D = x.shape
    L = S + N_META
    pos = jnp.arange(L)
    h = jnp.concatenate([jnp.broadcast_to(meta_tokens[None].astype(x.dtype), (B, N_META, D)), x], axis=1)
    split_idx = np.cumsum(IN_SPLITS)[:-1].tolist()

    def fox_heads(t):
        return t.reshape(B, L, FOX_HEADS, FOX_HEAD_DIM).transpose(0, 2, 1, 3)

    for l in range(DEPTH):
        u = rmsnorm(h, pre_norm_g[l])
        proj = u @ w_in[l]
        (cq, ckv, k_pe_raw, z_mla, fq, fk, fv, f_logit, z_fox, gate_a, gate_b) = jnp.split(proj, split_idx, axis=-1)

        q = (rmsnorm(cq, mla_q_norm_g[l]) @ w_uq[l]).reshape(B, L, MLA_HEADS, MLA_NOPE_DIM + MLA_ROPE_DIM).transpose(0, 2, 1, 3)
        kv = (rmsnorm(ckv, mla_kv_norm_g[l]) @ w_ukv[l]).reshape(B, L, MLA_HEADS, MLA_NOPE_DIM + MLA_V_DIM).transpose(0, 2, 1, 3)
        q_nope, q_pe = q[..., :MLA_NOPE_DIM], q[..., MLA_NOPE_DIM:]
        k_nope, v_mla = kv[..., :MLA_NOPE_DIM], kv[..., MLA_NOPE_DIM:]
        k_pe = rope(k_pe_raw, pos)[:, None]
        q_m = jnp.concatenate([q_nope, rope(q_pe, pos)], axis=-1)
        k_m = jnp.concatenate([k_nope, jnp.broadcast_to(k_pe, (B, MLA_HEADS, L, MLA_ROPE_DIM))], axis=-1)
        o_mla = causal_block_attention(q_m, k_m, v_mla, MLA_SCALE)
        o_mla = o_mla.transpose(0, 2, 1, 3).reshape(B, L, MLA_WIDTH)
        y_mla = (o_mla * jax.nn.silu(z_mla)) @ w_br_mla[l]

        log_f = jax.nn.log_sigmoid((f_logit + fox_forget_b[l]).astype(jnp.float32)).transpose(0, 2, 1)
        cum = jnp.cumsum(log_f, axis=-1)
        o_fox = causal_block_attention(fox_heads(fq), fox_heads(fk), fox_heads(fv), FOX_SCALE, cum)
        o_fox = o_fox.transpose(0, 2, 1, 3).reshape(B, L, FOX_WIDTH)
        y_fox = (o_fox * jax.nn.silu(z_fox)) @ w_br_fox[l]

        mixed = (jax.nn.sigmoid(gate_a) * y_mla + jax.nn.sigmoid(gate_b) * y_fox) @ w_out[l]
        h = h + rmsnorm(mixed, post_norm_g[l])

    return h[:, N_META:]
```

```python
import os
import numpy as np
from contextlib import ExitStack
import concourse.bass as bass
import concourse.mybir as mybir
from concourse.bass_utils import run_bass_kernel_spmd

F32 = mybir.dt.float32
BF16 = mybir.dt.bfloat16
ALU = mybir.AluOpType
AF = mybir.ActivationFunctionType
AX = mybir.AxisListType

D = 1024
NB = 33
T = NB * 128
NOWN = 2048
RMS_EPS = 1e-6
MLA_SCALE = 1.0 / float(np.sqrt(96.0))
FOX_SCALE = 0.125
O_CQ, O_CKV, O_KPE, O_ZM, O_FQ, O_FK, O_FV, O_FL, O_ZF, O_GA, O_GB = 0, 256, 384, 416, 1440, 2464, 3488, 4512, 4528, 5552, 6576
TWO_PI = 2.0 * np.pi
MAGIC = 12582912.0
CW1 = 6.28125
CW2 = float(np.float32(TWO_PI - CW1))
CW3 = float(TWO_PI - CW1 - CW2)
NEG_BIG = -30000.0


class Buf:
    def __init__(self, name, excl=False):
        self.name = name
        self.excl = excl
        self.w = {}
        self.r = {}
        self.war = {}
        self.phase = "w"


class EngQ:
    def __init__(self, nc, eng, name, es, is_pe=False):
        self.nc = nc
        self.eng = eng
        self.name = name
        self.sem = es.enter_context(nc.semaphore("q_" + name))
        self.count = 0
        self.real = 0
        self.vmap = {}
        self.seen = {}
        self.is_pe = is_pe


class DSem:
    def __init__(self, nc, key, i, es):
        self.key = key
        self.sem = es.enter_context(nc.semaphore("d%d" % i))
        self.count = 0


class Sched:
    def __init__(self, nc, es, needed=None, ndma=24):
        self.nc = nc
        self.needed = needed
        self.waited = set()
        self.pe = EngQ(nc, nc.tensor, "pe", es, is_pe=True)
        self.act = EngQ(nc, nc.scalar, "act", es)
        self.dve = EngQ(nc, nc.vector, "dve", es)
        self.pool = EngQ(nc, nc.gpsimd, "pool", es)
        self.sp = EngQ(nc, nc.sync, "sp", es)
        self.qs = {q.name: q for q in (self.pe, self.act, self.dve, self.pool, self.sp)}
        self.dsems = {"sp": [DSem(nc, ("d", "sp", i), i, es) for i in range(ndma)],
                      "pool": [DSem(nc, ("d", "pool", i), 100 + i, es) for i in range(ndma)],
                      "act": [DSem(nc, ("d", "act", i), 200 + i, es) for i in range(8)]}
        self.dnext = {"sp": 0, "pool": 0, "act": 0}
        self.all_dma_tokens = {}

    def _collect(self, q, reads, writes, partial, deps):
        need = {}

        def add(d):
            for k, (s, v) in d.items():
                if k == q.name and q.is_pe:
                    continue
                if k not in need or need[k][1] < v:
                    need[k] = (s, v)

        for b in reads:
            add(b.w)
            if b.excl:
                add({k: t for k, t in b.r.items() if k != q.name})
        for b in writes:
            if b.phase == "r":
                b.war = b.r
                b.r = {}
                b.w = {}
                b.phase = "w"
            add(b.war)
            if not partial:
                add(b.w)
        for d in deps:
            add(d)
        return need

    def _wait(self, q, need):
        for k, (s, v) in need.items():
            if q.seen.get(k, 0) < v:
                q.seen[k] = v
                if isinstance(k, str):
                    self.waited.add((k, v))
                    if self.needed is None:
                        rv = v
                    else:
                        rv = self.qs[k].vmap[v]
                else:
                    rv = v
                q.eng.wait_ge(s, rv)

    def op(self, q, fn, reads=(), writes=(), partial=False, deps=()):
        need = self._collect(q, reads, writes, partial, deps)
        self._wait(q, need)
        ins = fn(q.eng)
        q.count += 1
        if self.needed is None or (q.name, q.count) in self.needed:
            q.real += 1
            q.vmap[q.count] = q.real
            ins.then_inc(q.sem, 1)
        tok = (q.sem, q.count)
        for b in reads:
            b.r[q.name] = tok
            b.phase = "r"
        for b in writes:
            b.w[q.name] = tok
        return {q.name: tok}

    def dma(self, q, out, in_, reads=(), writes=(), partial=False, deps=(), batch=None, **kw):
        need = self._collect(q, reads, writes, partial, deps)
        if batch is not None and batch.get("ds") is not None:
            ds = batch["ds"]
        else:
            pool_ = self.dsems[q.name]
            ds = pool_[self.dnext[q.name]]
            self.dnext[q.name] = (self.dnext[q.name] + 1) % len(pool_)
            if ds.count > 0:
                need[ds.key] = (ds.sem, ds.count)
            if batch is not None:
                batch["ds"] = ds
        self._wait(q, need)
        ins = q.eng.dma_start(out=out, in_=in_, **kw)
        ds.count += 16
        ins.then_inc(ds.sem, 16)
        tok = (ds.sem, ds.count)
        for b in reads:
            b.r[ds.key] = tok
            b.phase = "r"
        for b in writes:
            b.w[ds.key] = tok
        self.all_dma_tokens[ds.key] = tok
        return {ds.key: tok}


def build_program(debug=False, stop=99, needed=None):
    nc = bass.Bass("TRN2", target_bir_lowering=False)
    es = ExitStack()

    def dram(name, shape, dt=F32, kind="ExternalInput"):
        return nc.dram_tensor(name, list(shape), dt, kind=kind).ap()

    xl = dram("xl", [T, D])
    posv = dram("posv", [128, NB])
    validv = dram("validv", [128, NB])
    w_in = dram("w_in", [D, 7600])
    w_uq = dram("w_uq", [256, 1536])
    w_ukv = dram("w_ukv", [128, 2048])
    w_bra = dram("w_bra", [D, D])
    w_brf = dram("w_brf", [D, D])
    w_out = dram("w_out", [D, D])
    gpre = dram("gpre", [1, D])
    gq = dram("gq", [1, 256])
    gkv = dram("gkv", [1, 128])
    gpost = dram("gpost", [1, D])
    fb = dram("fb", [1, 16])
    invf = dram("invf", [1, 16])
    outd = dram("out", [NOWN, D], kind="ExternalOutput")
    dbg = {}
    if debug:
        dbg["uT"] = dram("dbg_uT", [128, 8 * T], BF16, kind="ExternalOutput")
        dbg["ckvnT"] = dram("dbg_ckvnT", [128, T], BF16, kind="ExternalOutput")
        dbg["cqnT"] = dram("dbg_cqnT", [128, 2 * NOWN], BF16, kind="ExternalOutput")
        dbg["KA"] = dram("dbg_KA", [128, T], BF16, kind="ExternalOutput")
        dbg["QA"] = dram("dbg_QA", [128, NOWN], BF16, kind="ExternalOutput")
        dbg["VP"] = dram("dbg_VP", [128, NB * 192], BF16, kind="ExternalOutput")
        dbg["OGA"] = dram("dbg_OGA", [128, 8 * NOWN], BF16, kind="ExternalOutput")
        dbg["OGB"] = dram("dbg_OGB", [128, 8 * NOWN], BF16, kind="ExternalOutput")
        dbg["cum"] = dram("dbg_cum", [128, 528], F32, kind="ExternalOutput")
        dbg["TK"] = dram("dbg_TK", [128, T], BF16, kind="ExternalOutput")

    S = Sched(nc, es, needed=needed)
    PE, ACT, DVE, POOL, SP = S.pe, S.act, S.dve, S.pool, S.sp

    def sb(name, shape, dt):
        return es.enter_context(nc.sbuf_tensor(name, list(shape), dt))

    def ps(name, shape, dt=F32):
        return es.enter_context(nc.psum_tensor(name, list(shape), dt))

    uT = sb("uT", [128, 8, T], BF16)
    OGAf = sb("OGAf", [128, 8 * NOWN], BF16)
    OGBf = sb("OGBf", [128, 8 * NOWN], BF16)
    ARENA = sb("ARENA", [128, 24000], BF16)
    TKreg = sb("TKreg", [128, T], BF16)
    WBIG = [sb("WBIG%d" % i, [128, 8, 128], BF16) for i in range(4)]
    RT = sb("RT", [128, 512], F32)
    TMP = sb("TMP", [128, 512], F32)
    ident_bf = sb("ident_bf", [128, 128], BF16)
    ident_f = sb("ident_f", [128, 128], F32)
    maskneg = sb("maskneg", [128, 128], BF16)
    tri_f = sb("tri_f", [128, 128], F32)
    ones_f = sb("ones_f", [128, 128], F32)
    gq_b = sb("gq_b", [128, 256], F32)
    gkv_b = sb("gkv_b", [128, 128], F32)
    fb_b = sb("fb_b", [128, 16], F32)
    invf_b = sb("invf_b", [128, 16], F32)
    pos_t = sb("pos_t", [128, NB], F32)
    valid_t = sb("valid_t", [128, NB], F32)
    FLh = sb("FLh", [128, 16, NB], F32)
    mhalf = sb("mhalf", [128, 4], F32)
    stat = sb("stat", [128, 8 * NB], F32)
    WMS = sb("WMS", [128, 768], BF16)

    def view(base, byte_off, shape, dt):
        esz = 2 if dt == BF16 else 4
        n = int(np.prod(shape[1:]))
        assert byte_off % 4 == 0
        a = base[:, byte_off // 2: byte_off // 2 + n * esz // 2]
        if dt == F32:
            a = a.bitcast(F32)
        if len(shape) == 3:
            a = a.rearrange("p (a b) -> p a b", a=shape[1])
        return a

    OGA = view(OGAf, 0, [128, 8, NOWN], BF16)
    OGB = view(OGBf, 0, [128, 8, NOWN], BF16)
    KA = view(ARENA, 0, [128, T], BF16)
    KB = view(ARENA, 8448, [128, T], BF16)
    QA = view(ARENA, 16896, [128, NOWN], BF16)
    QB = view(ARENA, 20992, [128, NOWN], BF16)
    VP = view(ARENA, 25088, [128, NB, 192], BF16)
    PT = [view(ARENA, 37760 + i * 2048, [128, 2, 512], BF16) for i in range(3)]
    SZ = view(ARENA, 43904, [128, NOWN], BF16)
    TH = view(ARENA, 37760 + 2 * 2048, [128, 512], F32)
    xs = [view(OGAf, i * 4096, [128, 1024], F32) for i in range(3)]
    xn = [view(OGAf, 12288 + i * 2048, [128, 1024], BF16) for i in range(2)]
    sqj = view(OGAf, 16384, [128, 1024], BF16)
    gpre_b = view(OGAf, 18432, [128, 1024], F32)
    LT = [view(OGAf, 22528 + i * 1024, [128, 416], BF16) for i in range(2)]
    CS_tm = view(OGAf, 24576, [128, NB, 16], F32)
    SN_tm = view(OGAf, 26688, [128, NB, 16], F32)
    WL = view(ARENA, 16896, [128, 8, 432], BF16)
    cqnT = view(OGBf, 0, [128, 2, NOWN], BF16)
    CSq = view(OGBf, 8192, [128, NOWN], F32)
    SNq = view(OGBf, 16384, [128, NOWN], F32)
    ANGs = view(OGBf, 24576, [128, NB, 16], F32)
    KFs = view(OGBf, 26688, [128, NB, 16], F32)
    RRs = view(OGBf, 28800, [128, NB, 16], F32)
    ckvnT = view(TKreg, 0, [128, T], BF16)
    TK = view(TKreg, 0, [128, T], BF16)
    TKsrc = view(ARENA, 0, [128, NB, 64], BF16)
    TQsrc = view(ARENA, 4224, [128, 16, 64], BF16)
    CUM = view(ARENA, 6272, [128, 16, NB], F32)
    LFv = view(ARENA, 8384, [128, 16, NB], F32)
    INC = view(ARENA, 10496, [128, 16, NB], F32)
    SEG = view(ARENA, 12608, [128, 16, NB], F32)
    C8 = view(ARENA, 14720, [128, 16, NB], F32)
    R1 = view(ARENA, 16832, [128, 16, NB], F32)
    HI = view(ARENA, 18944, [128, 16, NB], BF16)
    MID = view(ARENA, 20000, [128, 16, NB], BF16)
    MT = view(ARENA, 0, [128, 8, NOWN], BF16)
    FT = [view(ARENA, 32768 + i * 2048, [128, 512], F32) for i in range(4)]
    CHW = [WBIG[i] for i in range(4)] + [view(TKreg, i * 2048, [128, 8, 128], BF16) for i in range(4)]

    ST = [ps("ST%d" % i, [128, 2, 512]) for i in range(2)]
    OP = [ps("OP%d" % i, [128, 512]) for i in range(2)]
    PJ_ = [ps("PJ%d" % i, [128, 512]) for i in range(2)]

    B = {}

    def buf(name, excl=False):
        if name not in B:
            B[name] = Buf(name, excl=excl)
        return B[name]

    b_uT = [buf("uT%d" % i) for i in range(NB)]
    b_ST = [buf("ST0", True), buf("ST1", True)]
    b_OP = [buf("OP0", True), buf("OP1", True)]
    b_PJ_ = [buf("PJ0", True), buf("PJ1", True)]
    PJX = [PJ_[0][:, :], PJ_[1][:, :], ST[0][:, 0, :], ST[1][:, 0, :]]
    b_PJX = [b_PJ_[0], b_PJ_[1], b_ST[0], b_ST[1]]
    b_PT = [buf("PT%d" % i) for i in range(3)]
    b_xs = [buf("xs%d" % i) for i in range(3)]
    b_xn = [buf("xn%d" % i) for i in range(2)]
    b_LT = [buf("LT0"), buf("LT1")]
    b_W = [buf("WBIG%d" % i) for i in range(4)]
    bc_ = buf("consts")

    pj_i = [0]

    def next_pj():
        i = pj_i[0]
        pj_i[0] = (i + 1) % 4
        return i

    ev_i = [0]

    def evac_engine():
        ev_i[0] += 1
        return ACT if (ev_i[0] % 2 == 0) else DVE

    def copy_op(q, out, in_, reads, writes, partial=True):
        if q is ACT:
            return S.op(ACT, lambda e: e.activation(out=out, in_=in_, func=AF.Copy), reads=reads, writes=writes, partial=partial)
        return S.op(q, lambda e: e.tensor_copy(out=out, in_=in_), reads=reads, writes=writes, partial=partial)

    def barrier():
        qs = [PE, ACT, DVE, POOL, SP]
        toks = {q.name: (q.sem, q.count) for q in qs if q.count > 0}
        toks.update(S.all_dma_tokens)
        for q in qs:
            S._wait(q, dict(toks))


    def finalize():
        S._wait(SP, dict(S.all_dma_tokens))
        barrier()
        return nc, es, S

    S.op(POOL, lambda e: e.memset(ident_bf[:, :], 1.0), writes=[buf("ident_bf")])
    S.op(POOL, lambda e: e.affine_select(out=ident_bf[:, :], in_=ident_bf[:, :], pattern=[[1, 128]], compare_op=ALU.is_equal,
                                          fill=0.0, base=0, channel_multiplier=-1), reads=[buf("ident_bf")], writes=[buf("ident_bf")])
    S.op(POOL, lambda e: e.memset(ident_f[:, :], 1.0), writes=[buf("ident_f")])
    S.op(POOL, lambda e: e.affine_select(out=ident_f[:, :], in_=ident_f[:, :], pattern=[[1, 128]], compare_op=ALU.is_equal,
                                          fill=0.0, base=0, channel_multiplier=-1), reads=[buf("ident_f")], writes=[buf("ident_f")])
    S.op(POOL, lambda e: e.memset(maskneg[:, :], NEG_BIG), writes=[buf("maskneg")])
    S.op(POOL, lambda e: e.affine_select(out=maskneg[:, :], in_=maskneg[:, :], pattern=[[-1, 128]], compare_op=ALU.is_gt,
                                          fill=0.0, base=0, channel_multiplier=1), reads=[buf("maskneg")], writes=[buf("maskneg")])
    S.op(POOL, lambda e: e.memset(tri_f[:, :], 1.0), writes=[buf("tri_f")])
    S.op(POOL, lambda e: e.affine_select(out=tri_f[:, :], in_=tri_f[:, :], pattern=[[1, 128]], compare_op=ALU.is_ge,
                                          fill=0.0, base=0, channel_multiplier=-1), reads=[buf("tri_f")], writes=[buf("tri_f")])
    S.op(POOL, lambda e: e.memset(ones_f[:, :], 1.0), writes=[buf("ones_f")])
    S.op(POOL, lambda e: e.memset(mhalf[:, :], -0.5), writes=[buf("mhalf")])

    for (dst, src, n) in [(gpre_b, gpre, D), (gq_b, gq, 256), (gkv_b, gkv, 128), (fb_b, fb, 16), (invf_b, invf, 16)]:
        S.dma(SP, dst[:, :], src.broadcast_to([128, n]), writes=[buf("smallconst")], partial=True)
    S.dma(SP, pos_t[:, :], posv[:, :], writes=[buf("smallconst")], partial=True)
    S.dma(SP, valid_t[:, :], validv[:, :], writes=[buf("smallconst")], partial=True)

    w_in_c = w_in.rearrange("(c p) n -> p c n", p=128)
    bt = {}
    S.dma(POOL, WL[:, :, 0:416], w_in_c[:, :, 0:416], writes=[buf("WL")], partial=True, batch=bt)
    S.dma(POOL, WL[:, :, 416:432], w_in_c[:, :, O_FL:O_FL + 16], writes=[buf("WL")], partial=True, batch=bt)

    b_tab = buf("tabscratch")
    sc = [buf("smallconst")]
    pos_b = pos_t[:, :].unsqueeze(2).broadcast_to([128, NB, 16])
    invf_bb = invf_b[:, :].unsqueeze(1).broadcast_to([128, NB, 16])
    S.op(DVE, lambda e: e.tensor_tensor(out=ANGs, in0=pos_b, in1=invf_bb, op=ALU.mult), reads=sc, writes=[b_tab])

    def make_table(dst, shift):
        S.op(DVE, lambda e: e.tensor_scalar(out=KFs, in0=ANGs, scalar1=float(shift), scalar2=float(1.0 / TWO_PI), op0=ALU.add, op1=ALU.mult),
             reads=[b_tab], writes=[buf("kf")])
        S.op(DVE, lambda e: e.tensor_scalar(out=KFs, in0=KFs, scalar1=MAGIC, scalar2=None, op0=ALU.add), reads=[buf("kf")], writes=[buf("kf")])
        S.op(DVE, lambda e: e.tensor_scalar(out=KFs, in0=KFs, scalar1=-MAGIC, scalar2=None, op0=ALU.add), reads=[buf("kf")], writes=[buf("kf")])
        S.op(DVE, lambda e: e.scalar_tensor_tensor(out=RRs, in0=KFs, scalar=-CW1, in1=ANGs, op0=ALU.mult, op1=ALU.add),
             reads=[buf("kf"), b_tab], writes=[buf("rr")])
        S.op(DVE, lambda e: e.scalar_tensor_tensor(out=RRs, in0=KFs, scalar=-CW2, in1=RRs, op0=ALU.mult, op1=ALU.add),
             reads=[buf("kf"), buf("rr")], writes=[buf("rr")])
        S.op(DVE, lambda e: e.scalar_tensor_tensor(out=RRs, in0=KFs, scalar=-CW3, in1=RRs, op0=ALU.mult, op1=ALU.add),
             reads=[buf("kf"), buf("rr")], writes=[buf("rr")])
        S.op(DVE, lambda e: e.tensor_scalar(out=RRs, in0=RRs, scalar1=float(shift), scalar2=float(np.pi), op0=ALU.add, op1=ALU.min),
             reads=[buf("rr")], writes=[buf("rr")])
        S.op(DVE, lambda e: e.tensor_scalar(out=RRs, in0=RRs, scalar1=float(-np.pi), scalar2=None, op0=ALU.max), reads=[buf("rr")], writes=[buf("rr")])
        S.op(ACT, lambda e: e.activation(out=dst[:, :, :], in_=RRs, func=AF.Sin), reads=[buf("rr")], writes=[buf("tables")], partial=True)

    make_table(SN_tm, 0.0)
    make_table(CS_tm, float(np.pi / 2))

    b_VPones = buf("VPones")
    S.op(DVE, lambda e: e.tensor_scalar(out=VP[:, :, 64:128], in0=valid_t[:, :].unsqueeze(2).broadcast_to([128, NB, 64]),
                                         scalar1=2.0, scalar2=None, op0=ALU.mult), reads=sc, writes=[b_VPones])

    xl_b = xl.rearrange("(b p) d -> b p d", p=128)
    b_ckvnT = buf("ckvnT")
    b_cqnT = buf("cqnT")
    def kcls(blk):
        ti = blk // 4
        return 0 if ti <= 2 else (1 if ti <= 4 else (2 if ti <= 6 else 3))

    b_KA = [buf("KA%d" % i) for i in range(4)]
    b_KB = [buf("KB%d" % i) for i in range(4)]
    b_VP = [buf("VP%d" % i) for i in range(4)]
    b_QA = [buf("QA%d" % i) for i in range(4)]
    b_QB = [buf("QB%d" % i) for i in range(4)]
    b_SZ = [buf("SZ%d" % i) for i in range(4)]
    b_FLh = buf("FLh")
    b_CSq = buf("CSq")
    tabs = [buf("tables")]

    TB = TMP[:, 0:64]
    bTB = buf("TMPn")
    for m in range(16):
        blk = 2 + 2 * m
        for (tab, dstT) in ((CS_tm, CSq), (SN_tm, SNq)):
            S.op(DVE, lambda e: e.tensor_copy(out=TB.rearrange("p (a b) -> p a b", a=4), in_=tab[:, blk, :].unsqueeze(1).broadcast_to([128, 4, 16])),
                 reads=tabs, writes=[bTB])
            pj = next_pj()
            S.op(PE, lambda e: e.transpose(out=PJX[pj][0:64, 0:128], in_=TB, identity=ident_f[:, :]), reads=[bTB, buf("ident_f")], writes=[b_PJX[pj]])
            copy_op(evac_engine(), dstT[0:64, m * 128:(m + 1) * 128], PJX[pj][0:64, 0:128], reads=[b_PJX[pj]], writes=[b_CSq])

    xs5 = xs + [view(ARENA, 37760, [128, 1024], F32), view(ARENA, 43904, [128, 1024], F32)]
    b_xs5 = b_xs + [buf("xs3"), buf("xs4")]
    NXS = 5

    def st_S1(blk):
        xi = blk % NXS
        S.dma(SP, xs5[xi], xl_b[blk, :, :], writes=[b_xs5[xi]])
        S.op(ACT, lambda e: e.activation(out=sqj, in_=xs5[xi], func=AF.Square, accum_out=stat[:, blk * 8:blk * 8 + 1]),
             reads=[b_xs5[xi]], writes=[buf("sqj"), buf("stat%d" % blk)])
        S.op(POOL, lambda e: e.tensor_scalar(out=stat[:, blk * 8 + 1:blk * 8 + 2], in0=stat[:, blk * 8:blk * 8 + 1], scalar1=1.0 / D, scalar2=RMS_EPS,
                                              op0=ALU.mult, op1=ALU.add), reads=[buf("stat%d" % blk)], writes=[buf("stat%d" % blk)])
        S.op(POOL, lambda e: e.tensor_tensor(out=stat[:, blk * 8 + 2:blk * 8 + 3], in0=stat[:, blk * 8 + 1:blk * 8 + 2], in1=mhalf[:, 0:1], op=ALU.pow),
             reads=[buf("stat%d" % blk), buf("mhalf")], writes=[buf("stat%d" % blk)])

    def st_S2(blk):
        xi = blk % NXS
        ni = blk % 2
        g = blk % 2
        S.op(DVE, lambda e: e.scalar_tensor_tensor(out=xn[ni], in0=xs5[xi], scalar=stat[:, blk * 8 + 2:blk * 8 + 3], in1=gpre_b,
                                                   op0=ALU.mult, op1=ALU.mult),
             reads=[b_xs5[xi], buf("stat%d" % blk)] + sc, writes=[b_xn[ni]])
        pjv = ST[g][:, 0, :].bitcast(BF16).rearrange("p (c t) -> p c t", c=8)
        for c in range(8):
            S.op(PE, lambda e: e.transpose(out=pjv[:, c, :], in_=xn[ni][:, c * 128:(c + 1) * 128], identity=ident_bf[:, :]),
                 reads=[b_xn[ni], buf("ident_bf")], writes=[b_ST[g]], partial=(c > 0))
        copy_op(evac_engine(), uT[:, :, blk * 128:(blk + 1) * 128], pjv, reads=[b_ST[g]], writes=[b_uT[blk]], partial=False)

    def st_A(blk):
        pj = blk % 2
        L = PJX[pj]
        for c in range(8):
            S.op(PE, lambda e: e.matmul(L[:, 0:432], lhsT=uT[:, c, blk * 128:(blk + 1) * 128], rhs=WL[:, c, :], start=(c == 0), stop=(c == 7)),
                 reads=[b_uT[blk], buf("WL")], writes=[b_PJX[pj]], partial=(c > 0))
        st = buf("lstat%d" % blk)
        s0 = blk * 8 + 3
        S.op(ACT, lambda e: e.activation(out=sqj[:, 0:256], in_=L[:, 0:256], func=AF.Square, scale=1.0 / 16.0, accum_out=stat[:, s0:s0 + 1]),
             reads=[b_PJX[pj]], writes=[buf("sqj"), st])
        S.op(ACT, lambda e: e.activation(out=sqj[:, 256:384], in_=L[:, 256:384], func=AF.Square, scale=float(1.0 / np.sqrt(128.0)), accum_out=stat[:, s0 + 1:s0 + 2]),
             reads=[b_PJX[pj]], writes=[buf("sqj"), st], partial=True)
        S.op(POOL, lambda e: e.tensor_scalar(out=stat[:, s0:s0 + 2], in0=stat[:, s0:s0 + 2], scalar1=RMS_EPS, scalar2=None, op0=ALU.add),
             reads=[st], writes=[st])
        S.op(POOL, lambda e: e.tensor_tensor(out=stat[:, s0 + 2:s0 + 4], in0=stat[:, s0:s0 + 2], in1=mhalf[:, 0:2], op=ALU.pow),
             reads=[st, buf("mhalf")], writes=[st])

    def st_B(blk):
        pj = blk % 2
        li = blk % 2
        L = PJX[pj]
        st = buf("lstat%d" % blk)
        s0 = blk * 8 + 3
        lt = LT[li]
        S.op(DVE, lambda e: e.scalar_tensor_tensor(out=lt[:, 0:256], in0=L[:, 0:256], scalar=stat[:, s0 + 2:s0 + 3], in1=gq_b[:, :], op0=ALU.mult, op1=ALU.mult),
             reads=[b_PJX[pj], st] + sc, writes=[b_LT[li]])
        S.op(DVE, lambda e: e.scalar_tensor_tensor(out=lt[:, 256:384], in0=L[:, 256:384], scalar=stat[:, s0 + 3:s0 + 4], in1=gkv_b[:, :], op0=ALU.mult, op1=ALU.mult),
             reads=[b_PJX[pj], st] + sc, writes=[b_LT[li]], partial=True)
        cs = CS_tm[:, blk, :]
        sn = SN_tm[:, blk, :]
        bth = buf("TMPn")
        X2 = L[:, 384:416].rearrange("p (a b) -> p a b", a=2)
        T1 = TMP[:, 0:32].rearrange("p (a b) -> p a b", a=2)
        T2 = TMP[:, 32:64].rearrange("p (a b) -> p a b", a=2)
        S.op(DVE, lambda e: e.tensor_tensor(out=T1, in0=X2, in1=cs.unsqueeze(1).broadcast_to([128, 2, 16]), op=ALU.mult), reads=[b_PJX[pj]] + tabs, writes=[bth])
        S.op(DVE, lambda e: e.tensor_tensor(out=T2, in0=X2, in1=sn.unsqueeze(1).broadcast_to([128, 2, 16]), op=ALU.mult), reads=[b_PJX[pj]] + tabs, writes=[bth], partial=True)
        S.op(DVE, lambda e: e.tensor_tensor(out=lt[:, 384:400], in0=TMP[:, 0:16], in1=TMP[:, 48:64], op=ALU.subtract), reads=[bth], writes=[b_LT[li]], partial=True)
        S.op(DVE, lambda e: e.tensor_tensor(out=lt[:, 400:416], in0=TMP[:, 32:48], in1=TMP[:, 16:32], op=ALU.add), reads=[bth], writes=[b_LT[li]], partial=True)
        S.op(DVE, lambda e: e.tensor_tensor(out=FLh[:, :, blk], in0=L[:, 416:432], in1=fb_b[:, :], op=ALU.add), reads=[b_PJX[pj]] + sc, writes=[b_FLh], partial=True)
        tp = OP[pj][:, :].bitcast(BF16)
        for j in range(3):
            S.op(PE, lambda e: e.transpose(out=tp[:, j * 128:(j + 1) * 128], in_=lt[:, j * 128:(j + 1) * 128], identity=ident_bf[:, :]),
                 reads=[b_LT[li], buf("ident_bf")], writes=[b_OP[pj]], partial=(j > 0))
        S.op(PE, lambda e: e.transpose(out=tp[:, 384:512], in_=lt[:, 288:416], identity=ident_bf[:, :]),
             reads=[b_LT[li], buf("ident_bf")], writes=[b_OP[pj]], partial=True)

    def st_C(blk):
        pj = blk % 2
        tp = OP[pj][:, :].bitcast(BF16)
        ev = evac_engine()
        if blk >= 2 and blk % 2 == 0:
            m = (blk - 2) // 2
            copy_op(ev, cqnT[:, :, m * 128:(m + 1) * 128], tp[:, 0:256].rearrange("p (c t) -> p c t", c=2), reads=[b_OP[pj]], writes=[b_cqnT])
        copy_op(ev, ckvnT[:, blk * 128:(blk + 1) * 128], tp[:, 256:384], reads=[b_OP[pj]], writes=[b_ckvnT])
        copy_op(ev, KA[64:96, blk * 128:(blk + 1) * 128], tp[96:128, 384:512], reads=[b_OP[pj]], writes=[b_KA[kcls(blk)]])
        copy_op(ev, KB[64:96, blk * 128:(blk + 1) * 128], tp[96:128, 384:512], reads=[b_OP[pj]], writes=[b_KB[kcls(blk)]])

    for i in range(NB + 4):
        if i < NB:
            st_S1(i)
        if 0 <= i - 1 < NB:
            st_S2(i - 1)
        if 0 <= i - 2 < NB:
            st_A(i - 2)
        if 0 <= i - 3 < NB:
            st_B(i - 3)
        if 0 <= i - 4 < NB:
            st_C(i - 4)

    if debug:
        S.dma(SP, dbg["uT"], uT[:, :, :].rearrange("p c t -> p (c t)"), reads=b_uT)
        S.dma(SP, dbg["ckvnT"], ckvnT, reads=[b_ckvnT])
        S.dma(SP, dbg["cqnT"], cqnT.rearrange("p c t -> p (c t)"), reads=[b_cqnT])
    if stop <= 2:
        return finalize()

    barrier()

    own_tok = lambda M: slice(M * 512, (M + 1) * 512)
    st_i = [0]
    pt_i = [0]
    op_i = [0]
    pj2_i = [0]
    TH2 = sb("TH2", [128, 512], F32)
    R16 = sb("R16", [128, 16], F32)
    b_TH2 = buf("TH2")

    def bank_for(instream):
        if instream:
            i = pj2_i[0]
            pj2_i[0] = (i + 1) % 2
            return i
        return next_pj()

    def ev_for(instream):
        return DVE if instream else evac_engine()

    def slot_blocks(M):
        blks = [(0, 512, 0, False)] + [(b, 512, 0, False) for b in range(1, 8 * M + 1)]
        for mm in range(4):
            n = 512 - 128 * mm
            blks.append((1 + 2 * (4 * M + mm), n, 128 * mm, False))
            blks.append((2 + 2 * (4 * M + mm), n, 128 * mm, True))
        groups = []
        i = 0
        while i < len(blks):
            if i + 1 < len(blks) and blks[i + 1][1] == blks[i][1]:
                groups.append([blks[i], blks[i + 1]])
                i += 2
            else:
                groups.append([blks[i]])
                i += 1
        return groups

    def attention_multi(order, scale, pair, win=None):
        win = win or {}
        stream = []
        seg_groups = []
        for si_, (hd, M) in enumerate(order):
            groups = slot_blocks(M)
            seg_groups.append(len(groups))
            for gi, g in enumerate(groups):
                stream.append((si_, hd, M, g, gi, len(groups)))
        wstate = {}
        for s0, items in win.items():
            tot = seg_groups[s0] + (seg_groups[s0 + 1] if s0 + 1 < len(seg_groups) else 0)
            wstate[s0] = [list(items), 0, tot, 0]
        pend = None
        oslot = {}
        nq = []

        def drain(n=None, upto=None):
            k = 0
            while nq and (n is None or k < n) and (upto is None or nq[0][0] <= upto):
                nq.pop(0)[1]()
                k += 1

        for idx in range(len(stream) + 1):
            if idx < len(stream):
                sg, hd, M, g, gi, ng_ = stream[idx]
                first = gi == 0
                last = gi == ng_ - 1
                if first:
                    oslot[sg] = op_i[0] % 2
                    op_i[0] += 1
                si = st_i[0] % 2
                st_i[0] += 1
                n = g[0][1]
                qoff = g[0][2]
                KT, QT, KR = hd["KT"], hd["QT"], hd["KR"]
                for j, (blk, n_, qoff_, diag) in enumerate(g):
                    S.op(PE, lambda e: e.matmul(ST[si][:, j, 0:n], lhsT=KT[0:KR, blk * 128:(blk + 1) * 128], rhs=QT[0:KR, M * 512 + qoff:(M + 1) * 512],
                                                start=True, stop=(not diag)),
                         reads=[hd["b_K"][kcls(blk)], hd["b_Q"][M]], writes=[b_ST[si]], partial=(j > 0))
                    if diag:
                        S.op(PE, lambda e: e.matmul(ST[si][:, j, 0:128], lhsT=ident_bf[:, :], rhs=maskneg[:, :], start=False, stop=True),
                             reads=[buf("ident_bf"), buf("maskneg")], writes=[b_ST[si]], partial=True)
                pi = pt_i[0] % 3
                pt_i[0] += 1
                ng = len(g)
                S.op(ACT, lambda e: e.activation(out=PT[pi][:, 0:ng, 0:n], in_=ST[si][:, 0:ng, 0:n], func=AF.Exp, scale=float(scale)),
                     reads=[b_ST[si]], writes=[b_PT[pi]])
                cur = (sg, hd, M, g, first, last, pi)
                s0 = sg - (sg % 2)
                if s0 in wstate:
                    w = wstate[s0]
                    rem = len(w[0]) - w[1]
                    left = w[2] - w[3]
                    if rem > 0:
                        k = -(-rem // max(left, 1))
                        for it in w[0][w[1]:w[1] + k]:
                            it()
                        w[1] += k
                    w[3] += 1
                drain(n=(2 if len(nq) > 6 else 1))
            else:
                cur = None
            if pend is not None:
                sg, hd, M, g, first, last, pi = pend
                o = oslot[sg]
                if first:
                    drain(upto=sg - 2)
                n = g[0][1]
                qoff = g[0][2]
                vcol0 = hd["vcol0"]
                for j, (blk, n_, qoff_, diag) in enumerate(g):
                    S.op(PE, lambda e: e.matmul(OP[o][:, qoff:512], lhsT=VP[:, blk, vcol0:vcol0 + 128], rhs=PT[pi][:, j, 0:n],
                                                start=(first and j == 0), stop=(last and j == len(g) - 1)),
                         reads=[b_PT[pi], b_VP[kcls(blk)], b_VPones], writes=[b_OP[o]], partial=not (first and j == 0))
                if last:
                    if not hd["is_B"]:
                        orow, drow = slice(0, 64), slice(64, 128)
                    else:
                        orow, drow = slice(64, 128), slice(0, 64)
                    OG = hd["OG"]
                    bRT, bTM, bR16 = buf("RT"), buf("TMPn"), buf("R16")
                    bOGx = hd["b_OG"]

                    def mk(o=o, orow=orow, drow=drow, OG=OG, M=M, bOGx=bOGx):
                        return [
                            lambda: S.op(DVE, lambda e: e.transpose(out=RT[drow, :], in_=OP[o][drow, :]), reads=[b_OP[o]], writes=[bRT]),
                            lambda: S.op(DVE, lambda e: e.reciprocal(out=R16[drow, :], in_=RT[drow, :].rearrange("p (b c) -> p b c", c=32)[:, :, 0]),
                                         reads=[bRT], writes=[bR16]),
                            lambda: S.op(DVE, lambda e: e.tensor_copy(out=TMP[drow, :].rearrange("p (b c) -> p b c", c=32),
                                                                      in_=R16[drow, :].unsqueeze(2).broadcast_to([64, 16, 32])), reads=[bR16], writes=[bTM]),
                            lambda: S.op(DVE, lambda e: e.transpose(out=RT[drow, :], in_=TMP[drow, :]), reads=[bTM], writes=[bRT]),
                            lambda: S.op(DVE, lambda e: e.tensor_tensor(out=TMP[orow, :], in0=OP[o][orow, :], in1=RT[drow, :], op=ALU.mult),
                                         reads=[b_OP[o], bRT], writes=[bTM]),
                            lambda: S.op(DVE, lambda e: e.tensor_tensor(out=OG[orow, pair, M * 512:(M + 1) * 512], in0=TMP[orow, :],
                                                                        in1=SZ[orow, M * 512:(M + 1) * 512], op=ALU.mult),
                                         reads=[bTM, b_SZ[M]], writes=[bOGx], partial=True),
                        ]
                    for fn in mk():
                        nq.append((sg, fn))
            pend = cur
        drain()

    def urhs(c, M):
        return uT[:, c, 128:T].rearrange("p (m two t) -> p m two t", two=2, t=128)[:, 4 * M:4 * M + 4, 1, :]

    def z_item(M, wz, b_wz, instream):
        def f():
            pj = bank_for(instream)
            for c in range(8):
                S.op(PE, lambda e: e.matmul(PJX[pj][:, :], lhsT=wz[:, c, :], rhs=urhs(c, M), start=(c == 0), stop=(c == 7)),
                     reads=b_uT + [b_wz], writes=[b_PJX[pj]], partial=(c > 0))
            S.op(ACT, lambda e: e.activation(out=TH2[:, :], in_=PJX[pj][:, :], func=AF.Tanh, scale=0.5), reads=[b_PJX[pj]], writes=[b_TH2])
            S.op(DVE, lambda e: e.scalar_tensor_tensor(out=SZ[:, M * 512:(M + 1) * 512], in0=TH2[:, :], scalar=1.0, in1=PJX[pj][:, :], op0=ALU.add, op1=ALU.mult),
                 reads=[b_TH2, b_PJX[pj]], writes=[b_SZ[M]])
        return f

    def load_w_in_cols(dst, b_dst, col0, ncols=128):
        S.dma(POOL, dst[:, :, 0:ncols], w_in_c[:, :, col0:col0 + ncols], writes=[b_dst])

    ntt = [(i * 512, 512) for i in range(8)] + [(4096, 128)]
    KT_OF = {0: [0, 1, 2], 1: [3, 4], 2: [5, 6], 3: [7, 8]}
    VG_OF = {0: [0, 4, 8], 1: [12, 16], 2: [20, 24], 3: [28, 32]}

    def run_pair(items_of, hA_, hB_, scale, pair, all_upfront=False):
        for it in items_of(0, False):
            it()
        if all_upfront:
            for c in range(1, 5):
                for it in items_of(c, False):
                    it()
            win = {}
        else:
            win = {0: items_of(1, True), 2: items_of(2, True), 4: items_of(3, True), 6: items_of(4, True)}
        order = [(hA_, 0), (hB_, 0), (hA_, 1), (hB_, 1), (hA_, 2), (hB_, 2), (hA_, 3), (hB_, 3)]
        attention_multi(order, scale, pair, win)

    w_ukv_h = w_ukv.rearrange("k (h c) -> k h c", h=16)
    w_uq_c = w_uq.rearrange("(c p) (h d) -> p c h d", p=128, h=16)
    WKVn = WMS[:, 0:128]
    WKVv = WMS[:, 128:256]
    WQn = WMS[:, 256:512].rearrange("p (c n) -> p c n", c=2)
    WQp = WMS[:, 512:640].rearrange("p (c n) -> p c n", c=2)
    WQr = WMS[:, 640:768].rearrange("p (c n) -> p c n", c=2)
    b_WMS = buf("WMS")
    b_OGA = buf("OGA")
    b_OGB = buf("OGB")
    b_neg = buf("WQrneg")

    def mla_items(c, instream):
        its = []
        if c > 3:
            return [z_item(3, WBIG[0], b_W[0], instream)]

        def kt(ti):
            def f():
                t0, tn = ntt[ti]
                pj = bank_for(instream)
                S.op(PE, lambda e: e.matmul(PJX[pj][:, 0:tn], lhsT=WKVn, rhs=ckvnT[:, t0:t0 + tn], start=True, stop=True),
                     reads=[b_WMS, b_ckvnT], writes=[b_PJX[pj]])
                ev = ev_for(instream)
                copy_op(ev, KA[0:64, t0:t0 + tn], PJX[pj][0:64, 0:tn], reads=[b_PJX[pj]], writes=[b_KA[c]])
                copy_op(ev, KB[0:64, t0:t0 + tn], PJX[pj][64:128, 0:tn], reads=[b_PJX[pj]], writes=[b_KB[c]])
            return f

        def vg(g0):
            def f():
                nb_ = min(4, NB - g0)
                pj = bank_for(instream)
                pv = PJX[pj].rearrange("p (b c) -> p b c", b=4)
                for j in range(nb_):
                    blk = g0 + j
                    S.op(PE, lambda e: e.matmul(pv[:, j, :], lhsT=ckvnT[:, blk * 128:(blk + 1) * 128], rhs=WKVv, start=True, stop=True),
                         reads=[b_WMS, b_ckvnT], writes=[b_PJX[pj]], partial=(j > 0))
                ev = ev_for(instream)
                copy_op(ev, VP[:, g0:g0 + nb_, 0:64], pv[:, 0:nb_, 0:64], reads=[b_PJX[pj]], writes=[b_VP[c]])
                copy_op(ev, VP[:, g0:g0 + nb_, 128:192], pv[:, 0:nb_, 64:128], reads=[b_PJX[pj]], writes=[b_VP[c]])
            return f

        def qq(M):
            def f():
                tok = own_tok(M)
                pj = bank_for(instream)
                for cc in range(2):
                    S.op(PE, lambda e: e.matmul(PJX[pj][:, :], lhsT=WQn[:, cc, :], rhs=cqnT[:, cc, tok], start=(cc == 0), stop=(cc == 1)),
                         reads=[b_WMS, b_cqnT], writes=[b_PJX[pj]], partial=(cc > 0))
                ev = ev_for(instream)
                copy_op(ev, QA[0:64, tok], PJX[pj][0:64, :], reads=[b_PJX[pj]], writes=[b_QA[M]], partial=False)
                copy_op(ev, QB[0:64, tok], PJX[pj][64:128, :], reads=[b_PJX[pj]], writes=[b_QB[M]], partial=False)
                pjp = bank_for(instream)
                for cc in range(2):
                    S.op(PE, lambda e: e.matmul(PJX[pjp][0:64, :], lhsT=WQp[:, cc, :], rhs=cqnT[:, cc, tok], start=(cc == 0), stop=(cc == 1)),
                         reads=[b_WMS, b_cqnT], writes=[b_PJX[pjp]], partial=(cc > 0))
                S.op(DVE, lambda e: e.tensor_tensor(out=TH2[0:64, :], in0=PJX[pjp][0:64, :], in1=CSq[0:64, tok], op=ALU.mult),
                     reads=[b_PJX[pjp], b_CSq], writes=[b_TH2])
                pjr = bank_for(instream)
                for cc in range(2):
                    S.op(PE, lambda e: e.matmul(PJX[pjr][0:64, :], lhsT=WQr[:, cc, :], rhs=cqnT[:, cc, tok], start=(cc == 0), stop=(cc == 1)),
                         reads=[b_WMS, b_neg, b_cqnT], writes=[b_PJX[pjr]], partial=(cc > 0))
                S.op(DVE, lambda e: e.tensor_tensor(out=PJX[pjr][0:64, :], in0=PJX[pjr][0:64, :], in1=SNq[0:64, tok], op=ALU.mult),
                     reads=[b_PJX[pjr], b_CSq], writes=[b_PJX[pjr]])
                S.op(DVE, lambda e: e.tensor_tensor(out=QA[64:96, tok], in0=PJX[pjr][0:32, :], in1=TH2[0:32, :], op=ALU.add),
                     reads=[b_PJX[pjr], b_TH2], writes=[b_QA[M]], partial=True)
                S.op(DVE, lambda e: e.tensor_tensor(out=QB[64:96, tok], in0=PJX[pjr][32:64, :], in1=TH2[32:64, :], op=ALU.add),
                     reads=[b_PJX[pjr], b_TH2], writes=[b_QB[M]], partial=True)
            return f

        for ti in KT_OF[c]:
            its.append(kt(ti))
        for g0 in VG_OF[c]:
            its.append(vg(g0))
        its.append(qq(c))
        if c >= 1:
            its.insert(0, z_item(c - 1, WBIG[0], b_W[0], instream))
        return its

    for pair in range(8):
        hA, hB = 2 * pair, 2 * pair + 1
        bt = {}
        for hi, h in enumerate((hA, hB)):
            S.dma(POOL, WKVn[:, hi * 64:(hi + 1) * 64], w_ukv_h[:, h, 0:64], writes=[b_WMS], partial=(hi > 0), batch=bt)
            S.dma(POOL, WKVv[:, hi * 64:(hi + 1) * 64], w_ukv_h[:, h, 64:128], writes=[b_WMS], partial=True, batch=bt)
            S.dma(POOL, WQn[:, :, hi * 64:(hi + 1) * 64], w_uq_c[:, :, h, 0:64], writes=[b_WMS], partial=True, batch=bt)
            S.dma(POOL, WQp[:, :, hi * 32:(hi + 1) * 32], w_uq_c[:, :, h, 64:96], writes=[b_WMS], partial=True, batch=bt)
            S.dma(POOL, WQr[:, :, hi * 32:hi * 32 + 16], w_uq_c[:, :, h, 80:96], writes=[b_WMS], partial=True, batch=bt)
            S.dma(POOL, WQr[:, :, hi * 32 + 16:hi * 32 + 32], w_uq_c[:, :, h, 64:80], writes=[b_WMS], partial=True, batch=bt)
        for hi in range(2):
            S.op(POOL, lambda e: e.tensor_scalar(out=WQr[:, :, hi * 32:hi * 32 + 16], in0=WQr[:, :, hi * 32:hi * 32 + 16], scalar1=-1.0, scalar2=None, op0=ALU.mult),
                 reads=[b_WMS], writes=[b_neg], partial=(hi > 0))
        load_w_in_cols(WBIG[0], b_W[0], O_ZM + pair * 128)
        hdA = dict(KT=KA, QT=QA, b_K=b_KA, b_Q=b_QA, KR=96, vcol0=0, is_B=False, OG=OGA, b_OG=b_OGA)
        hdB = dict(KT=KB, QT=QB, b_K=b_KB, b_Q=b_QB, KR=96, vcol0=64, is_B=True, OG=OGA, b_OG=b_OGA)
        if debug and pair == 0:
            for c in range(0, 5):
                for it in mla_items(c, False):
                    it()
            S.dma(SP, dbg["KA"][0:96, :], KA[0:96, :], reads=b_KA)
            S.dma(SP, dbg["QA"][0:96, :], QA[0:96, :], reads=b_QA)
            S.dma(SP, dbg["VP"], VP[:, :, :].rearrange("p b c -> p (b c)"), reads=b_VP + [b_VPones])
            if stop <= 4:
                attention_multi([(hdA, 0), (hdA, 1), (hdA, 2), (hdA, 3)], MLA_SCALE, pair)
                S.dma(SP, dbg["OGA"][0:64, 0:NOWN], OGA[0:64, 0, :], reads=[b_OGA])
                return finalize()
            order = [(hdA, 0), (hdB, 0), (hdA, 1), (hdB, 1), (hdA, 2), (hdB, 2), (hdA, 3), (hdB, 3)]
            attention_multi(order, MLA_SCALE, pair)
        else:
            run_pair(mla_items, hdA, hdB, MLA_SCALE, pair)

    if debug:
        S.dma(SP, dbg["OGA"], OGA[:, :, :].rearrange("p c t -> p (c t)"), reads=[b_OGA])

    if stop <= 5:
        return finalize()
    barrier()

    b_cum = buf("cumwork")
    S.op(ACT, lambda e: e.activation(out=LFv, in_=FLh[:, :, :], func=AF.Exp, scale=-1.0), reads=[b_FLh], writes=[b_cum])
    S.op(ACT, lambda e: e.activation(out=LFv, in_=LFv, func=AF.Ln, bias=1.0), reads=[b_cum], writes=[b_cum])
    S.op(DVE, lambda e: e.scalar_tensor_tensor(out=LFv, in0=LFv, scalar=-1.0, in1=valid_t[:, :].unsqueeze(1).broadcast_to([128, 16, NB]), op0=ALU.mult, op1=ALU.mult),
         reads=[b_cum] + sc, writes=[b_cum])
    S.op(POOL, lambda e: e.memset(SEG, 1.0), writes=[buf("SEG")])
    S.op(POOL, lambda e: e.memset(SEG[:, :, 0:1], 0.0), reads=[buf("SEG")], writes=[buf("SEG")])
    S.op(DVE, lambda e: e.tensor_tensor_scan(out=INC.rearrange("p a b -> p (a b)"), data0=SEG.rearrange("p a b -> p (a b)"),
                                              data1=LFv.rearrange("p a b -> p (a b)"), initial=0.0, op0=ALU.mult, op1=ALU.add),
         reads=[b_cum, buf("SEG")], writes=[buf("INC")])
    S.op(DVE, lambda e: e.tensor_tensor(out=INC, in0=INC, in1=LFv, op=ALU.subtract), reads=[buf("INC"), b_cum], writes=[buf("INC")])
    LF2 = LFv.rearrange("p a b -> p (a b)")
    EX2 = INC.rearrange("p a b -> p (a b)")
    CU2 = CUM.rearrange("p a b -> p (a b)")
    for (c0, cn, pj) in ((0, 495, 0), (495, 33, 1)):
        S.op(PE, lambda e: e.matmul(PJX[pj][:, 0:cn], lhsT=tri_f[:, :], rhs=LF2[:, c0:c0 + cn], start=True, stop=False),
             reads=[b_cum, buf("tri_f")], writes=[b_PJX[pj]])
        S.op(PE, lambda e: e.matmul(PJX[pj][:, 0:cn], lhsT=ones_f[:, :], rhs=EX2[:, c0:c0 + cn], start=False, stop=True),
             reads=[buf("INC"), buf("ones_f")], writes=[b_PJX[pj]], partial=True)
        S.op(DVE, lambda e: e.tensor_copy(out=CU2[:, c0:c0 + cn], in_=PJX[pj][:, 0:cn]), reads=[b_PJX[pj]], writes=[buf("CUM")], partial=True)
    if debug:
        S.dma(SP, dbg["cum"], CU2, reads=[buf("CUM")])
    bk = buf("TKsrc")
    S.op(DVE, lambda e: e.tensor_scalar(out=C8, in0=CUM, scalar1=-8.0, scalar2=None, op0=ALU.mult), reads=[buf("CUM")], writes=[buf("C8")])
    S.op(DVE, lambda e: e.tensor_copy(out=HI, in_=C8), reads=[buf("C8")], writes=[buf("HI")])
    S.op(DVE, lambda e: e.tensor_tensor(out=R1, in0=C8, in1=HI, op=ALU.subtract), reads=[buf("C8"), buf("HI")], writes=[buf("R1")])
    S.op(DVE, lambda e: e.tensor_copy(out=MID, in_=R1), reads=[buf("R1")], writes=[buf("MID")])
    S.op(DVE, lambda e: e.tensor_tensor(out=R1, in0=R1, in1=MID, op=ALU.subtract), reads=[buf("R1"), buf("MID")], writes=[buf("R1")])
    TKs4 = TKsrc.rearrange("p b (h f) -> p b h f", f=4)
    TQs4 = TQsrc.rearrange("p m (h f) -> p m h f", f=4)
    S.op(POOL, lambda e: e.memset(TKsrc, 1.0), writes=[bk])
    S.op(POOL, lambda e: e.memset(TQsrc, 1.0), writes=[buf("TQsrc")])
    hb = lambda a: a.rearrange("p h b -> p b h")
    S.op(DVE, lambda e: e.tensor_copy(out=TKs4[:, :, :, 1], in_=hb(HI)), reads=[buf("HI"), bk], writes=[bk])
    S.op(DVE, lambda e: e.tensor_copy(out=TKs4[:, :, :, 2], in_=hb(MID)), reads=[buf("MID")], writes=[bk], partial=True)
    S.op(DVE, lambda e: e.tensor_copy(out=TKs4[:, :, :, 3], in_=hb(R1)), reads=[buf("R1")], writes=[bk], partial=True)
    cum_own = CUM[:, :, 1:NB].rearrange("p h (m two) -> p m h two", two=2)[:, :, :, 1]
    S.op(DVE, lambda e: e.tensor_scalar(out=TQs4[:, :, :, 0], in0=cum_own, scalar1=8.0, scalar2=None, op0=ALU.mult),
         reads=[buf("CUM"), buf("TQsrc")], writes=[buf("TQsrc")])
    b_TK = buf("TK")
    for g0 in range(0, NB, 4):
        nb_ = min(4, NB - g0)
        pj = next_pj()
        tp = PJX[pj][:, :].bitcast(BF16)
        for j in range(nb_):
            S.op(PE, lambda e: e.transpose(out=tp[0:64, j * 128:(j + 1) * 128], in_=TKsrc[:, g0 + j, :], identity=ident_bf[:, :]),
                 reads=[bk, buf("ident_bf")], writes=[b_PJX[pj]], partial=(j > 0))
        copy_op(evac_engine(), TK[0:64, g0 * 128:(g0 + nb_) * 128], tp[0:64, 0:nb_ * 128], reads=[b_PJX[pj]], writes=[b_TK])
    for g0 in range(0, 16, 4):
        pj = next_pj()
        tp = PJX[pj][:, :].bitcast(BF16)
        for j in range(4):
            S.op(PE, lambda e: e.transpose(out=tp[0:64, j * 128:(j + 1) * 128], in_=TQsrc[:, g0 + j, :], identity=ident_bf[:, :]),
                 reads=[buf("TQsrc"), buf("ident_bf")], writes=[b_PJX[pj]], partial=(j > 0))
        copy_op(evac_engine(), TK[64:128, g0 * 128:(g0 + 4) * 128], tp[0:64, 0:512], reads=[b_PJX[pj]], writes=[b_TK])
    if debug:
        S.dma(SP, dbg["TK"], TK, reads=[b_TK])

    if stop <= 6:
        return finalize()
    barrier()

    def fox_items(c, instream):
        its = []
        if c > 3:
            return [z_item(3, WBIG[0], b_W[0], instream)]

        def kt(ti):
            def f():
                t0, tn = ntt[ti]
                pj = bank_for(instream)
                for cc in range(8):
                    S.op(PE, lambda e: e.matmul(PJX[pj][:, 0:tn], lhsT=WBIG[1][:, cc, :], rhs=uT[:, cc, t0:t0 + tn], start=(cc == 0), stop=(cc == 7)),
                         reads=b_uT + [b_W[1]], writes=[b_PJX[pj]], partial=(cc > 0))
                ev = ev_for(instream)
                copy_op(ev, KA[0:64, t0:t0 + tn], PJX[pj][0:64, 0:tn], reads=[b_PJX[pj]], writes=[b_KA[c]])
                copy_op(ev, KB[0:64, t0:t0 + tn], PJX[pj][64:128, 0:tn], reads=[b_PJX[pj]], writes=[b_KB[c]])
            return f

        def vg(g0):
            def f():
                nb_ = min(4, NB - g0)
                pj = bank_for(instream)
                pv = PJX[pj].rearrange("p (b c) -> p b c", b=4)
                for j in range(nb_):
                    blk = g0 + j
                    for cc in range(8):
                        S.op(PE, lambda e: e.matmul(pv[:, j, :], lhsT=uT[:, cc, blk * 128:(blk + 1) * 128], rhs=WBIG[2][:, cc, :], start=(cc == 0), stop=(cc == 7)),
                             reads=[b_uT[blk], b_W[2]], writes=[b_PJX[pj]], partial=(j > 0 or cc > 0))
                ev = ev_for(instream)
                copy_op(ev, VP[:, g0:g0 + nb_, 0:64], pv[:, 0:nb_, 0:64], reads=[b_PJX[pj]], writes=[b_VP[c]])
                copy_op(ev, VP[:, g0:g0 + nb_, 128:192], pv[:, 0:nb_, 64:128], reads=[b_PJX[pj]], writes=[b_VP[c]])
            return f

        def qq(M):
            def f():
                tok = own_tok(M)
                pj = bank_for(instream)
                for cc in range(8):
                    S.op(PE, lambda e: e.matmul(PJX[pj][:, :], lhsT=WBIG[3][:, cc, :], rhs=urhs(cc, M), start=(cc == 0), stop=(cc == 7)),
                         reads=b_uT + [b_W[3]], writes=[b_PJX[pj]], partial=(cc > 0))
                ev = ev_for(instream)
                copy_op(ev, QA[0:64, tok], PJX[pj][0:64, :], reads=[b_PJX[pj]], writes=[b_QA[M]])
                copy_op(ev, QB[0:64, tok], PJX[pj][64:128, :], reads=[b_PJX[pj]], writes=[b_QB[M]])
            return f

        for ti in KT_OF[c]:
            its.append(kt(ti))
        for g0 in VG_OF[c]:
            its.append(vg(g0))
        its.append(qq(c))
        if c >= 1:
            its.insert(0, z_item(c - 1, WBIG[0], b_W[0], instream))
        return its

    for pair in range(8):
        hA, hB = 2 * pair, 2 * pair + 1
        load_w_in_cols(WBIG[1], b_W[1], O_FK + pair * 128)
        load_w_in_cols(WBIG[2], b_W[2], O_FV + pair * 128)
        load_w_in_cols(WBIG[3], b_W[3], O_FQ + pair * 128)
        load_w_in_cols(WBIG[0], b_W[0], O_ZF + pair * 128)
        S.dma(SP, KA[64:68, :], TK[4 * hA:4 * hA + 4, :], reads=[b_TK], writes=b_KA, partial=True)
        S.dma(SP, KB[64:68, :], TK[4 * hB:4 * hB + 4, :], reads=[b_TK], writes=b_KB, partial=True)
        S.dma(SP, QA[64:68, :], TK[64 + 4 * hA:64 + 4 * hA + 4, 0:NOWN], reads=[b_TK], writes=b_QA, partial=True)
        S.dma(SP, QB[64:68, :], TK[64 + 4 * hB:64 + 4 * hB + 4, 0:NOWN], reads=[b_TK], writes=b_QB, partial=True)

        hdA = dict(KT=KA, QT=QA, b_K=b_KA, b_Q=b_QA, KR=68, vcol0=0, is_B=False, OG=OGB, b_OG=b_OGB)
        hdB = dict(KT=KB, QT=QB, b_K=b_KB, b_Q=b_QB, KR=68, vcol0=64, is_B=True, OG=OGB, b_OG=b_OGB)
        run_pair(fox_items, hdA, hdB, FOX_SCALE, pair)

    if debug:
        S.dma(SP, dbg["OGB"], OGB[:, :, :].rearrange("p c t -> p (c t)"), reads=[b_OGB])

    if stop <= 7:
        return finalize()
    barrier()
    w_bra_c = w_bra.rearrange("(c p) n -> p c n", p=128)
    w_brf_c = w_brf.rearrange("(c p) n -> p c n", p=128)
    w_out_c = w_out.rearrange("(c p) n -> p c n", p=128)
    b_CH = [buf("CH%d" % i) for i in range(8)]
    b_MT = buf("MT")
    b_FT = [buf("FT%d" % i) for i in range(4)]
    b_WO = buf("WO")
    b_XR = [buf("XR0"), buf("XR1")]
    b_RES = [buf("RES0"), buf("RES1")]
    WO = view(OGAf, 0, [128, 8, D], BF16)
    gpost_b = view(OGAf, 16384, [128, D], F32)
    XR = [view(OGAf, 20480 + i * 4096, [128, D], F32) for i in range(2)]
    RES = [view(ARENA, 32768 + i * 4096, [128, D], F32) for i in range(2)]
    sqj2 = view(ARENA, 40960, [128, D], BF16)
    slots = [(ST[0][:, 0, :], ST[0][:, 1, :], [b_ST[0]]), (ST[1][:, 0, :], ST[1][:, 1, :], [b_ST[1]]),
             (OP[0][:, :], PJX[0], [b_OP[0], b_PJX[0]]), (OP[1][:, :], PJX[1], [b_OP[1], b_PJX[1]])]
    grp = [0]

    def branch_pass(w_y_c, gcol0, OGsrc, b_OGsrc, chbase, accumulate):
        def load(cc):
            s0 = chbase + (cc % 2) * 2
            S.dma(POOL, CHW[s0][:, :, :], w_y_c[:, :, cc * 128:(cc + 1) * 128], writes=[b_CH[s0]])
            S.dma(POOL, CHW[s0 + 1][:, :, :], w_in_c[:, :, gcol0 + cc * 128:gcol0 + (cc + 1) * 128], writes=[b_CH[s0 + 1]])
        load(0)
        for cc in range(8):
            if cc + 1 < 8:
                load(cc + 1)
            s0 = chbase + (cc % 2) * 2
            for M in range(4):
                g = grp[0] % 4
                grp[0] += 1
                tok = own_tok(M)
                ybank, gbank, bb = slots[g]
                for c in range(8):
                    S.op(PE, lambda e: e.matmul(ybank, lhsT=CHW[s0][:, c, :], rhs=OGsrc[:, c, tok], start=(c == 0), stop=(c == 7)),
                         reads=[b_OGsrc, b_CH[s0]], writes=bb, partial=(c > 0))
                for c in range(8):
                    S.op(PE, lambda e: e.matmul(gbank, lhsT=CHW[s0 + 1][:, c, :], rhs=urhs(c, M), start=(c == 0), stop=(c == 7)),
                         reads=b_uT + [b_CH[s0 + 1]], writes=bb, partial=True)
                k = g
                S.op(ACT, lambda e: e.activation(out=FT[k], in_=gbank, func=AF.Tanh, scale=0.5), reads=bb, writes=[b_FT[k]])
                if not accumulate:
                    S.op(DVE, lambda e: e.scalar_tensor_tensor(out=MT[:, cc, tok], in0=FT[k], scalar=1.0, in1=ybank, op0=ALU.add, op1=ALU.mult),
                         reads=[b_FT[k]] + bb, writes=[b_MT], partial=True)
                else:
                    S.op(DVE, lambda e: e.scalar_tensor_tensor(out=FT[k], in0=FT[k], scalar=1.0, in1=ybank, op0=ALU.add, op1=ALU.mult),
                         reads=[b_FT[k]] + bb, writes=[b_FT[k]])
                    S.op(POOL, lambda e: e.tensor_tensor(out=MT[:, cc, tok], in0=MT[:, cc, tok], in1=FT[k], op=ALU.add),
                         reads=[b_FT[k], b_MT], writes=[b_MT], partial=True)

    branch_pass(w_bra_c, O_GA, OGA, b_OGA, 0, False)
    bt = {}
    for hh in range(2):
        S.dma(POOL, WO[:, :, hh * 512:(hh + 1) * 512], w_out_c[:, :, hh * 512:(hh + 1) * 512], writes=[b_WO, b_OGA], partial=(hh > 0), batch=bt)
    S.dma(SP, gpost_b[:, :], gpost.broadcast_to([128, D]), writes=[buf("gpost_b"), b_OGA], partial=True)
    outd_b = outd.rearrange("(m p) d -> m p d", p=128)
    branch_pass(w_brf_c, O_GB, OGB, b_OGB, 4, True)

    barrier()

    def f2_X(m):
        g = m % 2
        S.dma(SP, XR[g][:, :], xl_b[2 + 2 * m, :, :], writes=[b_XR[g]])
        for hh in range(2):
            for c in range(8):
                S.op(PE, lambda e: e.matmul(ST[g][:, hh, :], lhsT=MT[:, c, m * 128:(m + 1) * 128], rhs=WO[:, c, hh * 512:(hh + 1) * 512], start=(c == 0), stop=(c == 7)),
                     reads=[b_MT, b_WO], writes=[b_ST[g]], partial=(c > 0 or hh > 0))
        sb_ = buf("fstat%d" % m)
        s0 = m * 8
        S.op(ACT, lambda e: e.activation(out=sqj2.rearrange("p (a b) -> p a b", a=2), in_=ST[g][:, :, :], func=AF.Square, accum_out=stat[:, s0:s0 + 1]),
             reads=[b_ST[g]], writes=[buf("sqj2"), sb_])
        S.op(DVE, lambda e: e.tensor_scalar(out=stat[:, s0 + 1:s0 + 2], in0=stat[:, s0:s0 + 1], scalar1=1.0 / D, scalar2=0.25 * RMS_EPS, op0=ALU.mult, op1=ALU.add),
             reads=[sb_], writes=[sb_])
        S.op(POOL, lambda e: e.tensor_tensor(out=stat[:, s0 + 2:s0 + 3], in0=stat[:, s0 + 1:s0 + 2], in1=mhalf[:, 0:1], op=ALU.pow),
             reads=[sb_, buf("mhalf")], writes=[sb_])

    def f2_Y(m):
        g = m % 2
        sb_ = buf("fstat%d" % m)
        s0 = m * 8
        S.op(DVE, lambda e: e.scalar_tensor_tensor(out=RES[g].rearrange("p (a b) -> p a b", a=2), in0=ST[g][:, :, :], scalar=stat[:, s0 + 2:s0 + 3],
                                                   in1=gpost_b.rearrange("p (a b) -> p a b", a=2), op0=ALU.mult, op1=ALU.mult),
             reads=[b_ST[g], sb_, buf("gpost_b")], writes=[b_RES[g]])
        S.op(POOL, lambda e: e.tensor_tensor(out=RES[g][:, :], in0=RES[g][:, :], in1=XR[g][:, :], op=ALU.add),
             reads=[b_RES[g], b_XR[g]], writes=[b_RES[g]])
        S.dma(ACT, outd_b[m, :, :], RES[g][:, :], reads=[b_RES[g]])

    for i in range(17):
        if i < 16:
            f2_X(i)
        if i >= 1:
            f2_Y(i - 1)

    S._wait(SP, dict(S.all_dma_tokens))
    barrier()
    return nc, es, S


_CACHE = {}


def build_two_pass(debug=False, stop=99):
    _nc0, _es0, s0 = build_program(debug=debug, stop=stop, needed=None)
    needed = set(s0.waited)
    nc, _es, _s = build_program(debug=debug, stop=stop, needed=needed)
    _KEEP.append((_es, _es0))
    return nc


_KEEP = []


def _layout_core(x, meta, b, p):
    xl = np.zeros((T, D), np.float32)
    xl[0:16] = meta
    pos = np.zeros((128, NB), np.float32)
    valid = np.zeros((128, NB), np.float32)
    pos[0:16, 0] = np.arange(16, dtype=np.float32)
    valid[0:16, 0] = 1.0
    tt = np.arange(128, dtype=np.float32)
    for m in range(16):
        for j, g in enumerate((2 * m + p - 1, 2 * m + p)):
            blk = 1 + 2 * m + j
            if g < 0:
                continue
            xl[blk * 128:(blk + 1) * 128] = x[b, g * 128:(g + 1) * 128]
            pos[:, blk] = 16 + 128 * g + tt
            valid[:, blk] = 1.0
    return xl, pos, valid


def kernel(x, meta_tokens, pre_norm_g, w_in, fox_forget_b, mla_q_norm_g, mla_kv_norm_g,
           w_uq, w_ukv, w_br_mla, w_br_fox, w_out, post_norm_g):
    f = lambda a: np.ascontiguousarray(np.asarray(a, dtype=np.float32))
    x = f(x)
    meta = f(meta_tokens)
    debug = bool(int(os.environ.get("MK_DEBUG", "0")))
    stop = int(os.environ.get("MK_STOP", "99"))
    key = ("prog", debug, stop)
    if key not in _CACHE:
        _CACHE[key] = build_two_pass(debug=debug, stop=stop)
    nc = _CACHE[key]
    invf = (10000.0 ** (-(np.arange(16, dtype=np.float32)) / np.float32(16.0))).astype(np.float32).reshape(1, 16)
    shared = {
        "w_in": f(w_in)[0], "w_uq": f(w_uq)[0], "w_ukv": f(w_ukv)[0], "w_bra": f(w_br_mla)[0], "w_brf": f(w_br_fox)[0],
        "w_out": f(w_out)[0], "gpre": f(pre_norm_g).reshape(1, D), "gq": f(mla_q_norm_g).reshape(1, 256),
        "gkv": f(mla_kv_norm_g).reshape(1, 128), "gpost": f(post_norm_g).reshape(1, D), "fb": f(fox_forget_b).reshape(1, 16),
        "invf": invf,
    }
    in_maps = []
    for c in range(8):
        b, p = c // 2, c % 2
        xl, pos, valid = _layout_core(x, meta, b, p)
        d = dict(shared)
        d.update({"xl": xl, "posv": pos, "validv": valid})
        in_maps.append(d)
    res = run_bass_kernel_spmd(nc, in_maps, core_ids=list(range(8)))
    out = np.empty((4, 4096, D), np.float32)
    for c in range(8):
        b, p = c // 2, c % 2
        o = np.asarray(res.results[c]["out"]).reshape(16, 128, D)
        for m in range(16):
            g = 2 * m + p
            out[b, g * 128:(g + 1) * 128] = o[m]
    if debug:
        kernel.last_results = res.results
    return out
```

```python
import os
import numpy as np
from contextlib import ExitStack
import concourse.bass as bass
import concourse.mybir as mybir
from concourse.bass_utils import run_bass_kernel_spmd

F32 = mybir.dt.float32
BF16 = mybir.dt.bfloat16
ALU = mybir.AluOpType
AF = mybir.ActivationFunctionType
AX = mybir.AxisListType

D = 1024
NB = 33
T = NB * 128
NOWN = 2048
RMS_EPS = 1e-6
MLA_SCALE = 1.0 / float(np.sqrt(96.0))
FOX_SCALE = 0.125
O_CQ, O_CKV, O_KPE, O_ZM, O_FQ, O_FK, O_FV, O_FL, O_ZF, O_GA, O_GB = 0, 256, 384, 416, 1440, 2464, 3488, 4512, 4528, 5552, 6576
TWO_PI = 2.0 * np.pi
MAGIC = 12582912.0
CW1 = 6.28125
CW2 = float(np.float32(TWO_PI - CW1))
CW3 = float(TWO_PI - CW1 - CW2)
NEG_BIG = -30000.0


class Buf:
    def __init__(self, name, excl=False):
        self.name = name
        self.excl = excl
        self.w = {}
        self.r = {}
        self.war = {}
        self.phase = "w"


class EngQ:
    def __init__(self, nc, eng, name, es, is_pe=False):
        self.nc = nc
        self.eng = eng
        self.name = name
        self.sem = es.enter_context(nc.semaphore("q_" + name))
        self.count = 0
        self.real = 0
        self.vmap = {}
        self.seen = {}
        self.is_pe = is_pe


class DSem:
    def __init__(self, nc, key, i, es):
        self.key = key
        self.sem = es.enter_context(nc.semaphore("d%d" % i))
        self.count = 0


class Sched:
    def __init__(self, nc, es, needed=None, ndma=24):
        self.nc = nc
        self.needed = needed
        self.waited = set()
        self.pe = EngQ(nc, nc.tensor, "pe", es, is_pe=True)
        self.act = EngQ(nc, nc.scalar, "act", es)
        self.dve = EngQ(nc, nc.vector, "dve", es)
        self.pool = EngQ(nc, nc.gpsimd, "pool", es)
        self.sp = EngQ(nc, nc.sync, "sp", es)
        self.qs = {q.name: q for q in (self.pe, self.act, self.dve, self.pool, self.sp)}
        self.dsems = {"sp": [DSem(nc, ("d", "sp", i), i, es) for i in range(ndma)],
                      "pool": [DSem(nc, ("d", "pool", i), 100 + i, es) for i in range(ndma)],
                      "act": [DSem(nc, ("d", "act", i), 200 + i, es) for i in range(8)]}
        self.dnext = {"sp": 0, "pool": 0, "act": 0}
        self.all_dma_tokens = {}

    def _collect(self, q, reads, writes, partial, deps):
        need = {}

        def add(d):
            for k, (s, v) in d.items():
                if k == q.name and q.is_pe:
                    continue
                if k not in need or need[k][1] < v:
                    need[k] = (s, v)

        for b in reads:
            add(b.w)
            if b.excl:
                add({k: t for k, t in b.r.items() if k != q.name})
        for b in writes:
            if b.phase == "r":
                b.war = b.r
                b.r = {}
                b.w = {}
                b.phase = "w"
            add(b.war)
            if not partial:
                add(b.w)
        for d in deps:
            add(d)
        return need

    def _wait(self, q, need):
        for k, (s, v) in need.items():
            if q.seen.get(k, 0) < v:
                q.seen[k] = v
                if isinstance(k, str):
                    self.waited.add((k, v))
                    if self.needed is None:
                        rv = v
                    else:
                        rv = self.qs[k].vmap[v]
                else:
                    rv = v
                q.eng.wait_ge(s, rv)

    def op(self, q, fn, reads=(), writes=(), partial=False, deps=()):
        need = self._collect(q, reads, writes, partial, deps)
        self._wait(q, need)
        ins = fn(q.eng)
        q.count += 1
        if self.needed is None or (q.name, q.count) in self.needed:
            q.real += 1
            q.vmap[q.count] = q.real
            ins.then_inc(q.sem, 1)
        tok = (q.sem, q.count)
        for b in reads:
            b.r[q.name] = tok
            b.phase = "r"
        for b in writes:
            b.w[q.name] = tok
        return {q.name: tok}

    def dma(self, q, out, in_, reads=(), writes=(), partial=False, deps=(), batch=None, **kw):
        need = self._collect(q, reads, writes, partial, deps)
        if batch is not None and batch.get("ds") is not None:
            ds = batch["ds"]
        else:
            pool_ = self.dsems[q.name]
            ds = pool_[self.dnext[q.name]]
            self.dnext[q.name] = (self.dnext[q.name] + 1) % len(pool_)
            if ds.count > 0:
                need[ds.key] = (ds.sem, ds.count)
            if batch is not None:
                batch["ds"] = ds
        self._wait(q, need)
        ins = q.eng.dma_start(out=out, in_=in_, **kw)
        ds.count += 16
        ins.then_inc(ds.sem, 16)
        tok = (ds.sem, ds.count)
        for b in reads:
            b.r[ds.key] = tok
            b.phase = "r"
        for b in writes:
            b.w[ds.key] = tok
        self.all_dma_tokens[ds.key] = tok
        return {ds.key: tok}


def build_program(debug=False, stop=99, needed=None):
    nc = bass.Bass("TRN2", target_bir_lowering=False)
    es = ExitStack()

    def dram(name, shape, dt=F32, kind="ExternalInput"):
        return nc.dram_tensor(name, list(shape), dt, kind=kind).ap()

    xl = dram("xl", [T, D])
    posv = dram("posv", [128, NB])
    validv = dram("validv", [128, NB])
    w_in = dram("w_in", [D, 7600])
    w_uq = dram("w_uq", [256, 1536])
    w_ukv = dram("w_ukv", [128, 2048])
    w_bra = dram("w_bra", [D, D])
    w_brf = dram("w_brf", [D, D])
    w_out = dram("w_out", [D, D])
    gpre = dram("gpre", [1, D])
    gq = dram("gq", [1, 256])
    gkv = dram("gkv", [1, 128])
    gpost = dram("gpost", [1, D])
    fb = dram("fb", [1, 16])
    invf = dram("invf", [1, 16])
    outd = dram("out", [NOWN, D], kind="ExternalOutput")
    dbg = {}
    if debug:
        dbg["uT"] = dram("dbg_uT", [128, 8 * T], BF16, kind="ExternalOutput")
        dbg["ckvnT"] = dram("dbg_ckvnT", [128, T], BF16, kind="ExternalOutput")
        dbg["cqnT"] = dram("dbg_cqnT", [128, 2 * NOWN], BF16, kind="ExternalOutput")
        dbg["KA"] = dram("dbg_KA", [128, T], BF16, kind="ExternalOutput")
        dbg["QA"] = dram("dbg_QA", [128, NOWN], BF16, kind="ExternalOutput")
        dbg["VP"] = dram("dbg_VP", [128, NB * 192], BF16, kind="ExternalOutput")
        dbg["OGA"] = dram("dbg_OGA", [128, 8 * NOWN], BF16, kind="ExternalOutput")
        dbg["OGB"] = dram("dbg_OGB", [128, 8 * NOWN], BF16, kind="ExternalOutput")
        dbg["cum"] = dram("dbg_cum", [128, 528], F32, kind="ExternalOutput")
        dbg["TK"] = dram("dbg_TK", [128, T], BF16, kind="ExternalOutput")

    S = Sched(nc, es, needed=needed)
    PE, ACT, DVE, POOL, SP = S.pe, S.act, S.dve, S.pool, S.sp

    def sb(name, shape, dt):
        return es.enter_context(nc.sbuf_tensor(name, list(shape), dt))

    def ps(name, shape, dt=F32):
        return es.enter_context(nc.psum_tensor(name, list(shape), dt))

    uT = sb("uT", [128, 8, T], BF16)
    OGAf = sb("OGAf", [128, 8 * NOWN], BF16)
    OGBf = sb("OGBf", [128, 8 * NOWN], BF16)
    ARENA = sb("ARENA", [128, 24000], BF16)
    TKreg = sb("TKreg", [128, T], BF16)
    WBIG = [sb("WBIG%d" % i, [128, 8, 128], BF16) for i in range(4)]
    RT = sb("RT", [128, 512], F32)
    TMP = sb("TMP", [128, 512], F32)
    ident_bf = sb("ident_bf", [128, 128], BF16)
    ident_f = sb("ident_f", [128, 128], F32)
    maskneg = sb("maskneg", [128, 128], BF16)
    tri_f = sb("tri_f", [128, 128], F32)
    ones_f = sb("ones_f", [128, 128], F32)
    gq_b = sb("gq_b", [128, 256], F32)
    gkv_b = sb("gkv_b", [128, 128], F32)
    fb_b = sb("fb_b", [128, 16], F32)
    invf_b = sb("invf_b", [128, 16], F32)
    pos_t = sb("pos_t", [128, NB], F32)
    valid_t = sb("valid_t", [128, NB], F32)
    FLh = sb("FLh", [128, 16, NB], F32)
    mhalf = sb("mhalf", [128, 4], F32)
    stat = sb("stat", [128, 8 * NB], F32)
    WMS = sb("WMS", [128, 768], BF16)

    def view(base, byte_off, shape, dt):
        esz = 2 if dt == BF16 else 4
        n = int(np.prod(shape[1:]))
        assert byte_off % 4 == 0
        a = base[:, byte_off // 2: byte_off // 2 + n * esz // 2]
        if dt == F32:
            a = a.bitcast(F32)
        if len(shape) == 3:
            a = a.rearrange("p (a b) -> p a b", a=shape[1])
        return a

    OGA = view(OGAf, 0, [128, 8, NOWN], BF16)
    OGB = view(OGBf, 0, [128, 8, NOWN], BF16)
    KA = view(ARENA, 0, [128, T], BF16)
    KB = view(ARENA, 8448, [128, T], BF16)
    QA = view(ARENA, 16896, [128, NOWN], BF16)
    QB = view(ARENA, 20992, [128, NOWN], BF16)
    VP = view(ARENA, 25088, [128, NB, 192], BF16)
    PT = [view(ARENA, 37760 + i * 2048, [128, 2, 512], BF16) for i in range(3)]
    SZ = view(ARENA, 43904, [128, NOWN], BF16)
    TH = view(ARENA, 37760 + 2 * 2048, [128, 512], F32)
    xs = [view(OGAf, i * 4096, [128, 1024], F32) for i in range(3)]
    xn = [view(OGAf, 12288 + i * 2048, [128, 1024], BF16) for i in range(2)]
    sqj = view(OGAf, 16384, [128, 1024], BF16)
    gpre_b = view(OGAf, 18432, [128, 1024], F32)
    LT = [view(OGAf, 22528 + i * 1024, [128, 416], BF16) for i in range(2)]
    CS_tm = view(OGAf, 24576, [128, NB, 16], F32)
    SN_tm = view(OGAf, 26688, [128, NB, 16], F32)
    WL = view(ARENA, 16896, [128, 8, 432], BF16)
    cqnT = view(OGBf, 0, [128, 2, NOWN], BF16)
    CSq = view(OGBf, 8192, [128, NOWN], F32)
    SNq = view(OGBf, 16384, [128, NOWN], F32)
    ANGs = view(OGBf, 24576, [128, NB, 16], F32)
    KFs = view(OGBf, 26688, [128, NB, 16], F32)
    RRs = view(OGBf, 28800, [128, NB, 16], F32)
    ckvnT = view(TKreg, 0, [128, T], BF16)
    TK = view(TKreg, 0, [128, T], BF16)
    TKsrc = view(ARENA, 0, [128, NB, 64], BF16)
    TQsrc = view(ARENA, 4224, [128, 16, 64], BF16)
    CUM = view(ARENA, 6272, [128, 16, NB], F32)
    LFv = view(ARENA, 8384, [128, 16, NB], F32)
    INC = view(ARENA, 10496, [128, 16, NB], F32)
    SEG = view(ARENA, 12608, [128, 16, NB], F32)
    C8 = view(ARENA, 14720, [128, 16, NB], F32)
    R1 = view(ARENA, 16832, [128, 16, NB], F32)
    HI = view(ARENA, 18944, [128, 16, NB], BF16)
    MID = view(ARENA, 20000, [128, 16, NB], BF16)
    MT = view(ARENA, 0, [128, 8, NOWN], BF16)
    FT = [view(ARENA, 32768 + i * 2048, [128, 512], F32) for i in range(4)]
    CHW = [WBIG[i] for i in range(4)] + [view(TKreg, i * 2048, [128, 8, 128], BF16) for i in range(4)]

    ST = [ps("ST%d" % i, [128, 2, 512]) for i in range(2)]
    OP = [ps("OP%d" % i, [128, 512]) for i in range(2)]
    PJ_ = [ps("PJ%d" % i, [128, 512]) for i in range(2)]

    B = {}

    def buf(name, excl=False):
        if name not in B:
            B[name] = Buf(name, excl=excl)
        return B[name]

    b_uT = [buf("uT%d" % i) for i in range(NB)]
    b_ST = [buf("ST0", True), buf("ST1", True)]
    b_OP = [buf("OP0", True), buf("OP1", True)]
    b_PJ_ = [buf("PJ0", True), buf("PJ1", True)]
    PJX = [PJ_[0][:, :], PJ_[1][:, :], ST[0][:, 0, :], ST[1][:, 0, :]]
    b_PJX = [b_PJ_[0], b_PJ_[1], b_ST[0], b_ST[1]]
    b_PT = [buf("PT%d" % i) for i in range(3)]
    b_xs = [buf("xs%d" % i) for i in range(3)]
    b_xn = [buf("xn%d" % i) for i in range(2)]
    b_LT = [buf("LT0"), buf("LT1")]
    b_W = [buf("WBIG%d" % i) for i in range(4)]
    bc_ = buf("consts")

    pj_i = [0]

    def next_pj():
        i = pj_i[0]
        pj_i[0] = (i + 1) % 4
        return i

    ev_i = [0]

    def evac_engine():
        ev_i[0] += 1
        return ACT if (ev_i[0] % 2 == 0) else DVE

    def copy_op(q, out, in_, reads, writes, partial=True):
        if q is ACT:
            return S.op(ACT, lambda e: e.activation(out=out, in_=in_, func=AF.Copy), reads=reads, writes=writes, partial=partial)
        return S.op(q, lambda e: e.tensor_copy(out=out, in_=in_), reads=reads, writes=writes, partial=partial)

    def barrier():
        qs = [PE, ACT, DVE, POOL, SP]
        toks = {q.name: (q.sem, q.count) for q in qs if q.count > 0}
        toks.update(S.all_dma_tokens)
        for q in qs:
            S._wait(q, dict(toks))


    def finalize():
        S._wait(SP, dict(S.all_dma_tokens))
        barrier()
        return nc, es, S

    S.op(POOL, lambda e: e.memset(ident_bf[:, :], 1.0), writes=[buf("ident_bf")])
    S.op(POOL, lambda e: e.affine_select(out=ident_bf[:, :], in_=ident_bf[:, :], pattern=[[1, 128]], compare_op=ALU.is_equal,
                                          fill=0.0, base=0, channel_multiplier=-1), reads=[buf("ident_bf")], writes=[buf("ident_bf")])
    S.op(POOL, lambda e: e.memset(ident_f[:, :], 1.0), writes=[buf("ident_f")])
    S.op(POOL, lambda e: e.affine_select(out=ident_f[:, :], in_=ident_f[:, :], pattern=[[1, 128]], compare_op=ALU.is_equal,
                                          fill=0.0, base=0, channel_multiplier=-1), reads=[buf("ident_f")], writes=[buf("ident_f")])
    S.op(POOL, lambda e: e.memset(maskneg[:, :], NEG_BIG), writes=[buf("maskneg")])
    S.op(POOL, lambda e: e.affine_select(out=maskneg[:, :], in_=maskneg[:, :], pattern=[[-1, 128]], compare_op=ALU.is_gt,
                                          fill=0.0, base=0, channel_multiplier=1), reads=[buf("maskneg")], writes=[buf("maskneg")])
    S.op(POOL, lambda e: e.memset(tri_f[:, :], 1.0), writes=[buf("tri_f")])
    S.op(POOL, lambda e: e.affine_select(out=tri_f[:, :], in_=tri_f[:, :], pattern=[[1, 128]], compare_op=ALU.is_ge,
                                          fill=0.0, base=0, channel_multiplier=-1), reads=[buf("tri_f")], writes=[buf("tri_f")])
    S.op(POOL, lambda e: e.memset(ones_f[:, :], 1.0), writes=[buf("ones_f")])
    S.op(POOL, lambda e: e.memset(mhalf[:, :], -0.5), writes=[buf("mhalf")])

    for (dst, src, n) in [(gpre_b, gpre, D), (gq_b, gq, 256), (gkv_b, gkv, 128), (fb_b, fb, 16), (invf_b, invf, 16)]:
        S.dma(SP, dst[:, :], src.broadcast_to([128, n]), writes=[buf("smallconst")], partial=True)
    S.dma(SP, pos_t[:, :], posv[:, :], writes=[buf("smallconst")], partial=True)
    S.dma(SP, valid_t[:, :], validv[:, :], writes=[buf("smallconst")], partial=True)

    w_in_c = w_in.rearrange("(c p) n -> p c n", p=128)
    bt = {}
    S.dma(POOL, WL[:, :, 0:416], w_in_c[:, :, 0:416], writes=[buf("WL")], partial=True, batch=bt)
    S.dma(POOL, WL[:, :, 416:432], w_in_c[:, :, O_FL:O_FL + 16], writes=[buf("WL")], partial=True, batch=bt)

    b_tab = buf("tabscratch")
    sc = [buf("smallconst")]
    pos_b = pos_t[:, :].unsqueeze(2).broadcast_to([128, NB, 16])
    invf_bb = invf_b[:, :].unsqueeze(1).broadcast_to([128, NB, 16])
    S.op(DVE, lambda e: e.tensor_tensor(out=ANGs, in0=pos_b, in1=invf_bb, op=ALU.mult), reads=sc, writes=[b_tab])

    def make_table(dst, shift):
        S.op(DVE, lambda e: e.tensor_scalar(out=KFs, in0=ANGs, scalar1=float(shift), scalar2=float(1.0 / TWO_PI), op0=ALU.add, op1=ALU.mult),
             reads=[b_tab], writes=[buf("kf")])
        S.op(DVE, lambda e: e.tensor_scalar(out=KFs, in0=KFs, scalar1=MAGIC, scalar2=None, op0=ALU.add), reads=[buf("kf")], writes=[buf("kf")])
        S.op(DVE, lambda e: e.tensor_scalar(out=KFs, in0=KFs, scalar1=-MAGIC, scalar2=None, op0=ALU.add), reads=[buf("kf")], writes=[buf("kf")])
        S.op(DVE, lambda e: e.scalar_tensor_tensor(out=RRs, in0=KFs, scalar=-CW1, in1=ANGs, op0=ALU.mult, op1=ALU.add),
             reads=[buf("kf"), b_tab], writes=[buf("rr")])
        S.op(DVE, lambda e: e.scalar_tensor_tensor(out=RRs, in0=KFs, scalar=-CW2, in1=RRs, op0=ALU.mult, op1=ALU.add),
             reads=[buf("kf"), buf("rr")], writes=[buf("rr")])
        S.op(DVE, lambda e: e.scalar_tensor_tensor(out=RRs, in0=KFs, scalar=-CW3, in1=RRs, op0=ALU.mult, op1=ALU.add),
             reads=[buf("kf"), buf("rr")], writes=[buf("rr")])
        S.op(DVE, lambda e: e.tensor_scalar(out=RRs, in0=RRs, scalar1=float(shift), scalar2=float(np.pi), op0=ALU.add, op1=ALU.min),
             reads=[buf("rr")], writes=[buf("rr")])
        S.op(DVE, lambda e: e.tensor_scalar(out=RRs, in0=RRs, scalar1=float(-np.pi), scalar2=None, op0=ALU.max), reads=[buf("rr")], writes=[buf("rr")])
        S.op(ACT, lambda e: e.activation(out=dst[:, :, :], in_=RRs, func=AF.Sin), reads=[buf("rr")], writes=[buf("tables")], partial=True)

    make_table(SN_tm, 0.0)
    make_table(CS_tm, float(np.pi / 2))

    b_VPones = buf("VPones")
    S.op(DVE, lambda e: e.tensor_scalar(out=VP[:, :, 64:128], in0=valid_t[:, :].unsqueeze(2).broadcast_to([128, NB, 64]),
                                         scalar1=2.0, scalar2=None, op0=ALU.mult), reads=sc, writes=[b_VPones])

    xl_b = xl.rearrange("(b p) d -> b p d", p=128)
    b_ckvnT = buf("ckvnT")
    b_cqnT = buf("cqnT")
    def kcls(blk):
        ti = blk // 4
        return 0 if ti <= 2 else (1 if ti <= 4 else (2 if ti <= 6 else 3))

    b_KA = [buf("KA%d" % i) for i in range(4)]
    b_KB = [buf("KB%d" % i) for i in range(4)]
    b_VP = [buf("VP%d" % i) for i in range(4)]
    b_QA = [buf("QA%d" % i) for i in range(4)]
    b_QB = [buf("QB%d" % i) for i in range(4)]
    b_SZ = [buf("SZ%d" % i) for i in range(4)]
    b_FLh = buf("FLh")
    b_CSq = buf("CSq")
    tabs = [buf("tables")]

    TB = TMP[:, 0:64]
    bTB = buf("TMPn")
    for m in range(16):
        blk = 2 + 2 * m
        for (tab, dstT) in ((CS_tm, CSq), (SN_tm, SNq)):
            S.op(DVE, lambda e: e.tensor_copy(out=TB.rearrange("p (a b) -> p a b", a=4), in_=tab[:, blk, :].unsqueeze(1).broadcast_to([128, 4, 16])),
                 reads=tabs, writes=[bTB])
            pj = next_pj()
            S.op(PE, lambda e: e.transpose(out=PJX[pj][0:64, 0:128], in_=TB, identity=ident_f[:, :]), reads=[bTB, buf("ident_f")], writes=[b_PJX[pj]])
            copy_op(evac_engine(), dstT[0:64, m * 128:(m + 1) * 128], PJX[pj][0:64, 0:128], reads=[b_PJX[pj]], writes=[b_CSq])

    xs5 = xs + [view(ARENA, 37760, [128, 1024], F32), view(ARENA, 43904, [128, 1024], F32)]
    b_xs5 = b_xs + [buf("xs3"), buf("xs4")]
    NXS = 5

    def st_S1(blk):
        xi = blk % NXS
        S.dma(SP, xs5[xi], xl_b[blk, :, :], writes=[b_xs5[xi]])
        S.op(ACT, lambda e: e.activation(out=sqj, in_=xs5[xi], func=AF.Square, accum_out=stat[:, blk * 8:blk * 8 + 1]),
             reads=[b_xs5[xi]], writes=[buf("sqj"), buf("stat%d" % blk)])
        S.op(POOL, lambda e: e.tensor_scalar(out=stat[:, blk * 8 + 1:blk * 8 + 2], in0=stat[:, blk * 8:blk * 8 + 1], scalar1=1.0 / D, scalar2=RMS_EPS,
                                              op0=ALU.mult, op1=ALU.add), reads=[buf("stat%d" % blk)], writes=[buf("stat%d" % blk)])
        S.op(POOL, lambda e: e.tensor_tensor(out=stat[:, blk * 8 + 2:blk * 8 + 3], in0=stat[:, blk * 8 + 1:blk * 8 + 2], in1=mhalf[:, 0:1], op=ALU.pow),
             reads=[buf("stat%d" % blk), buf("mhalf")], writes=[buf("stat%d" % blk)])

    def st_S2(blk):
        xi = blk % NXS
        ni = blk % 2
        g = blk % 2
        S.op(DVE, lambda e: e.scalar_tensor_tensor(out=xn[ni], in0=xs5[xi], scalar=stat[:, blk * 8 + 2:blk * 8 + 3], in1=gpre_b,
                                                   op0=ALU.mult, op1=ALU.mult),
             reads=[b_xs5[xi], buf("stat%d" % blk)] + sc, writes=[b_xn[ni]])
        pjv = ST[g][:, 0, :].bitcast(BF16).rearrange("p (c t) -> p c t", c=8)
        for c in range(8):
            S.op(PE, lambda e: e.transpose(out=pjv[:, c, :], in_=xn[ni][:, c * 128:(c + 1) * 128], identity=ident_bf[:, :]),
                 reads=[b_xn[ni], buf("ident_bf")], writes=[b_ST[g]], partial=(c > 0))
        copy_op(evac_engine(), uT[:, :, blk * 128:(blk + 1) * 128], pjv, reads=[b_ST[g]], writes=[b_uT[blk]], partial=False)

    def st_A(blk):
        pj = blk % 2
        L = PJX[pj]
        for c in range(8):
            S.op(PE, lambda e: e.matmul(L[:, 0:432], lhsT=uT[:, c, blk * 128:(blk + 1) * 128], rhs=WL[:, c, :], start=(c == 0), stop=(c == 7)),
                 reads=[b_uT[blk], buf("WL")], writes=[b_PJX[pj]], partial=(c > 0))
        st = buf("lstat%d" % blk)
        s0 = blk * 8 + 3
        S.op(ACT, lambda e: e.activation(out=sqj[:, 0:256], in_=L[:, 0:256], func=AF.Square, scale=1.0 / 16.0, accum_out=stat[:, s0:s0 + 1]),
             reads=[b_PJX[pj]], writes=[buf("sqj"), st])
        S.op(ACT, lambda e: e.activation(out=sqj[:, 256:384], in_=L[:, 256:384], func=AF.Square, scale=float(1.0 / np.sqrt(128.0)), accum_out=stat[:, s0 + 1:s0 + 2]),
             reads=[b_PJX[pj]], writes=[buf("sqj"), st], partial=True)
        S.op(POOL, lambda e: e.tensor_scalar(out=stat[:, s0:s0 + 2], in0=stat[:, s0:s0 + 2], scalar1=RMS_EPS, scalar2=None, op0=ALU.add),
             reads=[st], writes=[st])
        S.op(POOL, lambda e: e.tensor_tensor(out=stat[:, s0 + 2:s0 + 4], in0=stat[:, s0:s0 + 2], in1=mhalf[:, 0:2], op=ALU.pow),
             reads=[st, buf("mhalf")], writes=[st])

    def st_B(blk):
        pj = blk % 2
        li = blk % 2
        L = PJX[pj]
        st = buf("lstat%d" % blk)
        s0 = blk * 8 + 3
        lt = LT[li]
        S.op(DVE, lambda e: e.scalar_tensor_tensor(out=lt[:, 0:256], in0=L[:, 0:256], scalar=stat[:, s0 + 2:s0 + 3], in1=gq_b[:, :], op0=ALU.mult, op1=ALU.mult),
             reads=[b_PJX[pj], st] + sc, writes=[b_LT[li]])
        S.op(DVE, lambda e: e.scalar_tensor_tensor(out=lt[:, 256:384], in0=L[:, 256:384], scalar=stat[:, s0 + 3:s0 + 4], in1=gkv_b[:, :], op0=ALU.mult, op1=ALU.mult),
             reads=[b_PJX[pj], st] + sc, writes=[b_LT[li]], partial=True)
        cs = CS_tm[:, blk, :]
        sn = SN_tm[:, blk, :]
        bth = buf("TMPn")
        X2 = L[:, 384:416].rearrange("p (a b) -> p a b", a=2)
        T1 = TMP[:, 0:32].rearrange("p (a b) -> p a b", a=2)
        T2 = TMP[:, 32:64].rearrange("p (a b) -> p a b", a=2)
        S.op(DVE, lambda e: e.tensor_tensor(out=T1, in0=X2, in1=cs.unsqueeze(1).broadcast_to([128, 2, 16]), op=ALU.mult), reads=[b_PJX[pj]] + tabs, writes=[bth])
        S.op(DVE, lambda e: e.tensor_tensor(out=T2, in0=X2, in1=sn.unsqueeze(1).broadcast_to([128, 2, 16]), op=ALU.mult), reads=[b_PJX[pj]] + tabs, writes=[bth], partial=True)
        S.op(DVE, lambda e: e.tensor_tensor(out=lt[:, 384:400], in0=TMP[:, 0:16], in1=TMP[:, 48:64], op=ALU.subtract), reads=[bth], writes=[b_LT[li]], partial=True)
        S.op(DVE, lambda e: e.tensor_tensor(out=lt[:, 400:416], in0=TMP[:, 32:48], in1=TMP[:, 16:32], op=ALU.add), reads=[bth], writes=[b_LT[li]], partial=True)
        S.op(DVE, lambda e: e.tensor_tensor(out=FLh[:, :, blk], in0=L[:, 416:432], in1=fb_b[:, :], op=ALU.add), reads=[b_PJX[pj]] + sc, writes=[b_FLh], partial=True)
        tp = OP[pj][:, :].bitcast(BF16)
        for j in range(3):
            S.op(PE, lambda e: e.transpose(out=tp[:, j * 128:(j + 1) * 128], in_=lt[:, j * 128:(j + 1) * 128], identity=ident_bf[:, :]),
                 reads=[b_LT[li], buf("ident_bf")], writes=[b_OP[pj]], partial=(j > 0))
        S.op(PE, lambda e: e.transpose(out=tp[:, 384:512], in_=lt[:, 288:416], identity=ident_bf[:, :]),
             reads=[b_LT[li], buf("ident_bf")], writes=[b_OP[pj]], partial=True)

    def st_C(blk):
        pj = blk % 2
        tp = OP[pj][:, :].bitcast(BF16)
        ev = evac_engine()
        if blk >= 2 and blk % 2 == 0:
            m = (blk - 2) // 2
            copy_op(ev, cqnT[:, :, m * 128:(m + 1) * 128], tp[:, 0:256].rearrange("p (c t) -> p c t", c=2), reads=[b_OP[pj]], writes=[b_cqnT])
        copy_op(ev, ckvnT[:, blk * 128:(blk + 1) * 128], tp[:, 256:384], reads=[b_OP[pj]], writes=[b_ckvnT])
        copy_op(ev, KA[64:96, blk * 128:(blk + 1) * 128], tp[96:128, 384:512], reads=[b_OP[pj]], writes=[b_KA[kcls(blk)]])
        copy_op(ev, KB[64:96, blk * 128:(blk + 1) * 128], tp[96:128, 384:512], reads=[b_OP[pj]], writes=[b_KB[kcls(blk)]])

    for i in range(NB + 4):
        if i < NB:
            st_S1(i)
        if 0 <= i - 1 < NB:
            st_S2(i - 1)
        if 0 <= i - 2 < NB:
            st_A(i - 2)
        if 0 <= i - 3 < NB:
            st_B(i - 3)
        if 0 <= i - 4 < NB:
            st_C(i - 4)

    if debug:
        S.dma(SP, dbg["uT"], uT[:, :, :].rearrange("p c t -> p (c t)"), reads=b_uT)
        S.dma(SP, dbg["ckvnT"], ckvnT, reads=[b_ckvnT])
        S.dma(SP, dbg["cqnT"], cqnT.rearrange("p c t -> p (c t)"), reads=[b_cqnT])
    if stop <= 2:
        return finalize()

    barrier()

    own_tok = lambda M: slice(M * 512, (M + 1) * 512)
    st_i = [0]
    pt_i = [0]
    op_i = [0]
    pj2_i = [0]
    TH2 = sb("TH2", [128, 512], F32)
    R16 = sb("R16", [128, 16], F32)
    b_TH2 = buf("TH2")

    def bank_for(instream):
        if instream:
            i = pj2_i[0]
            pj2_i[0] = (i + 1) % 2
            return i
        return next_pj()

    def ev_for(instream):
        return DVE if instream else evac_engine()

    def slot_blocks(M):
        blks = [(0, 512, 0, False)] + [(b, 512, 0, False) for b in range(1, 8 * M + 1)]
        for mm in range(4):
            n = 512 - 128 * mm
            blks.append((1 + 2 * (4 * M + mm), n, 128 * mm, False))
            blks.append((2 + 2 * (4 * M + mm), n, 128 * mm, True))
        groups = []
        i = 0
        while i < len(blks):
            if i + 1 < len(blks) and blks[i + 1][1] == blks[i][1]:
                groups.append([blks[i], blks[i + 1]])
                i += 2
            else:
                groups.append([blks[i]])
                i += 1
        return groups

    def attention_multi(order, scale, pair, win=None):
        win = win or {}
        stream = []
        seg_groups = []
        for si_, (hd, M) in enumerate(order):
            groups = slot_blocks(M)
            seg_groups.append(len(groups))
            for gi, g in enumerate(groups):
                stream.append((si_, hd, M, g, gi, len(groups)))
        wstate = {}
        for s0, items in win.items():
            tot = seg_groups[s0] + (seg_groups[s0 + 1] if s0 + 1 < len(seg_groups) else 0)
            wstate[s0] = [list(items), 0, tot, 0]
        pend = None
        oslot = {}
        nq = []

        def drain(n=None, upto=None):
            k = 0
            while nq and (n is None or k < n) and (upto is None or nq[0][0] <= upto):
                nq.pop(0)[1]()
                k += 1

        for idx in range(len(stream) + 1):
            if idx < len(stream):
                sg, hd, M, g, gi, ng_ = stream[idx]
                first = gi == 0
                last = gi == ng_ - 1
                if first:
                    oslot[sg] = op_i[0] % 2
                    op_i[0] += 1
                si = st_i[0] % 2
                st_i[0] += 1
                n = g[0][1]
                qoff = g[0][2]
                KT, QT, KR = hd["KT"], hd["QT"], hd["KR"]
                for j, (blk, n_, qoff_, diag) in enumerate(g):
                    S.op(PE, lambda e: e.matmul(ST[si][:, j, 0:n], lhsT=KT[0:KR, blk * 128:(blk + 1) * 128], rhs=QT[0:KR, M * 512 + qoff:(M + 1) * 512],
                                                start=True, stop=(not diag)),
                         reads=[hd["b_K"][kcls(blk)], hd["b_Q"][M]], writes=[b_ST[si]], partial=(j > 0))
                    if diag:
                        S.op(PE, lambda e: e.matmul(ST[si][:, j, 0:128], lhsT=ident_bf[:, :], rhs=maskneg[:, :], start=False, stop=True),
                             reads=[buf("ident_bf"), buf("maskneg")], writes=[b_ST[si]], partial=True)
                pi = pt_i[0] % 3
                pt_i[0] += 1
                ng = len(g)
                S.op(ACT, lambda e: e.activation(out=PT[pi][:, 0:ng, 0:n], in_=ST[si][:, 0:ng, 0:n], func=AF.Exp, scale=float(scale)),
                     reads=[b_ST[si]], writes=[b_PT[pi]])
                cur = (sg, hd, M, g, first, last, pi)
                s0 = sg - (sg % 2)
                if s0 in wstate:
                    w = wstate[s0]
                    rem = len(w[0]) - w[1]
                    left = w[2] - w[3]
                    if rem > 0:
                        k = -(-rem // max(left, 1))
                        for it in w[0][w[1]:w[1] + k]:
                            it()
                        w[1] += k
                    w[3] += 1
                drain(n=(2 if len(nq) > 6 else 1))
            else:
                cur = None
            if pend is not None:
                sg, hd, M, g, first, last, pi = pend
                o = oslot[sg]
                if first:
                    drain(upto=sg - 2)
                n = g[0][1]
                qoff = g[0][2]
                vcol0 = hd["vcol0"]
                for j, (blk, n_, qoff_, diag) in enumerate(g):
                    S.op(PE, lambda e: e.matmul(OP[o][:, qoff:512], lhsT=VP[:, blk, vcol0:vcol0 + 128], rhs=PT[pi][:, j, 0:n],
                                                start=(first and j == 0), stop=(last and j == len(g) - 1)),
                         reads=[b_PT[pi], b_VP[kcls(blk)], b_VPones], writes=[b_OP[o]], partial=not (first and j == 0))
                if last:
                    if not hd["is_B"]:
                        orow, drow = slice(0, 64), slice(64, 128)
                    else:
                        orow, drow = slice(64, 128), slice(0, 64)
                    OG = hd["OG"]
                    bRT, bTM, bR16 = buf("RT"), buf("TMPn"), buf("R16")
                    bOGx = hd["b_OG"]

                    def mk(o=o, orow=orow, drow=drow, OG=OG, M=M, bOGx=bOGx):
                        return [
                            lambda: S.op(DVE, lambda e: e.transpose(out=RT[drow, :], in_=OP[o][drow, :]), reads=[b_OP[o]], writes=[bRT]),
                            lambda: S.op(DVE, lambda e: e.reciprocal(out=R16[drow, :], in_=RT[drow, :].rearrange("p (b c) -> p b c", c=32)[:, :, 0]),
                                         reads=[bRT], writes=[bR16]),
                            lambda: S.op(DVE, lambda e: e.tensor_copy(out=TMP[drow, :].rearrange("p (b c) -> p b c", c=32),
                                                                      in_=R16[drow, :].unsqueeze(2).broadcast_to([64, 16, 32])), reads=[bR16], writes=[bTM]),
                            lambda: S.op(DVE, lambda e: e.transpose(out=RT[drow, :], in_=TMP[drow, :]), reads=[bTM], writes=[bRT]),
                            lambda: S.op(DVE, lambda e: e.tensor_tensor(out=TMP[orow, :], in0=OP[o][orow, :], in1=RT[drow, :], op=ALU.mult),
                                         reads=[b_OP[o], bRT], writes=[bTM]),
                            lambda: S.op(DVE, lambda e: e.tensor_tensor(out=OG[orow, pair, M * 512:(M + 1) * 512], in0=TMP[orow, :],
                                                                        in1=SZ[orow, M * 512:(M + 1) * 512], op=ALU.mult),
                                         reads=[bTM, b_SZ[M]], writes=[bOGx], partial=True),
                        ]
                    for fn in mk():
                        nq.append((sg, fn))
            pend = cur
        drain()

    def urhs(c, M):
        return uT[:, c, 128:T].rearrange("p (m two t) -> p m two t", two=2, t=128)[:, 4 * M:4 * M + 4, 1, :]

    def z_item(M, wz, b_wz, instream):
        def f():
            pj = bank_for(instream)
            for c in range(8):
                S.op(PE, lambda e: e.matmul(PJX[pj][:, :], lhsT=wz[:, c, :], rhs=urhs(c, M), start=(c == 0), stop=(c == 7)),
                     reads=b_uT + [b_wz], writes=[b_PJX[pj]], partial=(c > 0))
            S.op(ACT, lambda e: e.activation(out=TH2[:, :], in_=PJX[pj][:, :], func=AF.Tanh, scale=0.5), reads=[b_PJX[pj]], writes=[b_TH2])
            S.op(DVE, lambda e: e.scalar_tensor_tensor(out=SZ[:, M * 512:(M + 1) * 512], in0=TH2[:, :], scalar=1.0, in1=PJX[pj][:, :], op0=ALU.add, op1=ALU.mult),
                 reads=[b_TH2, b_PJX[pj]], writes=[b_SZ[M]])
        return f

    def load_w_in_cols(dst, b_dst, col0, ncols=128):
        S.dma(POOL, dst[:, :, 0:ncols], w_in_c[:, :, col0:col0 + ncols], writes=[b_dst])

    ntt = [(i * 512, 512) for i in range(8)] + [(4096, 128)]
    KT_OF = {0: [0, 1, 2], 1: [3, 4], 2: [5, 6], 3: [7, 8]}
    VG_OF = {0: [0, 4, 8], 1: [12, 16], 2: [20, 24], 3: [28, 32]}

    def run_pair(items_of, hA_, hB_, scale, pair, all_upfront=False):
        for it in items_of(0, False):
            it()
        if all_upfront:
            for c in range(1, 5):
                for it in items_of(c, False):
                    it()
            win = {}
        else:
            win = {0: items_of(1, True), 2: items_of(2, True), 4: items_of(3, True), 6: items_of(4, True)}
        order = [(hA_, 0), (hB_, 0), (hA_, 1), (hB_, 1), (hA_, 2), (hB_, 2), (hA_, 3), (hB_, 3)]
        attention_multi(order, scale, pair, win)

    w_ukv_h = w_ukv.rearrange("k (h c) -> k h c", h=16)
    w_uq_c = w_uq.rearrange("(c p) (h d) -> p c h d", p=128, h=16)
    WKVn = WMS[:, 0:128]
    WKVv = WMS[:, 128:256]
    WQn = WMS[:, 256:512].rearrange("p (c n) -> p c n", c=2)
    WQp = WMS[:, 512:640].rearrange("p (c n) -> p c n", c=2)
    WQr = WMS[:, 640:768].rearrange("p (c n) -> p c n", c=2)
    b_WMS = buf("WMS")
    b_OGA = buf("OGA")
    b_OGB = buf("OGB")
    b_neg = buf("WQrneg")

    def mla_items(c, instream):
        its = []
        if c > 3:
            return [z_item(3, WBIG[0], b_W[0], instream)]

        def kt(ti):
            def f():
                t0, tn = ntt[ti]
                pj = bank_for(instream)
                S.op(PE, lambda e: e.matmul(PJX[pj][:, 0:tn], lhsT=WKVn, rhs=ckvnT[:, t0:t0 + tn], start=True, stop=True),
                     reads=[b_WMS, b_ckvnT], writes=[b_PJX[pj]])
                ev = ev_for(instream)
                copy_op(ev, KA[0:64, t0:t0 + tn], PJX[pj][0:64, 0:tn], reads=[b_PJX[pj]], writes=[b_KA[c]])
                copy_op(ev, KB[0:64, t0:t0 + tn], PJX[pj][64:128, 0:tn], reads=[b_PJX[pj]], writes=[b_KB[c]])
            return f

        def vg(g0):
            def f():
                nb_ = min(4, NB - g0)
                pj = bank_for(instream)
                pv = PJX[pj].rearrange("p (b c) -> p b c", b=4)
                for j in range(nb_):
                    blk = g0 + j
                    S.op(PE, lambda e: e.matmul(pv[:, j, :], lhsT=ckvnT[:, blk * 128:(blk + 1) * 128], rhs=WKVv, start=True, stop=True),
                         reads=[b_WMS, b_ckvnT], writes=[b_PJX[pj]], partial=(j > 0))
                ev = ev_for(instream)
                copy_op(ev, VP[:, g0:g0 + nb_, 0:64], pv[:, 0:nb_, 0:64], reads=[b_PJX[pj]], writes=[b_VP[c]])
                copy_op(ev, VP[:, g0:g0 + nb_, 128:192], pv[:, 0:nb_, 64:128], reads=[b_PJX[pj]], writes=[b_VP[c]])
            return f

        def qq(M):
            def f():
                tok = own_tok(M)
                pj = bank_for(instream)
                for cc in range(2):
                    S.op(PE, lambda e: e.matmul(PJX[pj][:, :], lhsT=WQn[:, cc, :], rhs=cqnT[:, cc, tok], start=(cc == 0), stop=(cc == 1)),
                         reads=[b_WMS, b_cqnT], writes=[b_PJX[pj]], partial=(cc > 0))
                ev = ev_for(instream)
                copy_op(ev, QA[0:64, tok], PJX[pj][0:64, :], reads=[b_PJX[pj]], writes=[b_QA[M]], partial=False)
                copy_op(ev, QB[0:64, tok], PJX[pj][64:128, :], reads=[b_PJX[pj]], writes=[b_QB[M]], partial=False)
                pjp = bank_for(instream)
                for cc in range(2):
                    S.op(PE, lambda e: e.matmul(PJX[pjp][0:64, :], lhsT=WQp[:, cc, :], rhs=cqnT[:, cc, tok], start=(cc == 0), stop=(cc == 1)),
                         reads=[b_WMS, b_cqnT], writes=[b_PJX[pjp]], partial=(cc > 0))
                S.op(DVE, lambda e: e.tensor_tensor(out=TH2[0:64, :], in0=PJX[pjp][0:64, :], in1=CSq[0:64, tok], op=ALU.mult),
                     reads=[b_PJX[pjp], b_CSq], writes=[b_TH2])
                pjr = bank_for(instream)
                for cc in range(2):
                    S.op(PE, lambda e: e.matmul(PJX[pjr][0:64, :], lhsT=WQr[:, cc, :], rhs=cqnT[:, cc, tok], start=(cc == 0), stop=(cc == 1)),
                         reads=[b_WMS, b_neg, b_cqnT], writes=[b_PJX[pjr]], partial=(cc > 0))
                S.op(DVE, lambda e: e.tensor_tensor(out=PJX[pjr][0:64, :], in0=PJX[pjr][0:64, :], in1=SNq[0:64, tok], op=ALU.mult),
                     reads=[b_PJX[pjr], b_CSq], writes=[b_PJX[pjr]])
                S.op(DVE, lambda e: e.tensor_tensor(out=QA[64:96, tok], in0=PJX[pjr][0:32, :], in1=TH2[0:32, :], op=ALU.add),
                     reads=[b_PJX[pjr], b_TH2], writes=[b_QA[M]], partial=True)
                S.op(DVE, lambda e: e.tensor_tensor(out=QB[64:96, tok], in0=PJX[pjr][32:64, :], in1=TH2[32:64, :], op=ALU.add),
                     reads=[b_PJX[pjr], b_TH2], writes=[b_QB[M]], partial=True)
            return f

        for ti in KT_OF[c]:
            its.append(kt(ti))
        for g0 in VG_OF[c]:
            its.append(vg(g0))
        its.append(qq(c))
        if c >= 1:
            its.insert(0, z_item(c - 1, WBIG[0], b_W[0], instream))
        return its

    for pair in range(8):
        hA, hB = 2 * pair, 2 * pair + 1
        bt = {}
        for hi, h in enumerate((hA, hB)):
            S.dma(POOL, WKVn[:, hi * 64:(hi + 1) * 64], w_ukv_h[:, h, 0:64], writes=[b_WMS], partial=(hi > 0), batch=bt)
            S.dma(POOL, WKVv[:, hi * 64:(hi + 1) * 64], w_ukv_h[:, h, 64:128], writes=[b_WMS], partial=True, batch=bt)
            S.dma(POOL, WQn[:, :, hi * 64:(hi + 1) * 64], w_uq_c[:, :, h, 0:64], writes=[b_WMS], partial=True, batch=bt)
            S.dma(POOL, WQp[:, :, hi * 32:(hi + 1) * 32], w_uq_c[:, :, h, 64:96], writes=[b_WMS], partial=True, batch=bt)
            S.dma(POOL, WQr[:, :, hi * 32:hi * 32 + 16], w_uq_c[:, :, h, 80:96], writes=[b_WMS], partial=True, batch=bt)
            S.dma(POOL, WQr[:, :, hi * 32 + 16:hi * 32 + 32], w_uq_c[:, :, h, 64:80], writes=[b_WMS], partial=True, batch=bt)
        for hi in range(2):
            S.op(POOL, lambda e: e.tensor_scalar(out=WQr[:, :, hi * 32:hi * 32 + 16], in0=WQr[:, :, hi * 32:hi * 32 + 16], scalar1=-1.0, scalar2=None, op0=ALU.mult),
                 reads=[b_WMS], writes=[b_neg], partial=(hi > 0))
        load_w_in_cols(WBIG[0], b_W[0], O_ZM + pair * 128)
        hdA = dict(KT=KA, QT=QA, b_K=b_KA, b_Q=b_QA, KR=96, vcol0=0, is_B=False, OG=OGA, b_OG=b_OGA)
        hdB = dict(KT=KB, QT=QB, b_K=b_KB, b_Q=b_QB, KR=96, vcol0=64, is_B=True, OG=OGA, b_OG=b_OGA)
        if debug and pair == 0:
            for c in range(0, 5):
                for it in mla_items(c, False):
                    it()
            S.dma(SP, dbg["KA"][0:96, :], KA[0:96, :], reads=b_KA)
            S.dma(SP, dbg["QA"][0:96, :], QA[0:96, :], reads=b_QA)
            S.dma(SP, dbg["VP"], VP[:, :, :].rearrange("p b c -> p (b c)"), reads=b_VP + [b_VPones])
            if stop <= 4:
                attention_multi([(hdA, 0), (hdA, 1), (hdA, 2), (hdA, 3)], MLA_SCALE, pair)
                S.dma(SP, dbg["OGA"][0:64, 0:NOWN], OGA[0:64, 0, :], reads=[b_OGA])
                return finalize()
            order = [(hdA, 0), (hdB, 0), (hdA, 1), (hdB, 1), (hdA, 2), (hdB, 2), (hdA, 3), (hdB, 3)]
            attention_multi(order, MLA_SCALE, pair)
        else:
            run_pair(mla_items, hdA, hdB, MLA_SCALE, pair)

    if debug:
        S.dma(SP, dbg["OGA"], OGA[:, :, :].rearrange("p c t -> p (c t)"), reads=[b_OGA])

    if stop <= 5:
        return finalize()
    barrier()

    b_cum = buf("cumwork")
    S.op(ACT, lambda e: e.activation(out=LFv, in_=FLh[:, :, :], func=AF.Exp, scale=-1.0), reads=[b_FLh], writes=[b_cum])
    S.op(ACT, lambda e: e.activation(out=LFv, in_=LFv, func=AF.Ln, bias=1.0), reads=[b_cum], writes=[b_cum])
    S.op(DVE, lambda e: e.scalar_tensor_tensor(out=LFv, in0=LFv, scalar=-1.0, in1=valid_t[:, :].unsqueeze(1).broadcast_to([128, 16, NB]), op0=ALU.mult, op1=ALU.mult),
         reads=[b_cum] + sc, writes=[b_cum])
    S.op(POOL, lambda e: e.memset(SEG, 1.0), writes=[buf("SEG")])
    S.op(POOL, lambda e: e.memset(SEG[:, :, 0:1], 0.0), reads=[buf("SEG")], writes=[buf("SEG")])
    S.op(DVE, lambda e: e.tensor_tensor_scan(out=INC.rearrange("p a b -> p (a b)"), data0=SEG.rearrange("p a b -> p (a b)"),
                                              data1=LFv.rearrange("p a b -> p (a b)"), initial=0.0, op0=ALU.mult, op1=ALU.add),
         reads=[b_cum, buf("SEG")], writes=[buf("INC")])
    S.op(DVE, lambda e: e.tensor_tensor(out=INC, in0=INC, in1=LFv, op=ALU.subtract), reads=[buf("INC"), b_cum], writes=[buf("INC")])
    LF2 = LFv.rearrange("p a b -> p (a b)")
    EX2 = INC.rearrange("p a b -> p (a b)")
    CU2 = CUM.rearrange("p a b -> p (a b)")
    for (c0, cn, pj) in ((0, 495, 0), (495, 33, 1)):
        S.op(PE, lambda e: e.matmul(PJX[pj][:, 0:cn], lhsT=tri_f[:, :], rhs=LF2[:, c0:c0 + cn], start=True, stop=False),
             reads=[b_cum, buf("tri_f")], writes=[b_PJX[pj]])
        S.op(PE, lambda e: e.matmul(PJX[pj][:, 0:cn], lhsT=ones_f[:, :], rhs=EX2[:, c0:c0 + cn], start=False, stop=True),
             reads=[buf("INC"), buf("ones_f")], writes=[b_PJX[pj]], partial=True)
        S.op(DVE, lambda e: e.tensor_copy(out=CU2[:, c0:c0 + cn], in_=PJX[pj][:, 0:cn]), reads=[b_PJX[pj]], writes=[buf("CUM")], partial=True)
    if debug:
        S.dma(SP, dbg["cum"], CU2, reads=[buf("CUM")])
    bk = buf("TKsrc")
    S.op(DVE, lambda e: e.tensor_scalar(out=C8, in0=CUM, scalar1=-8.0, scalar2=None, op0=ALU.mult), reads=[buf("CUM")], writes=[buf("C8")])
    S.op(DVE, lambda e: e.tensor_copy(out=HI, in_=C8), reads=[buf("C8")], writes=[buf("HI")])
    S.op(DVE, lambda e: e.tensor_tensor(out=R1, in0=C8, in1=HI, op=ALU.subtract), reads=[buf("C8"), buf("HI")], writes=[buf("R1")])
    S.op(DVE, lambda e: e.tensor_copy(out=MID, in_=R1), reads=[buf("R1")], writes=[buf("MID")])
    S.op(DVE, lambda e: e.tensor_tensor(out=R1, in0=R1, in1=MID, op=ALU.subtract), reads=[buf("R1"), buf("MID")], writes=[buf("R1")])
    TKs4 = TKsrc.rearrange("p b (h f) -> p b h f", f=4)
    TQs4 = TQsrc.rearrange("p m (h f) -> p m h f", f=4)
    S.op(POOL, lambda e: e.memset(TKsrc, 1.0), writes=[bk])
    S.op(POOL, lambda e: e.memset(TQsrc, 1.0), writes=[buf("TQsrc")])
    hb = lambda a: a.rearrange("p h b -> p b h")
    S.op(DVE, lambda e: e.tensor_copy(out=TKs4[:, :, :, 1], in_=hb(HI)), reads=[buf("HI"), bk], writes=[bk])
    S.op(DVE, lambda e: e.tensor_copy(out=TKs4[:, :, :, 2], in_=hb(MID)), reads=[buf("MID")], writes=[bk], partial=True)
    S.op(DVE, lambda e: e.tensor_copy(out=TKs4[:, :, :, 3], in_=hb(R1)), reads=[buf("R1")], writes=[bk], partial=True)
    cum_own = CUM[:, :, 1:NB].rearrange("p h (m two) -> p m h two", two=2)[:, :, :, 1]
    S.op(DVE, lambda e: e.tensor_scalar(out=TQs4[:, :, :, 0], in0=cum_own, scalar1=8.0, scalar2=None, op0=ALU.mult),
         reads=[buf("CUM"), buf("TQsrc")], writes=[buf("TQsrc")])
    b_TK = buf("TK")
    for g0 in range(0, NB, 4):
        nb_ = min(4, NB - g0)
        pj = next_pj()
        tp = PJX[pj][:, :].bitcast(BF16)
        for j in range(nb_):
            S.op(PE, lambda e: e.transpose(out=tp[0:64, j * 128:(j + 1) * 128], in_=TKsrc[:, g0 + j, :], identity=ident_bf[:, :]),
                 reads=[bk, buf("ident_bf")], writes=[b_PJX[pj]], partial=(j > 0))
        copy_op(evac_engine(), TK[0:64, g0 * 128:(g0 + nb_) * 128], tp[0:64, 0:nb_ * 128], reads=[b_PJX[pj]], writes=[b_TK])
    for g0 in range(0, 16, 4):
        pj = next_pj()
        tp = PJX[pj][:, :].bitcast(BF16)
        for j in range(4):
            S.op(PE, lambda e: e.transpose(out=tp[0:64, j * 128:(j + 1) * 128], in_=TQsrc[:, g0 + j, :], identity=ident_bf[:, :]),
                 reads=[buf("TQsrc"), buf("ident_bf")], writes=[b_PJX[pj]], partial=(j > 0))
        copy_op(evac_engine(), TK[64:128, g0 * 128:(g0 + 4) * 128], tp[0:64, 0:512], reads=[b_PJX[pj]], writes=[b_TK])
    if debug:
        S.dma(SP, dbg["TK"], TK, reads=[b_TK])

    if stop <= 6:
        return finalize()
    barrier()

    def fox_items(c, instream):
        its = []
        if c > 3:
            return [z_item(3, WBIG[0], b_W[0], instream)]

        def kt(ti):
            def f():
                t0, tn = ntt[ti]
                pj = bank_for(instream)
                for cc in range(8):
                    S.op(PE, lambda e: e.matmul(PJX[pj][:, 0:tn], lhsT=WBIG[1][:, cc, :], rhs=uT[:, cc, t0:t0 + tn], start=(cc == 0), stop=(cc == 7)),
                         reads=b_uT + [b_W[1]], writes=[b_PJX[pj]], partial=(cc > 0))
                ev = ev_for(instream)
                copy_op(ev, KA[0:64, t0:t0 + tn], PJX[pj][0:64, 0:tn], reads=[b_PJX[pj]], writes=[b_KA[c]])
                copy_op(ev, KB[0:64, t0:t0 + tn], PJX[pj][64:128, 0:tn], reads=[b_PJX[pj]], writes=[b_KB[c]])
            return f

        def vg(g0):
            def f():
                nb_ = min(4, NB - g0)
                pj = bank_for(instream)
                pv = PJX[pj].rearrange("p (b c) -> p b c", b=4)
                for j in range(nb_):
                    blk = g0 + j
                    for cc in range(8):
                        S.op(PE, lambda e: e.matmul(pv[:, j, :], lhsT=uT[:, cc, blk * 128:(blk + 1) * 128], rhs=WBIG[2][:, cc, :], start=(cc == 0), stop=(cc == 7)),
                             reads=[b_uT[blk], b_W[2]], writes=[b_PJX[pj]], partial=(j > 0 or cc > 0))
                ev = ev_for(instream)
                copy_op(ev, VP[:, g0:g0 + nb_, 0:64], pv[:, 0:nb_, 0:64], reads=[b_PJX[pj]], writes=[b_VP[c]])
                copy_op(ev, VP[:, g0:g0 + nb_, 128:192], pv[:, 0:nb_, 64:128], reads=[b_PJX[pj]], writes=[b_VP[c]])
            return f

        def qq(M):
            def f():
                tok = own_tok(M)
                pj = bank_for(instream)
                for cc in range(8):
                    S.op(PE, lambda e: e.matmul(PJX[pj][:, :], lhsT=WBIG[3][:, cc, :], rhs=urhs(cc, M), start=(cc == 0), stop=(cc == 7)),
                         reads=b_uT + [b_W[3]], writes=[b_PJX[pj]], partial=(cc > 0))
                ev = ev_for(instream)
                copy_op(ev, QA[0:64, tok], PJX[pj][0:64, :], reads=[b_PJX[pj]], writes=[b_QA[M]])
                copy_op(ev, QB[0:64, tok], PJX[pj][64:128, :], reads=[b_PJX[pj]], writes=[b_QB[M]])
            return f

        for ti in KT_OF[c]:
            its.append(kt(ti))
        for g0 in VG_OF[c]:
            its.append(vg(g0))
        its.append(qq(c))
        if c >= 1:
            its.insert(0, z_item(c - 1, WBIG[0], b_W[0], instream))
        return its

    for pair in range(8):
        hA, hB = 2 * pair, 2 * pair + 1
        load_w_in_cols(WBIG[1], b_W[1], O_FK + pair * 128)
        load_w_in_cols(WBIG[2], b_W[2], O_FV + pair * 128)
        load_w_in_cols(WBIG[3], b_W[3], O_FQ + pair * 128)
        load_w_in_cols(WBIG[0], b_W[0], O_ZF + pair * 128)
        S.dma(SP, KA[64:68, :], TK[4 * hA:4 * hA + 4, :], reads=[b_TK], writes=b_KA, partial=True)
        S.dma(SP, KB[64:68, :], TK[4 * hB:4 * hB + 4, :], reads=[b_TK], writes=b_KB, partial=True)
        S.dma(SP, QA[64:68, :], TK[64 + 4 * hA:64 + 4 * hA + 4, 0:NOWN], reads=[b_TK], writes=b_QA, partial=True)
        S.dma(SP, QB[64:68, :], TK[64 + 4 * hB:64 + 4 * hB + 4, 0:NOWN], reads=[b_TK], writes=b_QB, partial=True)

        hdA = dict(KT=KA, QT=QA, b_K=b_KA, b_Q=b_QA, KR=68, vcol0=0, is_B=False, OG=OGB, b_OG=b_OGB)
        hdB = dict(KT=KB, QT=QB, b_K=b_KB, b_Q=b_QB, KR=68, vcol0=64, is_B=True, OG=OGB, b_OG=b_OGB)
        run_pair(fox_items, hdA, hdB, FOX_SCALE, pair)

    if debug:
        S.dma(SP, dbg["OGB"], OGB[:, :, :].rearrange("p c t -> p (c t)"), reads=[b_OGB])

    if stop <= 7:
        return finalize()
    barrier()
    w_bra_c = w_bra.rearrange("(c p) n -> p c n", p=128)
    w_brf_c = w_brf.rearrange("(c p) n -> p c n", p=128)
    w_out_c = w_out.rearrange("(c p) n -> p c n", p=128)
    b_CH = [buf("CH%d" % i) for i in range(8)]
    b_MT = buf("MT")
    b_FT = [buf("FT%d" % i) for i in range(4)]
    b_WO = buf("WO")
    b_XR = [buf("XR0"), buf("XR1"), buf("XR2")]
    b_RES = [buf("RES0"), buf("RES1")]
    WO = view(OGAf, 0, [128, 8, D], BF16)
    gpost_b = view(OGAf, 16384, [128, D], F32)
    XR = [view(OGAf, 20480 + i * 4096, [128, D], F32) for i in range(3)]
    RES = [view(ARENA, 32768 + i * 4096, [128, D], F32) for i in range(2)]
    sqj2 = view(ARENA, 40960, [128, D], BF16)
    slots = [(ST[0][:, 0, :], ST[0][:, 1, :], [b_ST[0]]), (ST[1][:, 0, :], ST[1][:, 1, :], [b_ST[1]]),
             (OP[0][:, :], PJX[0], [b_OP[0], b_PJX[0]]), (OP[1][:, :], PJX[1], [b_OP[1], b_PJX[1]])]
    grp = [0]

    def branch_pass(w_y_c, gcol0, OGsrc, b_OGsrc, chbase, accumulate):
        def load(cc):
            s0 = chbase + (cc % 2) * 2
            S.dma(POOL, CHW[s0][:, :, :], w_y_c[:, :, cc * 128:(cc + 1) * 128], writes=[b_CH[s0]])
            S.dma(POOL, CHW[s0 + 1][:, :, :], w_in_c[:, :, gcol0 + cc * 128:gcol0 + (cc + 1) * 128], writes=[b_CH[s0 + 1]])
        load(0)
        for cc in range(8):
            if cc + 1 < 8:
                load(cc + 1)
            s0 = chbase + (cc % 2) * 2
            for M in range(4):
                g = grp[0] % 4
                grp[0] += 1
                tok = own_tok(M)
                ybank, gbank, bb = slots[g]
                for c in range(8):
                    S.op(PE, lambda e: e.matmul(ybank, lhsT=CHW[s0][:, c, :], rhs=OGsrc[:, c, tok], start=(c == 0), stop=(c == 7)),
                         reads=[b_OGsrc, b_CH[s0]], writes=bb, partial=(c > 0))
                for c in range(8):
                    S.op(PE, lambda e: e.matmul(gbank, lhsT=CHW[s0 + 1][:, c, :], rhs=urhs(c, M), start=(c == 0), stop=(c == 7)),
                         reads=b_uT + [b_CH[s0 + 1]], writes=bb, partial=True)
                k = g
                S.op(ACT, lambda e: e.activation(out=FT[k], in_=gbank, func=AF.Tanh, scale=0.5), reads=bb, writes=[b_FT[k]])
                if not accumulate:
                    S.op(DVE, lambda e: e.scalar_tensor_tensor(out=MT[:, cc, tok], in0=FT[k], scalar=1.0, in1=ybank, op0=ALU.add, op1=ALU.mult),
                         reads=[b_FT[k]] + bb, writes=[b_MT], partial=True)
                else:
                    S.op(DVE, lambda e: e.scalar_tensor_tensor(out=FT[k], in0=FT[k], scalar=1.0, in1=ybank, op0=ALU.add, op1=ALU.mult),
                         reads=[b_FT[k]] + bb, writes=[b_FT[k]])
                    S.op(POOL, lambda e: e.tensor_tensor(out=MT[:, cc, tok], in0=MT[:, cc, tok], in1=FT[k], op=ALU.add),
                         reads=[b_FT[k], b_MT], writes=[b_MT], partial=True)

    branch_pass(w_bra_c, O_GA, OGA, b_OGA, 0, False)
    bt = {}
    for hh in range(2):
        S.dma(POOL, WO[:, :, hh * 512:(hh + 1) * 512], w_out_c[:, :, hh * 512:(hh + 1) * 512], writes=[b_WO, b_OGA], partial=(hh > 0), batch=bt)
    S.dma(SP, gpost_b[:, :], gpost.broadcast_to([128, D]), writes=[buf("gpost_b"), b_OGA], partial=True)
    outd_b = outd.rearrange("(m p) d -> m p d", p=128)
    branch_pass(w_brf_c, O_GB, OGB, b_OGB, 4, True)

    barrier()

    def f2_X(m):
        g = m % 2
        S.dma(SP, XR[m % 3][:, :], xl_b[2 + 2 * m, :, :], writes=[b_XR[m % 3]])
        for hh in range(2):
            for c in range(8):
                S.op(PE, lambda e: e.matmul(ST[g][:, hh, :], lhsT=MT[:, c, m * 128:(m + 1) * 128], rhs=WO[:, c, hh * 512:(hh + 1) * 512], start=(c == 0), stop=(c == 7)),
                     reads=[b_MT, b_WO], writes=[b_ST[g]], partial=(c > 0 or hh > 0))
        sb_ = buf("fstat%d" % m)
        s0 = m * 8
        S.op(ACT, lambda e: e.activation(out=sqj2.rearrange("p (a b) -> p a b", a=2), in_=ST[g][:, :, :], func=AF.Square, accum_out=stat[:, s0:s0 + 1]),
             reads=[b_ST[g]], writes=[buf("sqj2"), sb_])
        S.op(DVE, lambda e: e.tensor_scalar(out=stat[:, s0 + 1:s0 + 2], in0=stat[:, s0:s0 + 1], scalar1=1.0 / D, scalar2=0.25 * RMS_EPS, op0=ALU.mult, op1=ALU.add),
             reads=[sb_], writes=[sb_])
        S.op(POOL, lambda e: e.tensor_tensor(out=stat[:, s0 + 2:s0 + 3], in0=stat[:, s0 + 1:s0 + 2], in1=mhalf[:, 0:1], op=ALU.pow),
             reads=[sb_, buf("mhalf")], writes=[sb_])

    def f2_Y(m):
        g = m % 2
        sb_ = buf("fstat%d" % m)
        s0 = m * 8
        S.op(DVE, lambda e: e.scalar_tensor_tensor(out=RES[g].rearrange("p (a b) -> p a b", a=2), in0=ST[g][:, :, :], scalar=stat[:, s0 + 2:s0 + 3],
                                                   in1=gpost_b.rearrange("p (a b) -> p a b", a=2), op0=ALU.mult, op1=ALU.mult),
             reads=[b_ST[g], sb_, buf("gpost_b")], writes=[b_RES[g]])
        S.op(DVE, lambda e: e.tensor_tensor(out=RES[g][:, :], in0=RES[g][:, :], in1=XR[m % 3][:, :], op=ALU.add),
             reads=[b_RES[g], b_XR[m % 3]], writes=[b_RES[g]])
        S.dma(ACT, outd_b[m, :, :], RES[g][:, :], reads=[b_RES[g]])

    for i in range(17):
        if i < 16:
            f2_X(i)
        if i >= 1:
            f2_Y(i - 1)

    S._wait(SP, dict(S.all_dma_tokens))
    barrier()
    return nc, es, S


_CACHE = {}


def build_two_pass(debug=False, stop=99):
    _nc0, _es0, s0 = build_program(debug=debug, stop=stop, needed=None)
    needed = set(s0.waited)
    nc, _es, _s = build_program(debug=debug, stop=stop, needed=needed)
    _KEEP.append((_es, _es0))
    return nc


_KEEP = []


def _layout_core(x, meta, b, p):
    xl = np.zeros((T, D), np.float32)
    xl[0:16] = meta
    pos = np.zeros((128, NB), np.float32)
    valid = np.zeros((128, NB), np.float32)
    pos[0:16, 0] = np.arange(16, dtype=np.float32)
    valid[0:16, 0] = 1.0
    tt = np.arange(128, dtype=np.float32)
    for m in range(16):
        for j, g in enumerate((2 * m + p - 1, 2 * m + p)):
            blk = 1 + 2 * m + j
            if g < 0:
                continue
            xl[blk * 128:(blk + 1) * 128] = x[b, g * 128:(g + 1) * 128]
            pos[:, blk] = 16 + 128 * g + tt
            valid[:, blk] = 1.0
    return xl, pos, valid


def kernel(x, meta_tokens, pre_norm_g, w_in, fox_forget_b, mla_q_norm_g, mla_kv_norm_g,
           w_uq, w_ukv, w_br_mla, w_br_fox, w_out, post_norm_g):
    f = lambda a: np.ascontiguousarray(np.asarray(a, dtype=np.float32))
    x = f(x)
    meta = f(meta_tokens)
    debug = bool(int(os.environ.get("MK_DEBUG", "0")))
    stop = int(os.environ.get("MK_STOP", "99"))
    key = ("prog", debug, stop)
    if key not in _CACHE:
        _CACHE[key] = build_two_pass(debug=debug, stop=stop)
    nc = _CACHE[key]
    invf = (10000.0 ** (-(np.arange(16, dtype=np.float32)) / np.float32(16.0))).astype(np.float32).reshape(1, 16)
    shared = {
        "w_in": f(w_in)[0], "w_uq": f(w_uq)[0], "w_ukv": f(w_ukv)[0], "w_bra": f(w_br_mla)[0], "w_brf": f(w_br_fox)[0],
        "w_out": f(w_out)[0], "gpre": f(pre_norm_g).reshape(1, D), "gq": f(mla_q_norm_g).reshape(1, 256),
        "gkv": f(mla_kv_norm_g).reshape(1, 128), "gpost": f(post_norm_g).reshape(1, D), "fb": f(fox_forget_b).reshape(1, 16),
        "invf": invf,
    }
    in_maps = []
    for c in range(8):
        b, p = c // 2, c % 2
        xl, pos, valid = _layout_core(x, meta, b, p)
        d = dict(shared)
        d.update({"xl": xl, "posv": pos, "validv": valid})
        in_maps.append(d)
    res = run_bass_kernel_spmd(nc, in_maps, core_ids=list(range(8)))
    out = np.empty((4, 4096, D), np.float32)
    for c in range(8):
        b, p = c // 2, c % 2
        o = np.asarray(res.results[c]["out"]).reshape(16, 128, D)
        for m in range(16):
            g = 2 * m + p
            out[b, g * 128:(g + 1) * 128] = o[m]
    if debug:
        kernel.last_results = res.results
    return out
```

```python
import os
import numpy as np
from contextlib import ExitStack
import concourse.bass as bass
import concourse.mybir as mybir
from concourse.bass_utils import run_bass_kernel_spmd

F32 = mybir.dt.float32
BF16 = mybir.dt.bfloat16
ALU = mybir.AluOpType
AF = mybir.ActivationFunctionType
AX = mybir.AxisListType

D = 1024
NB = 33
T = NB * 128
NOWN = 2048
RMS_EPS = 1e-6
MLA_SCALE = 1.0 / float(np.sqrt(96.0))
FOX_SCALE = 0.125
O_CQ, O_CKV, O_KPE, O_ZM, O_FQ, O_FK, O_FV, O_FL, O_ZF, O_GA, O_GB = 0, 256, 384, 416, 1440, 2464, 3488, 4512, 4528, 5552, 6576
TWO_PI = 2.0 * np.pi
MAGIC = 12582912.0
CW1 = 6.28125
CW2 = float(np.float32(TWO_PI - CW1))
CW3 = float(TWO_PI - CW1 - CW2)
NEG_BIG = -30000.0


class Buf:
    def __init__(self, name, excl=False):
        self.name = name
        self.excl = excl
        self.w = {}
        self.r = {}
        self.war = {}
        self.phase = "w"


class EngQ:
    def __init__(self, nc, eng, name, es, is_pe=False):
        self.nc = nc
        self.eng = eng
        self.name = name
        self.sem = es.enter_context(nc.semaphore("q_" + name))
        self.count = 0
        self.real = 0
        self.vmap = {}
        self.seen = {}
        self.is_pe = is_pe


class DSem:
    def __init__(self, nc, key, i, es):
        self.key = key
        self.sem = es.enter_context(nc.semaphore("d%d" % i))
        self.count = 0


class Sched:
    def __init__(self, nc, es, needed=None, ndma=24):
        self.nc = nc
        self.needed = needed
        self.waited = set()
        self.pe = EngQ(nc, nc.tensor, "pe", es, is_pe=True)
        self.act = EngQ(nc, nc.scalar, "act", es)
        self.dve = EngQ(nc, nc.vector, "dve", es)
        self.pool = EngQ(nc, nc.gpsimd, "pool", es)
        self.sp = EngQ(nc, nc.sync, "sp", es)
        self.qs = {q.name: q for q in (self.pe, self.act, self.dve, self.pool, self.sp)}
        self.dsems = {"sp": [DSem(nc, ("d", "sp", i), i, es) for i in range(ndma)],
                      "pool": [DSem(nc, ("d", "pool", i), 100 + i, es) for i in range(ndma)],
                      "act": [DSem(nc, ("d", "act", i), 200 + i, es) for i in range(8)]}
        self.dnext = {"sp": 0, "pool": 0, "act": 0}
        self.all_dma_tokens = {}

    def _collect(self, q, reads, writes, partial, deps):
        need = {}

        def add(d):
            for k, (s, v) in d.items():
                if k == q.name and q.is_pe:
                    continue
                if k not in need or need[k][1] < v:
                    need[k] = (s, v)

        for b in reads:
            add(b.w)
            if b.excl:
                add({k: t for k, t in b.r.items() if k != q.name})
        for b in writes:
            if b.phase == "r":
                b.war = b.r
                b.r = {}
                b.w = {}
                b.phase = "w"
            add(b.war)
            if not partial:
                add(b.w)
        for d in deps:
            add(d)
        return need

    def _wait(self, q, need):
        for k, (s, v) in need.items():
            if q.seen.get(k, 0) < v:
                q.seen[k] = v
                if isinstance(k, str):
                    self.waited.add((k, v))
                    if self.needed is None:
                        rv = v
                    else:
                        rv = self.qs[k].vmap[v]
                else:
                    rv = v
                q.eng.wait_ge(s, rv)

    def op(self, q, fn, reads=(), writes=(), partial=False, deps=()):
        need = self._collect(q, reads, writes, partial, deps)
        self._wait(q, need)
        ins = fn(q.eng)
        q.count += 1
        if self.needed is None or (q.name, q.count) in self.needed:
            q.real += 1
            q.vmap[q.count] = q.real
            ins.then_inc(q.sem, 1)
        tok = (q.sem, q.count)
        for b in reads:
            b.r[q.name] = tok
            b.phase = "r"
        for b in writes:
            b.w[q.name] = tok
        return {q.name: tok}

    def dma(self, q, out, in_, reads=(), writes=(), partial=False, deps=(), batch=None, **kw):
        need = self._collect(q, reads, writes, partial, deps)
        if batch is not None and batch.get("ds") is not None:
            ds = batch["ds"]
        else:
            pool_ = self.dsems[q.name]
            ds = pool_[self.dnext[q.name]]
            self.dnext[q.name] = (self.dnext[q.name] + 1) % len(pool_)
            if ds.count > 0:
                need[ds.key] = (ds.sem, ds.count)
            if batch is not None:
                batch["ds"] = ds
        self._wait(q, need)
        ins = q.eng.dma_start(out=out, in_=in_, **kw)
        ds.count += 16
        ins.then_inc(ds.sem, 16)
        tok = (ds.sem, ds.count)
        for b in reads:
            b.r[ds.key] = tok
            b.phase = "r"
        for b in writes:
            b.w[ds.key] = tok
        self.all_dma_tokens[ds.key] = tok
        return {ds.key: tok}


def build_program(debug=False, stop=99, needed=None):
    nc = bass.Bass("TRN2", target_bir_lowering=False)
    es = ExitStack()

    def dram(name, shape, dt=F32, kind="ExternalInput"):
        return nc.dram_tensor(name, list(shape), dt, kind=kind).ap()

    xl = dram("xl", [T, D])
    posv = dram("posv", [128, NB])
    validv = dram("validv", [128, NB])
    w_in = dram("w_in", [D, 7600])
    w_uq = dram("w_uq", [256, 1536])
    w_ukv = dram("w_ukv", [128, 2048])
    w_bra = dram("w_bra", [D, D])
    w_brf = dram("w_brf", [D, D])
    w_out = dram("w_out", [D, D])
    gpre = dram("gpre", [1, D])
    gq = dram("gq", [1, 256])
    gkv = dram("gkv", [1, 128])
    gpost = dram("gpost", [1, D])
    fb = dram("fb", [1, 16])
    invf = dram("invf", [1, 16])
    outd = dram("out", [NOWN, D], kind="ExternalOutput")
    dbg = {}
    if debug:
        dbg["uT"] = dram("dbg_uT", [128, 8 * T], BF16, kind="ExternalOutput")
        dbg["ckvnT"] = dram("dbg_ckvnT", [128, T], BF16, kind="ExternalOutput")
        dbg["cqnT"] = dram("dbg_cqnT", [128, 2 * NOWN], BF16, kind="ExternalOutput")
        dbg["KA"] = dram("dbg_KA", [128, T], BF16, kind="ExternalOutput")
        dbg["QA"] = dram("dbg_QA", [128, NOWN], BF16, kind="ExternalOutput")
        dbg["VP"] = dram("dbg_VP", [128, NB * 192], BF16, kind="ExternalOutput")
        dbg["OGA"] = dram("dbg_OGA", [128, 8 * NOWN], BF16, kind="ExternalOutput")
        dbg["OGB"] = dram("dbg_OGB", [128, 8 * NOWN], BF16, kind="ExternalOutput")
        dbg["cum"] = dram("dbg_cum", [128, 528], F32, kind="ExternalOutput")
        dbg["TK"] = dram("dbg_TK", [128, T], BF16, kind="ExternalOutput")

    S = Sched(nc, es, needed=needed)
    PE, ACT, DVE, POOL, SP = S.pe, S.act, S.dve, S.pool, S.sp

    def sb(name, shape, dt):
        return es.enter_context(nc.sbuf_tensor(name, list(shape), dt))

    def ps(name, shape, dt=F32):
        return es.enter_context(nc.psum_tensor(name, list(shape), dt))

    uT = sb("uT", [128, 8, T], BF16)
    OGAf = sb("OGAf", [128, 8 * NOWN], BF16)
    OGBf = sb("OGBf", [128, 8 * NOWN], BF16)
    ARENA = sb("ARENA", [128, 24000], BF16)
    TKreg = sb("TKreg", [128, T], BF16)
    WBIG = [sb("WBIG%d" % i, [128, 8, 128], BF16) for i in range(4)]
    RT = sb("RT", [128, 512], F32)
    TMP = sb("TMP", [128, 512], F32)
    ident_bf = sb("ident_bf", [128, 128], BF16)
    ident_f = sb("ident_f", [128, 128], F32)
    maskneg = sb("maskneg", [128, 128], BF16)
    tri_f = sb("tri_f", [128, 128], F32)
    ones_f = sb("ones_f", [128, 128], F32)
    gq_b = sb("gq_b", [128, 256], F32)
    gkv_b = sb("gkv_b", [128, 128], F32)
    fb_b = sb("fb_b", [128, 16], F32)
    invf_b = sb("invf_b", [128, 16], F32)
    pos_t = sb("pos_t", [128, NB], F32)
    valid_t = sb("valid_t", [128, NB], F32)
    FLh = sb("FLh", [128, 16, NB], F32)
    mhalf = sb("mhalf", [128, 4], F32)
    stat = sb("stat", [128, 8 * NB], F32)
    WMS = sb("WMS", [128, 768], BF16)

    def view(base, byte_off, shape, dt):
        esz = 2 if dt == BF16 else 4
        n = int(np.prod(shape[1:]))
        assert byte_off % 4 == 0
        a = base[:, byte_off // 2: byte_off // 2 + n * esz // 2]
        if dt == F32:
            a = a.bitcast(F32)
        if len(shape) == 3:
            a = a.rearrange("p (a b) -> p a b", a=shape[1])
        return a

    OGA = view(OGAf, 0, [128, 8, NOWN], BF16)
    OGB = view(OGBf, 0, [128, 8, NOWN], BF16)
    KA = view(ARENA, 0, [128, T], BF16)
    KB = view(ARENA, 8448, [128, T], BF16)
    QA = view(ARENA, 16896, [128, NOWN], BF16)
    QB = view(ARENA, 20992, [128, NOWN], BF16)
    VP = view(ARENA, 25088, [128, NB, 192], BF16)
    PT = [view(ARENA, 37760 + i * 2048, [128, 2, 512], BF16) for i in range(3)]
    SZ = view(ARENA, 43904, [128, NOWN], BF16)
    TH = view(ARENA, 37760 + 2 * 2048, [128, 512], F32)
    xs = [view(OGAf, i * 4096, [128, 1024], F32) for i in range(3)]
    xn = [view(OGAf, 12288 + i * 2048, [128, 1024], BF16) for i in range(2)]
    sqj = view(OGAf, 16384, [128, 1024], BF16)
    gpre_b = view(OGAf, 18432, [128, 1024], F32)
    LT = [view(OGAf, 22528 + i * 1024, [128, 416], BF16) for i in range(2)]
    CS_tm = view(OGAf, 24576, [128, NB, 16], F32)
    SN_tm = view(OGAf, 26688, [128, NB, 16], F32)
    WL = view(ARENA, 16896, [128, 8, 432], BF16)
    cqnT = view(OGBf, 0, [128, 2, NOWN], BF16)
    CSq = view(OGBf, 8192, [128, NOWN], F32)
    SNq = view(OGBf, 16384, [128, NOWN], F32)
    ANGs = view(OGBf, 24576, [128, NB, 16], F32)
    KFs = view(OGBf, 26688, [128, NB, 16], F32)
    RRs = view(OGBf, 28800, [128, NB, 16], F32)
    ckvnT = view(TKreg, 0, [128, T], BF16)
    TK = view(TKreg, 0, [128, T], BF16)
    TKsrc = view(ARENA, 0, [128, NB, 64], BF16)
    TQsrc = view(ARENA, 4224, [128, 16, 64], BF16)
    CUM = view(ARENA, 6272, [128, 16, NB], F32)
    LFv = view(ARENA, 8384, [128, 16, NB], F32)
    INC = view(ARENA, 10496, [128, 16, NB], F32)
    SEG = view(ARENA, 12608, [128, 16, NB], F32)
    C8 = view(ARENA, 14720, [128, 16, NB], F32)
    R1 = view(ARENA, 16832, [128, 16, NB], F32)
    HI = view(ARENA, 18944, [128, 16, NB], BF16)
    MID = view(ARENA, 20000, [128, 16, NB], BF16)
    MT = view(ARENA, 0, [128, 8, NOWN], BF16)
    FT = [view(ARENA, 32768 + i * 2048, [128, 512], F32) for i in range(4)]
    CHW = [WBIG[i] for i in range(4)] + [view(TKreg, i * 2048, [128, 8, 128], BF16) for i in range(4)]

    ST = [ps("ST%d" % i, [128, 2, 512]) for i in range(2)]
    OP = [ps("OP%d" % i, [128, 512]) for i in range(2)]
    PJ_ = [ps("PJ%d" % i, [128, 512]) for i in range(2)]

    B = {}

    def buf(name, excl=False):
        if name not in B:
            B[name] = Buf(name, excl=excl)
        return B[name]

    b_uT = [buf("uT%d" % i) for i in range(NB)]
    b_ST = [buf("ST0", True), buf("ST1", True)]
    b_OP = [buf("OP0", True), buf("OP1", True)]
    b_PJ_ = [buf("PJ0", True), buf("PJ1", True)]
    PJX = [PJ_[0][:, :], PJ_[1][:, :], ST[0][:, 0, :], ST[1][:, 0, :]]
    b_PJX = [b_PJ_[0], b_PJ_[1], b_ST[0], b_ST[1]]
    b_PT = [buf("PT%d" % i) for i in range(3)]
    b_xs = [buf("xs%d" % i) for i in range(3)]
    b_xn = [buf("xn%d" % i) for i in range(2)]
    b_LT = [buf("LT0"), buf("LT1")]
    b_W = [buf("WBIG%d" % i) for i in range(4)]
    bc_ = buf("consts")

    pj_i = [0]

    def next_pj():
        i = pj_i[0]
        pj_i[0] = (i + 1) % 4
        return i

    ev_i = [0]

    def evac_engine():
        ev_i[0] += 1
        return ACT if (ev_i[0] % 2 == 0) else DVE

    def copy_op(q, out, in_, reads, writes, partial=True):
        if q is ACT:
            return S.op(ACT, lambda e: e.activation(out=out, in_=in_, func=AF.Copy), reads=reads, writes=writes, partial=partial)
        return S.op(q, lambda e: e.tensor_copy(out=out, in_=in_), reads=reads, writes=writes, partial=partial)

    def barrier():
        qs = [PE, ACT, DVE, POOL, SP]
        toks = {q.name: (q.sem, q.count) for q in qs if q.count > 0}
        toks.update(S.all_dma_tokens)
        for q in qs:
            S._wait(q, dict(toks))


    def finalize():
        S._wait(SP, dict(S.all_dma_tokens))
        barrier()
        return nc, es, S

    S.op(POOL, lambda e: e.memset(ident_bf[:, :], 1.0), writes=[buf("ident_bf")])
    S.op(POOL, lambda e: e.affine_select(out=ident_bf[:, :], in_=ident_bf[:, :], pattern=[[1, 128]], compare_op=ALU.is_equal,
                                          fill=0.0, base=0, channel_multiplier=-1), reads=[buf("ident_bf")], writes=[buf("ident_bf")])
    S.op(POOL, lambda e: e.memset(ident_f[:, :], 1.0), writes=[buf("ident_f")])
    S.op(POOL, lambda e: e.affine_select(out=ident_f[:, :], in_=ident_f[:, :], pattern=[[1, 128]], compare_op=ALU.is_equal,
                                          fill=0.0, base=0, channel_multiplier=-1), reads=[buf("ident_f")], writes=[buf("ident_f")])
    S.op(POOL, lambda e: e.memset(maskneg[:, :], NEG_BIG), writes=[buf("maskneg")])
    S.op(POOL, lambda e: e.affine_select(out=maskneg[:, :], in_=maskneg[:, :], pattern=[[-1, 128]], compare_op=ALU.is_gt,
                                          fill=0.0, base=0, channel_multiplier=1), reads=[buf("maskneg")], writes=[buf("maskneg")])
    S.op(POOL, lambda e: e.memset(tri_f[:, :], 1.0), writes=[buf("tri_f")])
    S.op(POOL, lambda e: e.affine_select(out=tri_f[:, :], in_=tri_f[:, :], pattern=[[1, 128]], compare_op=ALU.is_ge,
                                          fill=0.0, base=0, channel_multiplier=-1), reads=[buf("tri_f")], writes=[buf("tri_f")])
    S.op(POOL, lambda e: e.memset(ones_f[:, :], 1.0), writes=[buf("ones_f")])
    S.op(POOL, lambda e: e.memset(mhalf[:, :], -0.5), writes=[buf("mhalf")])

    for (dst, src, n) in [(gpre_b, gpre, D), (gq_b, gq, 256), (gkv_b, gkv, 128), (fb_b, fb, 16), (invf_b, invf, 16)]:
        S.dma(SP, dst[:, :], src.broadcast_to([128, n]), writes=[buf("smallconst")], partial=True)
    S.dma(SP, pos_t[:, :], posv[:, :], writes=[buf("smallconst")], partial=True)
    S.dma(SP, valid_t[:, :], validv[:, :], writes=[buf("smallconst")], partial=True)

    w_in_c = w_in.rearrange("(c p) n -> p c n", p=128)
    bt = {}
    S.dma(POOL, WL[:, :, 0:416], w_in_c[:, :, 0:416], writes=[buf("WL")], partial=True, batch=bt)
    S.dma(POOL, WL[:, :, 416:432], w_in_c[:, :, O_FL:O_FL + 16], writes=[buf("WL")], partial=True, batch=bt)

    b_tab = buf("tabscratch")
    sc = [buf("smallconst")]
    pos_b = pos_t[:, :].unsqueeze(2).broadcast_to([128, NB, 16])
    invf_bb = invf_b[:, :].unsqueeze(1).broadcast_to([128, NB, 16])
    S.op(DVE, lambda e: e.tensor_tensor(out=ANGs, in0=pos_b, in1=invf_bb, op=ALU.mult), reads=sc, writes=[b_tab])

    def make_table(dst, shift):
        S.op(DVE, lambda e: e.tensor_scalar(out=KFs, in0=ANGs, scalar1=float(shift), scalar2=float(1.0 / TWO_PI), op0=ALU.add, op1=ALU.mult),
             reads=[b_tab], writes=[buf("kf")])
        S.op(DVE, lambda e: e.tensor_scalar(out=KFs, in0=KFs, scalar1=MAGIC, scalar2=None, op0=ALU.add), reads=[buf("kf")], writes=[buf("kf")])
        S.op(DVE, lambda e: e.tensor_scalar(out=KFs, in0=KFs, scalar1=-MAGIC, scalar2=None, op0=ALU.add), reads=[buf("kf")], writes=[buf("kf")])
        S.op(DVE, lambda e: e.scalar_tensor_tensor(out=RRs, in0=KFs, scalar=-CW1, in1=ANGs, op0=ALU.mult, op1=ALU.add),
             reads=[buf("kf"), b_tab], writes=[buf("rr")])
        S.op(DVE, lambda e: e.scalar_tensor_tensor(out=RRs, in0=KFs, scalar=-CW2, in1=RRs, op0=ALU.mult, op1=ALU.add),
             reads=[buf("kf"), buf("rr")], writes=[buf("rr")])
        S.op(DVE, lambda e: e.scalar_tensor_tensor(out=RRs, in0=KFs, scalar=-CW3, in1=RRs, op0=ALU.mult, op1=ALU.add),
             reads=[buf("kf"), buf("rr")], writes=[buf("rr")])
        S.op(DVE, lambda e: e.tensor_scalar(out=RRs, in0=RRs, scalar1=float(shift), scalar2=float(np.pi), op0=ALU.add, op1=ALU.min),
             reads=[buf("rr")], writes=[buf("rr")])
        S.op(DVE, lambda e: e.tensor_scalar(out=RRs, in0=RRs, scalar1=float(-np.pi), scalar2=None, op0=ALU.max), reads=[buf("rr")], writes=[buf("rr")])
        S.op(ACT, lambda e: e.activation(out=dst[:, :, :], in_=RRs, func=AF.Sin), reads=[buf("rr")], writes=[buf("tables")], partial=True)

    make_table(SN_tm, 0.0)
    make_table(CS_tm, float(np.pi / 2))

    b_VPones = buf("VPones")
    S.op(DVE, lambda e: e.tensor_scalar(out=VP[:, :, 64:128], in0=valid_t[:, :].unsqueeze(2).broadcast_to([128, NB, 64]),
                                         scalar1=2.0, scalar2=None, op0=ALU.mult), reads=sc, writes=[b_VPones])

    xl_b = xl.rearrange("(b p) d -> b p d", p=128)
    b_ckvnT = buf("ckvnT")
    b_cqnT = buf("cqnT")
    def kcls(blk):
        ti = blk // 4
        return 0 if ti <= 2 else (1 if ti <= 4 else (2 if ti <= 6 else 3))

    b_KA = [buf("KA%d" % i) for i in range(4)]
    b_KB = [buf("KB%d" % i) for i in range(4)]
    b_VP = [buf("VP%d" % i) for i in range(4)]
    b_QA = [buf("QA%d" % i) for i in range(4)]
    b_QB = [buf("QB%d" % i) for i in range(4)]
    b_SZ = [buf("SZ%d" % i) for i in range(4)]
    b_FLh = buf("FLh")
    b_CSq = buf("CSq")
    tabs = [buf("tables")]

    TB = TMP[:, 0:64]
    bTB = buf("TMPn")
    for m in range(16):
        blk = 2 + 2 * m
        for (tab, dstT) in ((CS_tm, CSq), (SN_tm, SNq)):
            S.op(DVE, lambda e: e.tensor_copy(out=TB.rearrange("p (a b) -> p a b", a=4), in_=tab[:, blk, :].unsqueeze(1).broadcast_to([128, 4, 16])),
                 reads=tabs, writes=[bTB])
            pj = next_pj()
            S.op(PE, lambda e: e.transpose(out=PJX[pj][0:64, 0:128], in_=TB, identity=ident_f[:, :]), reads=[bTB, buf("ident_f")], writes=[b_PJX[pj]])
            copy_op(evac_engine(), dstT[0:64, m * 128:(m + 1) * 128], PJX[pj][0:64, 0:128], reads=[b_PJX[pj]], writes=[b_CSq])

    xs5 = xs + [view(ARENA, 37760, [128, 1024], F32), view(ARENA, 43904, [128, 1024], F32)]
    b_xs5 = b_xs + [buf("xs3"), buf("xs4")]
    NXS = 5

    def st_S1(blk):
        xi = blk % NXS
        S.dma(SP, xs5[xi], xl_b[blk, :, :], writes=[b_xs5[xi]])
        S.op(ACT, lambda e: e.activation(out=sqj, in_=xs5[xi], func=AF.Square, accum_out=stat[:, blk * 8:blk * 8 + 1]),
             reads=[b_xs5[xi]], writes=[buf("sqj"), buf("stat%d" % blk)])
        S.op(POOL, lambda e: e.tensor_scalar(out=stat[:, blk * 8 + 1:blk * 8 + 2], in0=stat[:, blk * 8:blk * 8 + 1], scalar1=1.0 / D, scalar2=RMS_EPS,
                                              op0=ALU.mult, op1=ALU.add), reads=[buf("stat%d" % blk)], writes=[buf("stat%d" % blk)])
        S.op(POOL, lambda e: e.tensor_tensor(out=stat[:, blk * 8 + 2:blk * 8 + 3], in0=stat[:, blk * 8 + 1:blk * 8 + 2], in1=mhalf[:, 0:1], op=ALU.pow),
             reads=[buf("stat%d" % blk), buf("mhalf")], writes=[buf("stat%d" % blk)])

    def st_S2(blk):
        xi = blk % NXS
        ni = blk % 2
        g = blk % 2
        S.op(DVE, lambda e: e.scalar_tensor_tensor(out=xn[ni], in0=xs5[xi], scalar=stat[:, blk * 8 + 2:blk * 8 + 3], in1=gpre_b,
                                                   op0=ALU.mult, op1=ALU.mult),
             reads=[b_xs5[xi], buf("stat%d" % blk)] + sc, writes=[b_xn[ni]])
        pjv = ST[g][:, 0, :].bitcast(BF16).rearrange("p (c t) -> p c t", c=8)
        for c in range(8):
            S.op(PE, lambda e: e.transpose(out=pjv[:, c, :], in_=xn[ni][:, c * 128:(c + 1) * 128], identity=ident_bf[:, :]),
                 reads=[b_xn[ni], buf("ident_bf")], writes=[b_ST[g]], partial=(c > 0))
        copy_op(evac_engine(), uT[:, :, blk * 128:(blk + 1) * 128], pjv, reads=[b_ST[g]], writes=[b_uT[blk]], partial=False)

    def st_A(blk):
        pj = blk % 2
        L = PJX[pj]
        for c in range(8):
            S.op(PE, lambda e: e.matmul(L[:, 0:432], lhsT=uT[:, c, blk * 128:(blk + 1) * 128], rhs=WL[:, c, :], start=(c == 0), stop=(c == 7)),
                 reads=[b_uT[blk], buf("WL")], writes=[b_PJX[pj]], partial=(c > 0))
        st = buf("lstat%d" % blk)
        s0 = blk * 8 + 3
        S.op(ACT, lambda e: e.activation(out=sqj[:, 0:256], in_=L[:, 0:256], func=AF.Square, scale=1.0 / 16.0, accum_out=stat[:, s0:s0 + 1]),
             reads=[b_PJX[pj]], writes=[buf("sqj"), st])
        S.op(ACT, lambda e: e.activation(out=sqj[:, 256:384], in_=L[:, 256:384], func=AF.Square, scale=float(1.0 / np.sqrt(128.0)), accum_out=stat[:, s0 + 1:s0 + 2]),
             reads=[b_PJX[pj]], writes=[buf("sqj"), st], partial=True)
        S.op(POOL, lambda e: e.tensor_scalar(out=stat[:, s0:s0 + 2], in0=stat[:, s0:s0 + 2], scalar1=RMS_EPS, scalar2=None, op0=ALU.add),
             reads=[st], writes=[st])
        S.op(POOL, lambda e: e.tensor_tensor(out=stat[:, s0 + 2:s0 + 4], in0=stat[:, s0:s0 + 2], in1=mhalf[:, 0:2], op=ALU.pow),
             reads=[st, buf("mhalf")], writes=[st])

    def st_B(blk):
        pj = blk % 2
        li = blk % 2
        L = PJX[pj]
        st = buf("lstat%d" % blk)
        s0 = blk * 8 + 3
        lt = LT[li]
        S.op(DVE, lambda e: e.scalar_tensor_tensor(out=lt[:, 0:256], in0=L[:, 0:256], scalar=stat[:, s0 + 2:s0 + 3], in1=gq_b[:, :], op0=ALU.mult, op1=ALU.mult),
             reads=[b_PJX[pj], st] + sc, writes=[b_LT[li]])
        S.op(DVE, lambda e: e.scalar_tensor_tensor(out=lt[:, 256:384], in0=L[:, 256:384], scalar=stat[:, s0 + 3:s0 + 4], in1=gkv_b[:, :], op0=ALU.mult, op1=ALU.mult),
             reads=[b_PJX[pj], st] + sc, writes=[b_LT[li]], partial=True)
        cs = CS_tm[:, blk, :]
        sn = SN_tm[:, blk, :]
        bth = buf("TMPn")
        X2 = L[:, 384:416].rearrange("p (a b) -> p a b", a=2)
        T1 = TMP[:, 0:32].rearrange("p (a b) -> p a b", a=2)
        T2 = TMP[:, 32:64].rearrange("p (a b) -> p a b", a=2)
        S.op(DVE, lambda e: e.tensor_tensor(out=T1, in0=X2, in1=cs.unsqueeze(1).broadcast_to([128, 2, 16]), op=ALU.mult), reads=[b_PJX[pj]] + tabs, writes=[bth])
        S.op(DVE, lambda e: e.tensor_tensor(out=T2, in0=X2, in1=sn.unsqueeze(1).broadcast_to([128, 2, 16]), op=ALU.mult), reads=[b_PJX[pj]] + tabs, writes=[bth], partial=True)
        S.op(DVE, lambda e: e.tensor_tensor(out=lt[:, 384:400], in0=TMP[:, 0:16], in1=TMP[:, 48:64], op=ALU.subtract), reads=[bth], writes=[b_LT[li]], partial=True)
        S.op(DVE, lambda e: e.tensor_tensor(out=lt[:, 400:416], in0=TMP[:, 32:48], in1=TMP[:, 16:32], op=ALU.add), reads=[bth], writes=[b_LT[li]], partial=True)
        S.op(DVE, lambda e: e.tensor_tensor(out=FLh[:, :, blk], in0=L[:, 416:432], in1=fb_b[:, :], op=ALU.add), reads=[b_PJX[pj]] + sc, writes=[b_FLh], partial=True)
        tp = OP[pj][:, :].bitcast(BF16)
        for j in range(3):
            S.op(PE, lambda e: e.transpose(out=tp[:, j * 128:(j + 1) * 128], in_=lt[:, j * 128:(j + 1) * 128], identity=ident_bf[:, :]),
                 reads=[b_LT[li], buf("ident_bf")], writes=[b_OP[pj]], partial=(j > 0))
        S.op(PE, lambda e: e.transpose(out=tp[:, 384:512], in_=lt[:, 288:416], identity=ident_bf[:, :]),
             reads=[b_LT[li], buf("ident_bf")], writes=[b_OP[pj]], partial=True)

    def st_C(blk):
        pj = blk % 2
        tp = OP[pj][:, :].bitcast(BF16)
        ev = evac_engine()
        if blk >= 2 and blk % 2 == 0:
            m = (blk - 2) // 2
            copy_op(ev, cqnT[:, :, m * 128:(m + 1) * 128], tp[:, 0:256].rearrange("p (c t) -> p c t", c=2), reads=[b_OP[pj]], writes=[b_cqnT])
        copy_op(ev, ckvnT[:, blk * 128:(blk + 1) * 128], tp[:, 256:384], reads=[b_OP[pj]], writes=[b_ckvnT])
        copy_op(ev, KA[64:96, blk * 128:(blk + 1) * 128], tp[96:128, 384:512], reads=[b_OP[pj]], writes=[b_KA[kcls(blk)]])
        copy_op(ev, KB[64:96, blk * 128:(blk + 1) * 128], tp[96:128, 384:512], reads=[b_OP[pj]], writes=[b_KB[kcls(blk)]])

    for i in range(NB + 4):
        if i < NB:
            st_S1(i)
        if 0 <= i - 1 < NB:
            st_S2(i - 1)
        if 0 <= i - 2 < NB:
            st_A(i - 2)
        if 0 <= i - 3 < NB:
            st_B(i - 3)
        if 0 <= i - 4 < NB:
            st_C(i - 4)

    if debug:
        S.dma(SP, dbg["uT"], uT[:, :, :].rearrange("p c t -> p (c t)"), reads=b_uT)
        S.dma(SP, dbg["ckvnT"], ckvnT, reads=[b_ckvnT])
        S.dma(SP, dbg["cqnT"], cqnT.rearrange("p c t -> p (c t)"), reads=[b_cqnT])
    if stop <= 2:
        return finalize()

    barrier()

    own_tok = lambda M: slice(M * 512, (M + 1) * 512)
    st_i = [0]
    pt_i = [0]
    op_i = [0]
    pj2_i = [0]
    TH2 = sb("TH2", [128, 512], F32)
    R16 = sb("R16", [128, 16], F32)
    b_TH2 = buf("TH2")

    def bank_for(instream):
        if instream:
            i = pj2_i[0]
            pj2_i[0] = (i + 1) % 2
            return i
        return next_pj()

    def ev_for(instream):
        return DVE if instream else evac_engine()

    def slot_blocks(M):
        blks = [(0, 512, 0, False)] + [(b, 512, 0, False) for b in range(1, 8 * M + 1)]
        for mm in range(4):
            n = 512 - 128 * mm
            blks.append((1 + 2 * (4 * M + mm), n, 128 * mm, False))
            blks.append((2 + 2 * (4 * M + mm), n, 128 * mm, True))
        groups = []
        i = 0
        while i < len(blks):
            if i + 1 < len(blks) and blks[i + 1][1] == blks[i][1]:
                groups.append([blks[i], blks[i + 1]])
                i += 2
            else:
                groups.append([blks[i]])
                i += 1
        return groups

    def attention_multi(order, scale, pair, win=None):
        win = win or {}
        stream = []
        seg_groups = []
        for si_, (hd, M) in enumerate(order):
            groups = slot_blocks(M)
            seg_groups.append(len(groups))
            for gi, g in enumerate(groups):
                stream.append((si_, hd, M, g, gi, len(groups)))
        wstate = {}
        for s0, items in win.items():
            tot = seg_groups[s0] + (seg_groups[s0 + 1] if s0 + 1 < len(seg_groups) else 0)
            wstate[s0] = [list(items), 0, tot, 0]
        pend = None
        oslot = {}
        nq = []

        def drain(n=None, upto=None):
            k = 0
            while nq and (n is None or k < n) and (upto is None or nq[0][0] <= upto):
                nq.pop(0)[1]()
                k += 1

        for idx in range(len(stream) + 1):
            if idx < len(stream):
                sg, hd, M, g, gi, ng_ = stream[idx]
                first = gi == 0
                last = gi == ng_ - 1
                if first:
                    oslot[sg] = op_i[0] % 2
                    op_i[0] += 1
                si = st_i[0] % 2
                st_i[0] += 1
                n = g[0][1]
                qoff = g[0][2]
                KT, QT, KR = hd["KT"], hd["QT"], hd["KR"]
                for j, (blk, n_, qoff_, diag) in enumerate(g):
                    S.op(PE, lambda e: e.matmul(ST[si][:, j, 0:n], lhsT=KT[0:KR, blk * 128:(blk + 1) * 128], rhs=QT[0:KR, M * 512 + qoff:(M + 1) * 512],
                                                start=True, stop=(not diag)),
                         reads=[hd["b_K"][kcls(blk)], hd["b_Q"][M]], writes=[b_ST[si]], partial=(j > 0))
                    if diag:
                        S.op(PE, lambda e: e.matmul(ST[si][:, j, 0:128], lhsT=ident_bf[:, :], rhs=maskneg[:, :], start=False, stop=True),
                             reads=[buf("ident_bf"), buf("maskneg")], writes=[b_ST[si]], partial=True)
                pi = pt_i[0] % 3
                pt_i[0] += 1
                ng = len(g)
                S.op(ACT, lambda e: e.activation(out=PT[pi][:, 0:ng, 0:n], in_=ST[si][:, 0:ng, 0:n], func=AF.Exp, scale=float(scale)),
                     reads=[b_ST[si]], writes=[b_PT[pi]])
                cur = (sg, hd, M, g, first, last, pi)
                s0 = sg - (sg % 2)
                if s0 in wstate:
                    w = wstate[s0]
                    rem = len(w[0]) - w[1]
                    left = w[2] - w[3]
                    if rem > 0:
                        k = -(-rem // max(left, 1))
                        for it in w[0][w[1]:w[1] + k]:
                            it()
                        w[1] += k
                    w[3] += 1
                drain(n=(2 if len(nq) > 6 else 1))
            else:
                cur = None
            if pend is not None:
                sg, hd, M, g, first, last, pi = pend
                o = oslot[sg]
                if first:
                    drain(upto=sg - 2)
                n = g[0][1]
                qoff = g[0][2]
                vcol0 = hd["vcol0"]
                for j, (blk, n_, qoff_, diag) in enumerate(g):
                    S.op(PE, lambda e: e.matmul(OP[o][:, qoff:512], lhsT=VP[:, blk, vcol0:vcol0 + 128], rhs=PT[pi][:, j, 0:n],
                                                start=(first and j == 0), stop=(last and j == len(g) - 1)),
                         reads=[b_PT[pi], b_VP[kcls(blk)], b_VPones], writes=[b_OP[o]], partial=not (first and j == 0))
                if last:
                    if not hd["is_B"]:
                        orow, drow = slice(0, 64), slice(64, 128)
                    else:
                        orow, drow = slice(64, 128), slice(0, 64)
                    OG = hd["OG"]
                    bRT, bTM, bR16 = buf("RT"), buf("TMPn"), buf("R16")
                    bOGx = hd["b_OG"]

                    def mk(o=o, orow=orow, drow=drow, OG=OG, M=M, bOGx=bOGx):
                        return [
                            lambda: S.op(DVE, lambda e: e.transpose(out=RT[drow, :], in_=OP[o][drow, :]), reads=[b_OP[o]], writes=[bRT]),
                            lambda: S.op(DVE, lambda e: e.reciprocal(out=R16[drow, :], in_=RT[drow, :].rearrange("p (b c) -> p b c", c=32)[:, :, 0]),
                                         reads=[bRT], writes=[bR16]),
                            lambda: S.op(DVE, lambda e: e.tensor_copy(out=TMP[drow, :].rearrange("p (b c) -> p b c", c=32),
                                                                      in_=R16[drow, :].unsqueeze(2).broadcast_to([64, 16, 32])), reads=[bR16], writes=[bTM]),
                            lambda: S.op(DVE, lambda e: e.transpose(out=RT[drow, :], in_=TMP[drow, :]), reads=[bTM], writes=[bRT]),
                            lambda: S.op(DVE, lambda e: e.tensor_tensor(out=TMP[orow, :], in0=OP[o][orow, :], in1=RT[drow, :], op=ALU.mult),
                                         reads=[b_OP[o], bRT], writes=[bTM]),
                            lambda: S.op(DVE, lambda e: e.tensor_tensor(out=OG[orow, pair, M * 512:(M + 1) * 512], in0=TMP[orow, :],
                                                                        in1=SZ[orow, M * 512:(M + 1) * 512], op=ALU.mult),
                                         reads=[bTM, b_SZ[M]], writes=[bOGx], partial=True),
                        ]
                    for fn in mk():
                        nq.append((sg, fn))
            pend = cur
        drain()

    def urhs(c, M):
        return uT[:, c, 128:T].rearrange("p (m two t) -> p m two t", two=2, t=128)[:, 4 * M:4 * M + 4, 1, :]

    def z_item(M, wz, b_wz, instream):
        def f():
            pj = bank_for(instream)
            for c in range(8):
                S.op(PE, lambda e: e.matmul(PJX[pj][:, :], lhsT=wz[:, c, :], rhs=urhs(c, M), start=(c == 0), stop=(c == 7)),
                     reads=b_uT + [b_wz], writes=[b_PJX[pj]], partial=(c > 0))
            S.op(ACT, lambda e: e.activation(out=TH2[:, :], in_=PJX[pj][:, :], func=AF.Tanh, scale=0.5), reads=[b_PJX[pj]], writes=[b_TH2])
            S.op(DVE, lambda e: e.scalar_tensor_tensor(out=SZ[:, M * 512:(M + 1) * 512], in0=TH2[:, :], scalar=1.0, in1=PJX[pj][:, :], op0=ALU.add, op1=ALU.mult),
                 reads=[b_TH2, b_PJX[pj]], writes=[b_SZ[M]])
        return f

    def load_w_in_cols(dst, b_dst, col0, ncols=128):
        S.dma(POOL, dst[:, :, 0:ncols], w_in_c[:, :, col0:col0 + ncols], writes=[b_dst])

    ntt = [(i * 512, 512) for i in range(8)] + [(4096, 128)]
    KT_OF = {0: [0, 1, 2], 1: [3, 4], 2: [5, 6], 3: [7, 8]}
    VG_OF = {0: [0, 4, 8], 1: [12, 16], 2: [20, 24], 3: [28, 32]}

    def run_pair(items_of, hA_, hB_, scale, pair, all_upfront=False):
        for it in items_of(0, False):
            it()
        if all_upfront:
            for c in range(1, 5):
                for it in items_of(c, False):
                    it()
            win = {}
        else:
            win = {0: items_of(1, True), 2: items_of(2, True), 4: items_of(3, True), 6: items_of(4, True)}
        order = [(hA_, 0), (hB_, 0), (hA_, 1), (hB_, 1), (hA_, 2), (hB_, 2), (hA_, 3), (hB_, 3)]
        attention_multi(order, scale, pair, win)

    w_ukv_h = w_ukv.rearrange("k (h c) -> k h c", h=16)
    w_uq_c = w_uq.rearrange("(c p) (h d) -> p c h d", p=128, h=16)
    WKVn = WMS[:, 0:128]
    WKVv = WMS[:, 128:256]
    WQn = WMS[:, 256:512].rearrange("p (c n) -> p c n", c=2)
    WQp = WMS[:, 512:640].rearrange("p (c n) -> p c n", c=2)
    WQr = WMS[:, 640:768].rearrange("p (c n) -> p c n", c=2)
    b_WMS = buf("WMS")
    b_OGA = buf("OGA")
    b_OGB = buf("OGB")
    b_neg = buf("WQrneg")

    def mla_items(c, instream):
        its = []
        if c > 3:
            return [z_item(3, WBIG[0], b_W[0], instream)]

        def kt(ti):
            def f():
                t0, tn = ntt[ti]
                pj = bank_for(instream)
                S.op(PE, lambda e: e.matmul(PJX[pj][:, 0:tn], lhsT=WKVn, rhs=ckvnT[:, t0:t0 + tn], start=True, stop=True),
                     reads=[b_WMS, b_ckvnT], writes=[b_PJX[pj]])
                ev = ev_for(instream)
                copy_op(ev, KA[0:64, t0:t0 + tn], PJX[pj][0:64, 0:tn], reads=[b_PJX[pj]], writes=[b_KA[c]])
                copy_op(ev, KB[0:64, t0:t0 + tn], PJX[pj][64:128, 0:tn], reads=[b_PJX[pj]], writes=[b_KB[c]])
            return f

        def vg(g0):
            def f():
                nb_ = min(4, NB - g0)
                pj = bank_for(instream)
                pv = PJX[pj].rearrange("p (b c) -> p b c", b=4)
                for j in range(nb_):
                    blk = g0 + j
                    S.op(PE, lambda e: e.matmul(pv[:, j, :], lhsT=ckvnT[:, blk * 128:(blk + 1) * 128], rhs=WKVv, start=True, stop=True),
                         reads=[b_WMS, b_ckvnT], writes=[b_PJX[pj]], partial=(j > 0))
                ev = ev_for(instream)
                copy_op(ev, VP[:, g0:g0 + nb_, 0:64], pv[:, 0:nb_, 0:64], reads=[b_PJX[pj]], writes=[b_VP[c]])
                copy_op(ev, VP[:, g0:g0 + nb_, 128:192], pv[:, 0:nb_, 64:128], reads=[b_PJX[pj]], writes=[b_VP[c]])
            return f

        def qq(M):
            def f():
                tok = own_tok(M)
                pj = bank_for(instream)
                for cc in range(2):
                    S.op(PE, lambda e: e.matmul(PJX[pj][:, :], lhsT=WQn[:, cc, :], rhs=cqnT[:, cc, tok], start=(cc == 0), stop=(cc == 1)),
                         reads=[b_WMS, b_cqnT], writes=[b_PJX[pj]], partial=(cc > 0))
                ev = ev_for(instream)
                copy_op(ev, QA[0:64, tok], PJX[pj][0:64, :], reads=[b_PJX[pj]], writes=[b_QA[M]], partial=False)
                copy_op(ev, QB[0:64, tok], PJX[pj][64:128, :], reads=[b_PJX[pj]], writes=[b_QB[M]], partial=False)
                pjp = bank_for(instream)
                for cc in range(2):
                    S.op(PE, lambda e: e.matmul(PJX[pjp][0:64, :], lhsT=WQp[:, cc, :], rhs=cqnT[:, cc, tok], start=(cc == 0), stop=(cc == 1)),
                         reads=[b_WMS, b_cqnT], writes=[b_PJX[pjp]], partial=(cc > 0))
                S.op(DVE, lambda e: e.tensor_tensor(out=TH2[0:64, :], in0=PJX[pjp][0:64, :], in1=CSq[0:64, tok], op=ALU.mult),
                     reads=[b_PJX[pjp], b_CSq], writes=[b_TH2])
                pjr = bank_for(instream)
                for cc in range(2):
                    S.op(PE, lambda e: e.matmul(PJX[pjr][0:64, :], lhsT=WQr[:, cc, :], rhs=cqnT[:, cc, tok], start=(cc == 0), stop=(cc == 1)),
                         reads=[b_WMS, b_neg, b_cqnT], writes=[b_PJX[pjr]], partial=(cc > 0))
                S.op(DVE, lambda e: e.tensor_tensor(out=PJX[pjr][0:64, :], in0=PJX[pjr][0:64, :], in1=SNq[0:64, tok], op=ALU.mult),
                     reads=[b_PJX[pjr], b_CSq], writes=[b_PJX[pjr]])
                S.op(DVE, lambda e: e.tensor_tensor(out=QA[64:96, tok], in0=PJX[pjr][0:32, :], in1=TH2[0:32, :], op=ALU.add),
                     reads=[b_PJX[pjr], b_TH2], writes=[b_QA[M]], partial=True)
                S.op(DVE, lambda e: e.tensor_tensor(out=QB[64:96, tok], in0=PJX[pjr][32:64, :], in1=TH2[32:64, :], op=ALU.add),
                     reads=[b_PJX[pjr], b_TH2], writes=[b_QB[M]], partial=True)
            return f

        for ti in KT_OF[c]:
            its.append(kt(ti))
        for g0 in VG_OF[c]:
            its.append(vg(g0))
        its.append(qq(c))
        if c >= 1:
            its.insert(0, z_item(c - 1, WBIG[0], b_W[0], instream))
        return its

    for pair in range(8):
        hA, hB = 2 * pair, 2 * pair + 1
        bt = {}
        for hi, h in enumerate((hA, hB)):
            S.dma(POOL, WKVn[:, hi * 64:(hi + 1) * 64], w_ukv_h[:, h, 0:64], writes=[b_WMS], partial=(hi > 0), batch=bt)
            S.dma(POOL, WKVv[:, hi * 64:(hi + 1) * 64], w_ukv_h[:, h, 64:128], writes=[b_WMS], partial=True, batch=bt)
            S.dma(POOL, WQn[:, :, hi * 64:(hi + 1) * 64], w_uq_c[:, :, h, 0:64], writes=[b_WMS], partial=True, batch=bt)
            S.dma(POOL, WQp[:, :, hi * 32:(hi + 1) * 32], w_uq_c[:, :, h, 64:96], writes=[b_WMS], partial=True, batch=bt)
            S.dma(POOL, WQr[:, :, hi * 32:hi * 32 + 16], w_uq_c[:, :, h, 80:96], writes=[b_WMS], partial=True, batch=bt)
            S.dma(POOL, WQr[:, :, hi * 32 + 16:hi * 32 + 32], w_uq_c[:, :, h, 64:80], writes=[b_WMS], partial=True, batch=bt)
        for hi in range(2):
            S.op(POOL, lambda e: e.tensor_scalar(out=WQr[:, :, hi * 32:hi * 32 + 16], in0=WQr[:, :, hi * 32:hi * 32 + 16], scalar1=-1.0, scalar2=None, op0=ALU.mult),
                 reads=[b_WMS], writes=[b_neg], partial=(hi > 0))
        load_w_in_cols(WBIG[0], b_W[0], O_ZM + pair * 128)
        hdA = dict(KT=KA, QT=QA, b_K=b_KA, b_Q=b_QA, KR=96, vcol0=0, is_B=False, OG=OGA, b_OG=b_OGA)
        hdB = dict(KT=KB, QT=QB, b_K=b_KB, b_Q=b_QB, KR=96, vcol0=64, is_B=True, OG=OGA, b_OG=b_OGA)
        if debug and pair == 0:
            for c in range(0, 5):
                for it in mla_items(c, False):
                    it()
            S.dma(SP, dbg["KA"][0:96, :], KA[0:96, :], reads=b_KA)
            S.dma(SP, dbg["QA"][0:96, :], QA[0:96, :], reads=b_QA)
            S.dma(SP, dbg["VP"], VP[:, :, :].rearrange("p b c -> p (b c)"), reads=b_VP + [b_VPones])
            if stop <= 4:
                attention_multi([(hdA, 0), (hdA, 1), (hdA, 2), (hdA, 3)], MLA_SCALE, pair)
                S.dma(SP, dbg["OGA"][0:64, 0:NOWN], OGA[0:64, 0, :], reads=[b_OGA])
                return finalize()
            order = [(hdA, 0), (hdB, 0), (hdA, 1), (hdB, 1), (hdA, 2), (hdB, 2), (hdA, 3), (hdB, 3)]
            attention_multi(order, MLA_SCALE, pair)
        else:
            run_pair(mla_items, hdA, hdB, MLA_SCALE, pair)

    if debug:
        S.dma(SP, dbg["OGA"], OGA[:, :, :].rearrange("p c t -> p (c t)"), reads=[b_OGA])

    if stop <= 5:
        return finalize()
    load_w_in_cols(WBIG[1], b_W[1], O_FK)
    load_w_in_cols(WBIG[2], b_W[2], O_FV)
    load_w_in_cols(WBIG[3], b_W[3], O_FQ)
    barrier()

    b_cum = buf("cumwork")
    S.op(ACT, lambda e: e.activation(out=LFv, in_=FLh[:, :, :], func=AF.Exp, scale=-1.0), reads=[b_FLh], writes=[b_cum])
    S.op(ACT, lambda e: e.activation(out=LFv, in_=LFv, func=AF.Ln, bias=1.0), reads=[b_cum], writes=[b_cum])
    S.op(DVE, lambda e: e.scalar_tensor_tensor(out=LFv, in0=LFv, scalar=-1.0, in1=valid_t[:, :].unsqueeze(1).broadcast_to([128, 16, NB]), op0=ALU.mult, op1=ALU.mult),
         reads=[b_cum] + sc, writes=[b_cum])
    S.op(POOL, lambda e: e.memset(SEG, 1.0), writes=[buf("SEG")])
    S.op(POOL, lambda e: e.memset(SEG[:, :, 0:1], 0.0), reads=[buf("SEG")], writes=[buf("SEG")])
    S.op(DVE, lambda e: e.tensor_tensor_scan(out=INC.rearrange("p a b -> p (a b)"), data0=SEG.rearrange("p a b -> p (a b)"),
                                              data1=LFv.rearrange("p a b -> p (a b)"), initial=0.0, op0=ALU.mult, op1=ALU.add),
         reads=[b_cum, buf("SEG")], writes=[buf("INC")])
    S.op(DVE, lambda e: e.tensor_tensor(out=INC, in0=INC, in1=LFv, op=ALU.subtract), reads=[buf("INC"), b_cum], writes=[buf("INC")])
    LF2 = LFv.rearrange("p a b -> p (a b)")
    EX2 = INC.rearrange("p a b -> p (a b)")
    CU2 = CUM.rearrange("p a b -> p (a b)")
    for (c0, cn, pj) in ((0, 495, 0), (495, 33, 1)):
        S.op(PE, lambda e: e.matmul(PJX[pj][:, 0:cn], lhsT=tri_f[:, :], rhs=LF2[:, c0:c0 + cn], start=True, stop=False),
             reads=[b_cum, buf("tri_f")], writes=[b_PJX[pj]])
        S.op(PE, lambda e: e.matmul(PJX[pj][:, 0:cn], lhsT=ones_f[:, :], rhs=EX2[:, c0:c0 + cn], start=False, stop=True),
             reads=[buf("INC"), buf("ones_f")], writes=[b_PJX[pj]], partial=True)
        S.op(DVE, lambda e: e.tensor_copy(out=CU2[:, c0:c0 + cn], in_=PJX[pj][:, 0:cn]), reads=[b_PJX[pj]], writes=[buf("CUM")], partial=True)
    if debug:
        S.dma(SP, dbg["cum"], CU2, reads=[buf("CUM")])
    bk = buf("TKsrc")
    S.op(DVE, lambda e: e.tensor_scalar(out=C8, in0=CUM, scalar1=-8.0, scalar2=None, op0=ALU.mult), reads=[buf("CUM")], writes=[buf("C8")])
    S.op(DVE, lambda e: e.tensor_copy(out=HI, in_=C8), reads=[buf("C8")], writes=[buf("HI")])
    S.op(DVE, lambda e: e.tensor_tensor(out=R1, in0=C8, in1=HI, op=ALU.subtract), reads=[buf("C8"), buf("HI")], writes=[buf("R1")])
    S.op(DVE, lambda e: e.tensor_copy(out=MID, in_=R1), reads=[buf("R1")], writes=[buf("MID")])
    S.op(DVE, lambda e: e.tensor_tensor(out=R1, in0=R1, in1=MID, op=ALU.subtract), reads=[buf("R1"), buf("MID")], writes=[buf("R1")])
    TKs4 = TKsrc.rearrange("p b (h f) -> p b h f", f=4)
    TQs4 = TQsrc.rearrange("p m (h f) -> p m h f", f=4)
    S.op(POOL, lambda e: e.memset(TKsrc, 1.0), writes=[bk])
    S.op(POOL, lambda e: e.memset(TQsrc, 1.0), writes=[buf("TQsrc")])
    hb = lambda a: a.rearrange("p h b -> p b h")
    S.op(DVE, lambda e: e.tensor_copy(out=TKs4[:, :, :, 1], in_=hb(HI)), reads=[buf("HI"), bk], writes=[bk])
    S.op(DVE, lambda e: e.tensor_copy(out=TKs4[:, :, :, 2], in_=hb(MID)), reads=[buf("MID")], writes=[bk], partial=True)
    S.op(DVE, lambda e: e.tensor_copy(out=TKs4[:, :, :, 3], in_=hb(R1)), reads=[buf("R1")], writes=[bk], partial=True)
    cum_own = CUM[:, :, 1:NB].rearrange("p h (m two) -> p m h two", two=2)[:, :, :, 1]
    S.op(DVE, lambda e: e.tensor_scalar(out=TQs4[:, :, :, 0], in0=cum_own, scalar1=8.0, scalar2=None, op0=ALU.mult),
         reads=[buf("CUM"), buf("TQsrc")], writes=[buf("TQsrc")])
    b_TK = buf("TK")
    for g0 in range(0, NB, 4):
        nb_ = min(4, NB - g0)
        pj = next_pj()
        tp = PJX[pj][:, :].bitcast(BF16)
        for j in range(nb_):
            S.op(PE, lambda e: e.transpose(out=tp[0:64, j * 128:(j + 1) * 128], in_=TKsrc[:, g0 + j, :], identity=ident_bf[:, :]),
                 reads=[bk, buf("ident_bf")], writes=[b_PJX[pj]], partial=(j > 0))
        copy_op(evac_engine(), TK[0:64, g0 * 128:(g0 + nb_) * 128], tp[0:64, 0:nb_ * 128], reads=[b_PJX[pj]], writes=[b_TK])
    for g0 in range(0, 16, 4):
        pj = next_pj()
        tp = PJX[pj][:, :].bitcast(BF16)
        for j in range(4):
            S.op(PE, lambda e: e.transpose(out=tp[0:64, j * 128:(j + 1) * 128], in_=TQsrc[:, g0 + j, :], identity=ident_bf[:, :]),
                 reads=[buf("TQsrc"), buf("ident_bf")], writes=[b_PJX[pj]], partial=(j > 0))
        copy_op(evac_engine(), TK[64:128, g0 * 128:(g0 + 4) * 128], tp[0:64, 0:512], reads=[b_PJX[pj]], writes=[b_TK])
    if debug:
        S.dma(SP, dbg["TK"], TK, reads=[b_TK])

    if stop <= 6:
        return finalize()
    barrier()

    def fox_items(c, instream):
        its = []
        if c > 3:
            return [z_item(3, WBIG[0], b_W[0], instream)]

        def kt(ti):
            def f():
                t0, tn = ntt[ti]
                pj = bank_for(instream)
                for cc in range(8):
                    S.op(PE, lambda e: e.matmul(PJX[pj][:, 0:tn], lhsT=WBIG[1][:, cc, :], rhs=uT[:, cc, t0:t0 + tn], start=(cc == 0), stop=(cc == 7)),
                         reads=b_uT + [b_W[1]], writes=[b_PJX[pj]], partial=(cc > 0))
                ev = ev_for(instream)
                copy_op(ev, KA[0:64, t0:t0 + tn], PJX[pj][0:64, 0:tn], reads=[b_PJX[pj]], writes=[b_KA[c]])
                copy_op(ev, KB[0:64, t0:t0 + tn], PJX[pj][64:128, 0:tn], reads=[b_PJX[pj]], writes=[b_KB[c]])
            return f

        def vg(g0):
            def f():
                nb_ = min(4, NB - g0)
                pj = bank_for(instream)
                pv = PJX[pj].rearrange("p (b c) -> p b c", b=4)
                for j in range(nb_):
                    blk = g0 + j
                    for cc in range(8):
                        S.op(PE, lambda e: e.matmul(pv[:, j, :], lhsT=uT[:, cc, blk * 128:(blk + 1) * 128], rhs=WBIG[2][:, cc, :], start=(cc == 0), stop=(cc == 7)),
                             reads=[b_uT[blk], b_W[2]], writes=[b_PJX[pj]], partial=(j > 0 or cc > 0))
                ev = ev_for(instream)
                copy_op(ev, VP[:, g0:g0 + nb_, 0:64], pv[:, 0:nb_, 0:64], reads=[b_PJX[pj]], writes=[b_VP[c]])
                copy_op(ev, VP[:, g0:g0 + nb_, 128:192], pv[:, 0:nb_, 64:128], reads=[b_PJX[pj]], writes=[b_VP[c]])
            return f

        def qq(M):
            def f():
                tok = own_tok(M)
                pj = bank_for(instream)
                for cc in range(8):
                    S.op(PE, lambda e: e.matmul(PJX[pj][:, :], lhsT=WBIG[3][:, cc, :], rhs=urhs(cc, M), start=(cc == 0), stop=(cc == 7)),
                         reads=b_uT + [b_W[3]], writes=[b_PJX[pj]], partial=(cc > 0))
                ev = ev_for(instream)
                copy_op(ev, QA[0:64, tok], PJX[pj][0:64, :], reads=[b_PJX[pj]], writes=[b_QA[M]])
                copy_op(ev, QB[0:64, tok], PJX[pj][64:128, :], reads=[b_PJX[pj]], writes=[b_QB[M]])
            return f

        for ti in KT_OF[c]:
            its.append(kt(ti))
        for g0 in VG_OF[c]:
            its.append(vg(g0))
        its.append(qq(c))
        if c >= 1:
            its.insert(0, z_item(c - 1, WBIG[0], b_W[0], instream))
        return its

    for pair in range(8):
        hA, hB = 2 * pair, 2 * pair + 1
        if pair > 0:
            load_w_in_cols(WBIG[1], b_W[1], O_FK + pair * 128)
            load_w_in_cols(WBIG[2], b_W[2], O_FV + pair * 128)
            load_w_in_cols(WBIG[3], b_W[3], O_FQ + pair * 128)
        load_w_in_cols(WBIG[0], b_W[0], O_ZF + pair * 128)
        S.dma(SP, KA[64:68, :], TK[4 * hA:4 * hA + 4, :], reads=[b_TK], writes=b_KA, partial=True)
        S.dma(SP, KB[64:68, :], TK[4 * hB:4 * hB + 4, :], reads=[b_TK], writes=b_KB, partial=True)
        S.dma(SP, QA[64:68, :], TK[64 + 4 * hA:64 + 4 * hA + 4, 0:NOWN], reads=[b_TK], writes=b_QA, partial=True)
        S.dma(SP, QB[64:68, :], TK[64 + 4 * hB:64 + 4 * hB + 4, 0:NOWN], reads=[b_TK], writes=b_QB, partial=True)

        hdA = dict(KT=KA, QT=QA, b_K=b_KA, b_Q=b_QA, KR=68, vcol0=0, is_B=False, OG=OGB, b_OG=b_OGB)
        hdB = dict(KT=KB, QT=QB, b_K=b_KB, b_Q=b_QB, KR=68, vcol0=64, is_B=True, OG=OGB, b_OG=b_OGB)
        run_pair(fox_items, hdA, hdB, FOX_SCALE, pair)

    if debug:
        S.dma(SP, dbg["OGB"], OGB[:, :, :].rearrange("p c t -> p (c t)"), reads=[b_OGB])

    if stop <= 7:
        return finalize()
    barrier()
    w_bra_c = w_bra.rearrange("(c p) n -> p c n", p=128)
    w_brf_c = w_brf.rearrange("(c p) n -> p c n", p=128)
    w_out_c = w_out.rearrange("(c p) n -> p c n", p=128)
    b_CH = [buf("CH%d" % i) for i in range(8)]
    b_MT = buf("MT")
    b_FT = [buf("FT%d" % i) for i in range(4)]
    b_WO = buf("WO")
    b_XR = [buf("XR0"), buf("XR1")]
    b_RES = [buf("RES0"), buf("RES1")]
    WO = view(OGAf, 0, [128, 8, D], BF16)
    gpost_b = view(OGAf, 16384, [128, D], F32)
    XR = [view(OGAf, 20480 + i * 4096, [128, D], F32) for i in range(2)]
    RES = [view(ARENA, 32768 + i * 4096, [128, D], F32) for i in range(2)]
    sqj2 = view(ARENA, 40960, [128, D], BF16)
    slots = [(ST[0][:, 0, :], ST[0][:, 1, :], [b_ST[0]]), (ST[1][:, 0, :], ST[1][:, 1, :], [b_ST[1]]),
             (OP[0][:, :], PJX[0], [b_OP[0], b_PJX[0]]), (OP[1][:, :], PJX[1], [b_OP[1], b_PJX[1]])]
    grp = [0]

    def branch_pass(w_y_c, gcol0, OGsrc, b_OGsrc, chbase, accumulate):
        def load(cc):
            s0 = chbase + (cc % 2) * 2
            S.dma(POOL, CHW[s0][:, :, :], w_y_c[:, :, cc * 128:(cc + 1) * 128], writes=[b_CH[s0]])
            S.dma(POOL, CHW[s0 + 1][:, :, :], w_in_c[:, :, gcol0 + cc * 128:gcol0 + (cc + 1) * 128], writes=[b_CH[s0 + 1]])
        load(0)
        for cc in range(8):
            if cc + 1 < 8:
                load(cc + 1)
            s0 = chbase + (cc % 2) * 2
            for M in range(4):
                g = grp[0] % 4
                grp[0] += 1
                tok = own_tok(M)
                ybank, gbank, bb = slots[g]
                for c in range(8):
                    S.op(PE, lambda e: e.matmul(ybank, lhsT=CHW[s0][:, c, :], rhs=OGsrc[:, c, tok], start=(c == 0), stop=(c == 7)),
                         reads=[b_OGsrc, b_CH[s0]], writes=bb, partial=(c > 0))
                for c in range(8):
                    S.op(PE, lambda e: e.matmul(gbank, lhsT=CHW[s0 + 1][:, c, :], rhs=urhs(c, M), start=(c == 0), stop=(c == 7)),
                         reads=b_uT + [b_CH[s0 + 1]], writes=bb, partial=True)
                k = g
                S.op(ACT, lambda e: e.activation(out=FT[k], in_=gbank, func=AF.Tanh, scale=0.5), reads=bb, writes=[b_FT[k]])
                if not accumulate:
                    S.op(DVE, lambda e: e.scalar_tensor_tensor(out=MT[:, cc, tok], in0=FT[k], scalar=1.0, in1=ybank, op0=ALU.add, op1=ALU.mult),
                         reads=[b_FT[k]] + bb, writes=[b_MT], partial=True)
                else:
                    S.op(DVE, lambda e: e.scalar_tensor_tensor(out=FT[k], in0=FT[k], scalar=1.0, in1=ybank, op0=ALU.add, op1=ALU.mult),
                         reads=[b_FT[k]] + bb, writes=[b_FT[k]])
                    S.op(POOL, lambda e: e.tensor_tensor(out=MT[:, cc, tok], in0=MT[:, cc, tok], in1=FT[k], op=ALU.add),
                         reads=[b_FT[k], b_MT], writes=[b_MT], partial=True)

    branch_pass(w_bra_c, O_GA, OGA, b_OGA, 0, False)
    bt = {}
    for hh in range(2):
        S.dma(POOL, WO[:, :, hh * 512:(hh + 1) * 512], w_out_c[:, :, hh * 512:(hh + 1) * 512], writes=[b_WO, b_OGA], partial=(hh > 0), batch=bt)
    S.dma(SP, gpost_b[:, :], gpost.broadcast_to([128, D]), writes=[buf("gpost_b"), b_OGA], partial=True)
    outd_b = outd.rearrange("(m p) d -> m p d", p=128)
    branch_pass(w_brf_c, O_GB, OGB, b_OGB, 4, True)

    barrier()

    def f2_X(m):
        g = m % 2
        S.dma(SP, XR[g][:, :], xl_b[2 + 2 * m, :, :], writes=[b_XR[g]])
        for hh in range(2):
            for c in range(8):
                S.op(PE, lambda e: e.matmul(ST[g][:, hh, :], lhsT=MT[:, c, m * 128:(m + 1) * 128], rhs=WO[:, c, hh * 512:(hh + 1) * 512], start=(c == 0), stop=(c == 7)),
                     reads=[b_MT, b_WO], writes=[b_ST[g]], partial=(c > 0 or hh > 0))
        sb_ = buf("fstat%d" % m)
        s0 = m * 8
        S.op(ACT, lambda e: e.activation(out=sqj2.rearrange("p (a b) -> p a b", a=2), in_=ST[g][:, :, :], func=AF.Square, accum_out=stat[:, s0:s0 + 1]),
             reads=[b_ST[g]], writes=[buf("sqj2"), sb_])
        S.op(DVE, lambda e: e.tensor_scalar(out=stat[:, s0 + 1:s0 + 2], in0=stat[:, s0:s0 + 1], scalar1=1.0 / D, scalar2=0.25 * RMS_EPS, op0=ALU.mult, op1=ALU.add),
             reads=[sb_], writes=[sb_])
        S.op(POOL, lambda e: e.tensor_tensor(out=stat[:, s0 + 2:s0 + 3], in0=stat[:, s0 + 1:s0 + 2], in1=mhalf[:, 0:1], op=ALU.pow),
             reads=[sb_, buf("mhalf")], writes=[sb_])

    def f2_Y(m):
        g = m % 2
        sb_ = buf("fstat%d" % m)
        s0 = m * 8
        S.op(DVE, lambda e: e.scalar_tensor_tensor(out=RES[g].rearrange("p (a b) -> p a b", a=2), in0=ST[g][:, :, :], scalar=stat[:, s0 + 2:s0 + 3],
                                                   in1=gpost_b.rearrange("p (a b) -> p a b", a=2), op0=ALU.mult, op1=ALU.mult),
             reads=[b_ST[g], sb_, buf("gpost_b")], writes=[b_RES[g]])
        S.op(POOL, lambda e: e.tensor_tensor(out=RES[g][:, :], in0=RES[g][:, :], in1=XR[g][:, :], op=ALU.add),
             reads=[b_RES[g], b_XR[g]], writes=[b_RES[g]])
        S.dma(ACT, outd_b[m, :, :], RES[g][:, :], reads=[b_RES[g]])

    for i in range(17):
        if i < 16:
            f2_X(i)
        if i >= 1:
            f2_Y(i - 1)

    S._wait(SP, dict(S.all_dma_tokens))
    barrier()
    return nc, es, S


_CACHE = {}


def build_two_pass(debug=False, stop=99):
    _nc0, _es0, s0 = build_program(debug=debug, stop=stop, needed=None)
    needed = set(s0.waited)
    nc, _es, _s = build_program(debug=debug, stop=stop, needed=needed)
    _KEEP.append((_es, _es0))
    return nc


_KEEP = []


def _layout_core(x, meta, b, p):
    xl = np.zeros((T, D), np.float32)
    xl[0:16] = meta
    pos = np.zeros((128, NB), np.float32)
    valid = np.zeros((128, NB), np.float32)
    pos[0:16, 0] = np.arange(16, dtype=np.float32)
    valid[0:16, 0] = 1.0
    tt = np.arange(128, dtype=np.float32)
    for m in range(16):
        for j, g in enumerate((2 * m + p - 1, 2 * m + p)):
            blk = 1 + 2 * m + j
            if g < 0:
                continue
            xl[blk * 128:(blk + 1) * 128] = x[b, g * 128:(g + 1) * 128]
            pos[:, blk] = 16 + 128 * g + tt
            valid[:, blk] = 1.0
    return xl, pos, valid


def kernel(x, meta_tokens, pre_norm_g, w_in, fox_forget_b, mla_q_norm_g, mla_kv_norm_g,
           w_uq, w_ukv, w_br_mla, w_br_fox, w_out, post_norm_g):
    f = lambda a: np.ascontiguousarray(np.asarray(a, dtype=np.float32))
    x = f(x)
    meta = f(meta_tokens)
    debug = bool(int(os.environ.get("MK_DEBUG", "0")))
    stop = int(os.environ.get("MK_STOP", "99"))
    key = ("prog", debug, stop)
    if key not in _CACHE:
        _CACHE[key] = build_two_pass(debug=debug, stop=stop)
    nc = _CACHE[key]
    invf = (10000.0 ** (-(np.arange(16, dtype=np.float32)) / np.float32(16.0))).astype(np.float32).reshape(1, 16)
    shared = {
        "w_in": f(w_in)[0], "w_uq": f(w_uq)[0], "w_ukv": f(w_ukv)[0], "w_bra": f(w_br_mla)[0], "w_brf": f(w_br_fox)[0],
        "w_out": f(w_out)[0], "gpre": f(pre_norm_g).reshape(1, D), "gq": f(mla_q_norm_g).reshape(1, 256),
        "gkv": f(mla_kv_norm_g).reshape(1, 128), "gpost": f(post_norm_g).reshape(1, D), "fb": f(fox_forget_b).reshape(1, 16),
        "invf": invf,
    }
    in_maps = []
    for c in range(8):
        b, p = c // 2, c % 2
        xl, pos, valid = _layout_core(x, meta, b, p)
        d = dict(shared)
        d.update({"xl": xl, "posv": pos, "validv": valid})
        in_maps.append(d)
    res = run_bass_kernel_spmd(nc, in_maps, core_ids=list(range(8)))
    out = np.empty((4, 4096, D), np.float32)
    for c in range(8):
        b, p = c // 2, c % 2
        o = np.asarray(res.results[c]["out"]).reshape(16, 128, D)
        for m in range(16):
            g = 2 * m + p
            out[b, g * 128:(g + 1) * 128] = o[m]
    if debug:
        kernel.last_results = res.results
    return out
```
